# Optimizing a Trainium2 kernel written in Bass

```python
import jax, jax.numpy as jnp
from jax import lax
import numpy as np

D_MODEL = 1024
BATCH = 8
SEQ = 2048
DEPTH = 1
DEC_BATCH = 128
DEC_SEQ = 4
PAST_LEN = 16384
PAGE_SIZE = 128

N_META = 16
D_FF = 2816
CONV_A_WIDTH = 3
D_CONV_A = D_MODEL
SSM_EXPAND = 2
D_SSM = SSM_EXPAND * D_MODEL
SSM_HEAD_DIM = 64
SSM_HEADS = D_SSM // SSM_HEAD_DIM
SSM_GROUPS = 4
HEADS_PER_GROUP = SSM_HEADS // SSM_GROUPS
SSM_STATE = 128
SSM_CONV_WIDTH = 4
SSM_CHUNK = 128
D_XBC = D_SSM + 2 * SSM_GROUPS * SSM_STATE
PROJ_SPLITS = (D_CONV_A, D_CONV_A, D_CONV_A, D_SSM, D_XBC, SSM_HEADS, D_MODEL, D_MODEL)
D_IN_PROJ = 3 * D_CONV_A + D_SSM + D_XBC + SSM_HEADS + 2 * D_MODEL
EPS = 1e-6

kernel_name = "hybrid_shortconv_ssd_macaron_step"


def rmsnorm(x, w):
    xf = x.astype(jnp.float32)
    y = xf * lax.rsqrt(jnp.mean(xf * xf, axis=-1, keepdims=True) + EPS)
    return (y * w.astype(jnp.float32)).astype(x.dtype)


def swiglu(x, w_gu, w_down):
    g, u = jnp.split(x @ w_gu, 2, axis=-1)
    return (jax.nn.silu(g) * u) @ w_down


def causal_dwconv(x, buf, w):
    width = w.shape[0]
    seqlen = x.shape[1]
    xp = jnp.concatenate([buf.astype(x.dtype), x], axis=1)
    y = xp[:, 0:seqlen] * w[0]
    for k in range(1, width):
        y = y + xp[:, k:k + seqlen] * w[k]
    return y, xp[:, seqlen:]


def ssd_chunked(xh, dt, a, bm, cm, h0, chunk):
    b, l = xh.shape[:2]
    c = l // chunk
    x = xh.reshape(b, c, chunk, SSM_GROUPS, HEADS_PER_GROUP, SSM_HEAD_DIM)
    dtc = dt.reshape(b, c, chunk, SSM_GROUPS, HEADS_PER_GROUP)
    bc = bm.reshape(b, c, chunk, SSM_GROUPS, SSM_STATE)
    cc = cm.reshape(b, c, chunk, SSM_GROUPS, SSM_STATE)
    acs = jnp.cumsum(dtc * a.reshape(SSM_GROUPS, HEADS_PER_GROUP), axis=2)
    xdt = x * dtc[..., None]
    mask = jnp.tril(jnp.ones((chunk, chunk), dtype=bool))[:, :, None, None]
    seg = acs[:, :, :, None] - acs[:, :, None, :]
    lmat = jnp.exp(jnp.where(mask, seg, -jnp.inf))
    cb = jnp.einsum('bcqgn,bcsgn->bcqsg', cc, bc)
    y_diag = jnp.einsum('bcqsg,bcqsgr,bcsgrp->bcqgrp', cb, lmat, xdt)
    decay = jnp.exp(acs[:, :, -1:] - acs)
    states = jnp.einsum('bcsgn,bcsgr,bcsgrp->bcgrpn', bc, decay, xdt)
    chunk_decay = jnp.exp(acs[:, :, -1])
    h_init = h0.reshape(b, SSM_GROUPS, HEADS_PER_GROUP, SSM_HEAD_DIM, SSM_STATE)

    def step(h, inp):
        s, d = inp
        return h * d[..., None, None] + s, h

    h_last, h_prev = lax.scan(step, h_init, (jnp.swapaxes(states, 0, 1), jnp.swapaxes(chunk_decay, 0, 1)))
    h_prev = jnp.swapaxes(h_prev, 0, 1)
    y_off = jnp.einsum('bcqgn,bcgrpn,bcqgr->bcqgrp', cc, h_prev, jnp.exp(acs))
    y = (y_diag + y_off).reshape(b, l, SSM_HEADS, SSM_HEAD_DIM)
    return y, h_last.reshape(b, SSM_HEADS, SSM_HEAD_DIM, SSM_STATE)


def token_mix(u, buf_a, buf_ssm_conv, h_ssm, segments, w_in, conv_a_w, w_a_out,
              ssm_conv_w, ssm_conv_b, dt_bias, a_log, d_skip, ssm_norm_w, w_b_out, w_o):
    f32 = jnp.float32
    bsz, seqlen, _ = u.shape
    idx = np.cumsum(PROJ_SPLITS)[:-1].tolist()
    a_b, a_c, a_h, z, xbc, dt_raw, g_a, g_b = jnp.split(u @ w_in, idx, axis=-1)
    conv_a, new_buf_a = causal_dwconv(a_c * a_h, buf_a, conv_a_w)
    y_a = (a_b * conv_a) @ w_a_out
    xbc_c, new_buf_ssm = causal_dwconv(xbc, buf_ssm_conv, ssm_conv_w)
    xbc_c = jax.nn.silu(xbc_c + ssm_conv_b)
    xs, bm, cm = jnp.split(xbc_c, [D_SSM, D_SSM + SSM_GROUPS * SSM_STATE], axis=-1)
    xh = xs.astype(f32).reshape(bsz, seqlen, SSM_HEADS, SSM_HEAD_DIM)
    bm = bm.astype(f32).reshape(bsz, seqlen, SSM_GROUPS, SSM_STATE)
    cm = cm.astype(f32).reshape(bsz, seqlen, SSM_GROUPS, SSM_STATE)
    dt = jax.nn.softplus(dt_raw.astype(f32) + dt_bias.astype(f32))
    a = -jnp.exp(a_log.astype(f32))
    h = h_ssm.astype(f32)
    ys = []
    start = 0
    for seg_len, chunk in segments:
        y_seg, h = ssd_chunked(xh[:, start:start + seg_len], dt[:, start:start + seg_len], a,
                               bm[:, start:start + seg_len], cm[:, start:start + seg_len], h, chunk)
        ys.append(y_seg)
        start += seg_len
    y = jnp.concatenate(ys, axis=1) + d_skip.astype(f32)[:, None] * xh
    y = y.reshape(bsz, seqlen, D_SSM) * jax.nn.silu(z.astype(f32))
    yg = y.reshape(bsz, seqlen, SSM_GROUPS, D_SSM // SSM_GROUPS)
    yg = yg * lax.rsqrt(jnp.mean(yg * yg, axis=-1, keepdims=True) + EPS)
    y_b = (yg.reshape(bsz, seqlen, D_SSM) * ssm_norm_w.astype(f32)).astype(u.dtype) @ w_b_out
    merged = jax.nn.sigmoid(g_a) * y_a + jax.nn.sigmoid(g_b) * y_b
    return merged @ w_o, new_buf_a, new_buf_ssm, h.astype(h_ssm.dtype)


def run_trunk(x, bufs_a, bufs_ssm_conv, hs_ssm, segments, norm_ffn1, ffn1_w_gu, ffn1_w_down,
              norm_mix, w_in, conv_a_w, w_a_out, ssm_conv_w, ssm_conv_b, dt_bias, a_log,
              d_skip, ssm_norm_w, w_b_out, w_o, norm_ffn2, ffn2_w_gu, ffn2_w_down, norm_final):
    new_a, new_c, new_h = [], [], []
    h = x
    for i in range(DEPTH):
        h = h + 0.5 * swiglu(rmsnorm(h, norm_ffn1[i]), ffn1_w_gu[i], ffn1_w_down[i])
        m, ba, bc, hs = token_mix(rmsnorm(h, norm_mix[i]), bufs_a[i], bufs_ssm_conv[i], hs_ssm[i],
                                  segments, w_in[i], conv_a_w[i], w_a_out[i], ssm_conv_w[i],
                                  ssm_conv_b[i], dt_bias[i], a_log[i], d_skip[i], ssm_norm_w[i],
                                  w_b_out[i], w_o[i])
        h = h + m
        h = h + 0.5 * swiglu(rmsnorm(h, norm_ffn2[i]), ffn2_w_gu[i], ffn2_w_down[i])
        new_a.append(ba)
        new_c.append(bc)
        new_h.append(hs)
    return rmsnorm(h, norm_final), jnp.stack(new_a), jnp.stack(new_c), jnp.stack(new_h)


def setup_inputs(seed: int = 0) -> dict:
    key = jax.random.key(seed)
    ks = list(jax.random.split(key, 32))
    nrm = jax.random.normal
    f32 = jnp.float32

    def gain(k):
        return 1.0 + 0.01 * nrm(k, (DEPTH, D_MODEL), f32)

    dt0 = jnp.exp(jax.random.uniform(ks[20], (DEPTH, SSM_HEADS), f32, np.log(1e-3), np.log(1e-1)))
    return {
        "x_prompt": nrm(ks[0], (BATCH, SEQ, D_MODEL), f32),
        "x_sample": nrm(ks[1], (DEC_BATCH, DEC_SEQ, D_MODEL), f32),
        "state_conv_a": 0.5 * nrm(ks[2], (DEPTH, DEC_BATCH, CONV_A_WIDTH - 1, D_CONV_A), f32),
        "state_ssm_conv": nrm(ks[3], (DEPTH, DEC_BATCH, SSM_CONV_WIDTH - 1, D_XBC), f32),
        "state_ssm": 0.1 * nrm(ks[4], (DEPTH, DEC_BATCH, SSM_HEADS, SSM_HEAD_DIM, SSM_STATE), f32),
        "meta_tokens": nrm(ks[5], (N_META, D_MODEL), f32),
        "norm_ffn1": gain(ks[6]),
        "ffn1_w_gu": nrm(ks[7], (DEPTH, D_MODEL, 2 * D_FF), f32) * D_MODEL ** -0.5,
        "ffn1_w_down": nrm(ks[8], (DEPTH, D_FF, D_MODEL), f32) * D_FF ** -0.5,
        "norm_mix": gain(ks[9]),
        "w_in": nrm(ks[10], (DEPTH, D_MODEL, D_IN_PROJ), f32) * D_MODEL ** -0.5,
        "conv_a_w": nrm(ks[11], (DEPTH, CONV_A_WIDTH, D_CONV_A), f32) * CONV_A_WIDTH ** -0.5,
        "w_a_out": nrm(ks[12], (DEPTH, D_CONV_A, D_MODEL), f32) * D_CONV_A ** -0.5,
        "ssm_conv_w": nrm(ks[13], (DEPTH, SSM_CONV_WIDTH, D_XBC), f32) * SSM_CONV_WIDTH ** -0.5,
        "ssm_conv_b": 0.01 * nrm(ks[14], (DEPTH, D_XBC), f32),
        "dt_bias": dt0 + jnp.log(-jnp.expm1(-dt0)),
        "a_log": jnp.log(jax.random.uniform(ks[15], (DEPTH, SSM_HEADS), f32, 1.0, 16.0)),
        "d_skip": 1.0 + 0.01 * nrm(ks[16], (DEPTH, SSM_HEADS), f32),
        "ssm_norm_w": 1.0 + 0.01 * nrm(ks[17], (DEPTH, D_SSM), f32),
        "w_b_out": nrm(ks[18], (DEPTH, D_SSM, D_MODEL), f32) * D_SSM ** -0.5,
        "w_o": nrm(ks[19], (DEPTH, D_MODEL, D_MODEL), f32) * D_MODEL ** -0.5,
        "norm_ffn2": gain(ks[21]),
        "ffn2_w_gu": nrm(ks[22], (DEPTH, D_MODEL, 2 * D_FF), f32) * D_MODEL ** -0.5,
        "ffn2_w_down": nrm(ks[23], (DEPTH, D_FF, D_MODEL), f32) * D_FF ** -0.5,
        "norm_final": 1.0 + 0.01 * nrm(ks[24], (D_MODEL,), f32),
    }


def reference(x_prompt, x_sample, state_conv_a, state_ssm_conv, state_ssm, meta_tokens,
              norm_ffn1, ffn1_w_gu, ffn1_w_down, norm_mix, w_in, conv_a_w, w_a_out,
              ssm_conv_w, ssm_conv_b, dt_bias, a_log, d_skip, ssm_norm_w, w_b_out, w_o,
              norm_ffn2, ffn2_w_gu, ffn2_w_down, norm_final):
    weights = (norm_ffn1, ffn1_w_gu, ffn1_w_down, norm_mix, w_in, conv_a_w, w_a_out,
               ssm_conv_w, ssm_conv_b, dt_bias, a_log, d_skip, ssm_norm_w, w_b_out, w_o,
               norm_ffn2, ffn2_w_gu, ffn2_w_down, norm_final)
    bsz = x_prompt.shape[0]
    dt_p = x_prompt.dtype
    meta = jnp.broadcast_to(meta_tokens.astype(dt_p)[None], (bsz, N_META, D_MODEL))
    xp = jnp.concatenate([meta, x_prompt], axis=1)
    z_a = jnp.zeros((DEPTH, bsz, CONV_A_WIDTH - 1, D_CONV_A), dt_p)
    z_c = jnp.zeros((DEPTH, bsz, SSM_CONV_WIDTH - 1, D_XBC), dt_p)
    z_h = jnp.zeros((DEPTH, bsz, SSM_HEADS, SSM_HEAD_DIM, SSM_STATE), dt_p)
    seg_prompt = ((N_META, N_META), (x_prompt.shape[1], SSM_CHUNK))
    yp, prompt_conv_a, prompt_ssm_conv, prompt_ssm = run_trunk(xp, z_a, z_c, z_h, seg_prompt, *weights)
    y_prompt = yp[:, N_META:]
    seg_sample = ((x_sample.shape[1], x_sample.shape[1]),)
    y_sample, sample_conv_a, sample_ssm_conv, sample_ssm = run_trunk(
        x_sample, state_conv_a, state_ssm_conv, state_ssm, seg_sample, *weights)
    return (y_prompt, y_sample, prompt_conv_a, prompt_ssm_conv, prompt_ssm,
            sample_conv_a, sample_ssm_conv, sample_ssm)
```

```python
import numpy as np
from contextlib import ExitStack
import concourse.bass as bass
import concourse.mybir as mybir
from concourse.bass_utils import run_bass_kernel_spmd

F32 = mybir.dt.float32
BF16 = mybir.dt.bfloat16
AF = mybir.ActivationFunctionType
ALU = mybir.AluOpType
AX = mybir.AxisListType

ENGS = ("pe", "act", "dve", "pool", "sp")
SAME_SYNC = {"pe": False, "act": True, "dve": True, "pool": True, "sp": False}
N_DMA_SEMS = 24


class Res:
    __slots__ = ("name", "w", "r", "excl")

    def __init__(self, name="", excl=False):
        self.name = name
        self.w = None
        self.r = {}
        self.excl = excl


class Sched:
    def __init__(self, nc, st):
        self.nc = nc
        self.st = st
        self.prog = {e: [] for e in ENGS}
        self.cnt = {e: 0 for e in ENGS}
        self.seen = {e: {} for e in ENGS}
        self.dma_val = {("d", i): 0 for i in range(N_DMA_SEMS)}
        self.dma_i = 0
        self.nsb = 0

    def sb(self, shape, dtype, name=None):
        self.nsb += 1
        return self.st.enter_context(self.nc.sbuf_tensor("s_" + (name or f"sb{self.nsb}"), list(shape), dtype))

    def ps(self, shape, dtype=F32, name=None):
        self.nsb += 1
        return self.st.enter_context(self.nc.psum_tensor("p_" + (name or f"ps{self.nsb}"), list(shape), dtype))

    def _need(self, eng, deps):
        need = {}
        seen = self.seen[eng]
        for key, val in deps:
            if key == eng and not SAME_SYNC[eng]:
                continue
            if seen.get(key, 0) >= val:
                continue
            if need.get(key, 0) < val:
                need[key] = val
        for key, val in need.items():
            seen[key] = val
        return list(need.items())

    @staticmethod
    def _deps(reads, writes, eng=None):
        deps = []
        for r in reads:
            if r.w is not None:
                deps.append(r.w)
            if r.excl:
                deps.extend((k, v) for k, v in r.r.items() if k != eng)
        for w in writes:
            if w.w is not None:
                deps.append(w.w)
            deps.extend(w.r.items())
        return deps

    @staticmethod
    def _mark(tok, reads, writes):
        for r in reads:
            if r.r.get(tok[0], 0) < tok[1]:
                r.r[tok[0]] = tok[1]
        for w in writes:
            w.w = tok
            w.r = {}

    def op(self, eng, fn, reads=(), writes=()):
        waits = self._need(eng, self._deps(reads, writes, eng))
        self.cnt[eng] += 1
        tok = (eng, self.cnt[eng])
        self.prog[eng].append((waits, fn, tok))
        self._mark(tok, reads, writes)
        return tok

    def dma(self, q, out, in_, reads=(), writes=(), **kw):
        key = ("d", self.dma_i % N_DMA_SEMS)
        self.dma_i += 1
        prev = self.dma_val[key]
        deps = self._deps(reads, writes, q)
        if prev:
            deps.append((key, prev))
        waits = self._need(q, deps)
        self.dma_val[key] = prev + 16
        tok = (key, prev + 16)
        self.prog[q].append((waits, lambda e: e.dma_start(out=out, in_=in_, **kw), tok))
        self._mark(tok, reads, writes)
        return tok

    def wait_all(self, eng, toks):
        waits = self._need(eng, toks)
        self.prog[eng].append((waits, None, None))

    def emit(self):
        nc = self.nc
        marked = {e: set() for e in ENGS}
        for e in ENGS:
            for waits, fn, tok in self.prog[e]:
                for key, val in waits:
                    if key in marked:
                        marked[key].add(val)
        rank = {}
        for e in ENGS:
            for i, v in enumerate(sorted(marked[e])):
                rank[(e, v)] = i + 1
        sems = {}
        for e in ENGS:
            sems[e] = self.st.enter_context(nc.semaphore(f"sem_{e}"))
        for i in range(N_DMA_SEMS):
            sems[("d", i)] = self.st.enter_context(nc.semaphore(f"sem_d{i}"))
        handles = {"pe": "tensor", "act": "scalar", "dve": "vector", "pool": "gpsimd", "sp": "sync"}

        def run(e, eng_handle):
            for waits, fn, tok in self.prog[e]:
                for key, val in waits:
                    v = rank[(key, val)] if key in marked else val
                    eng_handle.wait_ge(sems[key], v)
                if fn is None:
                    continue
                inst = fn(eng_handle)
                if tok[0] in marked:
                    if tok[1] in marked[tok[0]]:
                        inst.then_inc(sems[tok[0]], 1)
                else:
                    inst.then_inc(sems[tok[0]], 16)

        with nc.Block() as block:
            for e in ENGS:
                getattr(block, handles[e])(lambda h, e=e: run(e, h))


D = 1024
DFF = 2816
NFF = 22
DSSM = 2048
DXBC = 3072
DIN = 10272
OFF_AB, OFF_AC, OFF_AH, OFF_Z, OFF_X, OFF_B, OFF_C, OFF_DT, OFF_GA, OFF_GB = (
    0, 1024, 2048, 3072, 5120, 7168, 7680, 8192, 8224, 9248)
EPS = 1e-6
TSMAX = 592
NEGBIG = -30000.0
CONV_ENG = "dve"
USE_HILO = False
USE_YCOPY = False
USE_WSCALE = False


class Kern:
    def __init__(self, S, nc, d):
        self.S = S
        self.nc = nc
        self.d = d
        self.res = {}
        self.bank_i = 0
        self.pinned = set()
        self.page_i = 0

    def R(self, *key):
        r = self.res.get(key)
        if r is None:
            r = self.res[key] = Res(str(key), excl=(key[0] == "bank"))
        return r

    def alloc(self):
        S = self.S
        self.banks = [S.ps([128, 512], F32, name=f"bank{i}") for i in range(8)]
        self.pages = [S.sb([128, 8192], BF16, name=f"page{i}") for i in range(4)]
        self.r = S.sb([128, 8, TSMAX], F32, name="r")
        self.xn = S.sb([128, 8, TSMAX], BF16, name="xn")
        self.Y = S.sb([128, 16, TSMAX], BF16, name="Y")
        self.ident_f = S.sb([128, 128], F32, name="ident_f")
        self.ident_b = S.sb([128, 128], BF16, name="ident_b")
        self.ones_b = S.sb([128, 128], BF16, name="ones_b")
        self.BT = S.sb([128, 128], F32, name="BT")
        self.NEG = S.sb([128, 128], BF16, name="NEGm")
        self.BT64 = S.sb([64, 64], F32, name="BT64")
        self.BO64 = S.sb([64, 64], F32, name="BO64")
        self.NEG64 = S.sb([64, 64], BF16, name="NEG64")
        self.ones64 = S.sb([64, 128], F32, name="ones64")
        self.sel = S.sb([64, 16], F32, name="sel")
        self.sel3 = S.sb([64, 16], F32, name="sel3")
        self.maskall = S.sb([128, 16, 64], BF16, name="maskall")
        self.hhm = S.sb([128, 1], F32, name="hhm")
        self.cvec = S.sb([128, 96], F32, name="cvec")
        self.cw = S.sb([128, 96], F32, name="cw")
        self.cst1 = S.sb([96, 128], F32, name="cst1")
        self.cst2 = S.sb([96, 128], F32, name="cst2")
        self.dtb = S.sb([32, 1], F32, name="dtb")
        self.alog = S.sb([32, 1], F32, name="alog")
        self.aneg = S.sb([32, 1], F32, name="aneg")
        self.Dbc = S.sb([128, 32], F32, name="Dbc")
        self.stg = [S.sb([128, 1024], F32, name=f"stg{i}") for i in range(2)]
        self.stg_i = 0
        self.sg = [S.sb([128, 512], F32, name=f"sg{i}") for i in range(2)]
        self.hb = S.sb([128, 4, 512], BF16, name="hb")
        self.ma = S.sb([128, 8, 512], BF16, name="ma")
        self.haloA = S.sb([128, 8, 2], F32, name="haloA")
        self.stA = S.sb([128, 8, 32], F32, name="stA")
        self.outA = self.stA
        self.pre = S.sb([128, 2, 520], F32, name="pre")
        self.xc = S.sb([128, 4, 512], F32, name="xc")
        self.Bc = S.sb([128, 512], BF16, name="Bc")
        self.Cc = S.sb([128, 512], BF16, name="Cc")
        self.haloB = S.sb([128, 24, 3], F32, name="haloB")
        self.stB = S.sb([128, 24, 48], F32, name="stB")
        self.outB = self.stB
        self.dtfm = S.sb([32, TSMAX], F32, name="dtfm")
        self.dafm = S.sb([32, TSMAX], F32, name="dafm")
        self.ldtfm = S.sb([32, TSMAX], F32, name="ldtfm")
        NCH = 6
        self.dt_t = S.sb([128, NCH, 32], F32, name="dt_t")
        self.da_t = S.sb([128, NCH, 32], F32, name="da_t")
        self.nacs_t = S.sb([128, NCH, 32], F32, name="nacs_t")
        self.wdec_t = S.sb([128, NCH, 32], F32, name="wdec_t")
        self.eacs_t = S.sb([128, NCH, 32], F32, name="eacs_t")
        self.etot_t = S.sb([128, NCH, 32], F32, name="etot_t")
        self.tmp32 = S.sb([128, 32], F32, name="tmp32")
        self.gz = S.sb([128, 4, 512], BF16, name="gz")
        self.xdt = S.sb([128, 512], BF16, name="xdt")
        self.xw2 = [S.sb([128, 512], BF16, name=f"xw{i}") for i in range(2)]
        self.xD2 = [S.sb([128, 512], F32, name=f"xD{i}") for i in range(2)]
        self.Btok2 = [S.sb([128, 128], BF16, name=f"Btok{i}") for i in range(2)]
        self.cbT = S.sb([128, 128], BF16, name="cbT")
        self.LT = S.sb([128, 8, 128], BF16, name="LT")
        self.MT = S.sb([128, 8, 128], BF16, name="MT")
        self.t1 = S.sb([128, 512], F32, name="t1")
        self.yt = S.sb([128, 512], F32, name="yt")
        self.sz = S.sb([128, 512], F32, name="sz")
        self.ynb = S.sb([128, 512], BF16, name="ynb")
        self.hT = S.sb([128, 2048], F32, name="hT")
        self.hTb = S.sb([128, 512], BF16, name="hTb")
        self.hTb2 = [self.hTb, S.sb([128, 512], BF16, name="hTb_b")]
        self.etn = S.sb([128, 16, 16], F32, name="etn")

    def bank(self):
        while True:
            i = self.bank_i % 8
            self.bank_i += 1
            if i not in self.pinned:
                break
        self.last_bank = i
        return self.banks[i], self.R("bank", i)

    def mm(self, out, lhsT, rhs, start, stop, reads, writes):
        self.S.op("pe", lambda e: e.matmul(out, lhsT=lhsT, rhs=rhs, start=start, stop=stop), reads, writes)

    def tp(self, out, in_, ident, reads, writes):
        self.S.op("pe", lambda e: e.transpose(out, in_, ident), reads, writes)

    def act(self, out, in_, func, reads, writes, **kw):
        self.S.op("act", lambda e: e.activation(out=out, in_=in_, func=func, **kw), reads, writes)

    def tt(self, eng, out, in0, in1, op, reads, writes):
        self.S.op(eng, lambda e: e.tensor_tensor(out=out, in0=in0, in1=in1, op=op), reads, writes)

    def ts(self, eng, out, in0, s1, s2, op0, op1, reads, writes):
        self.S.op(eng, lambda e: e.tensor_scalar(out=out, in0=in0, scalar1=s1, scalar2=s2, op0=op0, op1=op1), reads, writes)

    def stt(self, eng, out, in0, scalar, in1, op0, op1, reads, writes):
        self.S.op(eng, lambda e: e.scalar_tensor_tensor(out=out, in0=in0, scalar=scalar, in1=in1, op0=op0, op1=op1), reads, writes)

    def cp(self, eng, out, in_, reads, writes):
        if eng == "act":
            self.S.op("act", lambda e: e.copy(out=out, in_=in_), reads, writes)
        else:
            self.S.op(eng, lambda e: e.tensor_copy(out=out, in_=in_), reads, writes)

    def asel(self, out, in_, pattern, cmp, fill, base, cm, res):
        self.S.op("pool", lambda e: e.affine_select(out=out, in_=in_, pattern=pattern, compare_op=cmp,
                                                    fill=fill, base=base, channel_multiplier=cm), [res], [res])

    def setup(self):
        S, d = self.S, self.d
        R = self.R
        ms = lambda t, v, res: S.op("pool", lambda e: e.memset(t, v), [], [res])
        ms(self.ident_f[:], 0.0, R("ident_f"))
        self.asel(self.ident_f[:], self.ident_f[:], [[-1, 128]], ALU.not_equal, 1.0, 0, 1, R("ident_f"))
        self.cp("dve", self.ident_b[:], self.ident_f[:], [R("ident_f")], [R("ident_b")])
        ms(self.ones_b[:], 1.0, R("ones_b"))
        ms(self.ones64[:], 1.0, R("ones64"))
        ms(self.BT[:], 1.0, R("BT"))
        self.asel(self.BT[:], self.BT[:], [[1, 128]], ALU.is_ge, 0.0, 0, -1, R("BT"))
        ms(self.NEG[:], 0.0, R("NEG"))
        self.asel(self.NEG[:], self.NEG[:], [[1, 128]], ALU.is_ge, NEGBIG, 0, -1, R("NEG"))
        v3 = lambda t: t[:].rearrange("p (b t) -> p b t", t=4)
        ms(self.BO64[:], 1.0, R("BO64"))
        self.asel(v3(self.BO64), v3(self.BO64), [[-4, 16], [0, 4]], ALU.is_ge, 0.0, 0, 1, R("BO64"))
        self.asel(v3(self.BO64), v3(self.BO64), [[4, 16], [0, 4]], ALU.is_ge, 0.0, 3, -1, R("BO64"))
        ms(self.BT64[:], 1.0, R("BT64"))
        self.asel(v3(self.BT64), v3(self.BT64), [[-4, 16], [0, 4]], ALU.is_ge, 0.0, 0, 1, R("BT64"))
        self.asel(v3(self.BT64), v3(self.BT64), [[4, 16], [1, 4]], ALU.is_ge, 0.0, 0, -1, R("BT64"))
        ms(self.NEG64[:], 0.0, R("NEG64"))
        self.asel(v3(self.NEG64), v3(self.NEG64), [[-4, 16], [0, 4]], ALU.is_ge, NEGBIG, 0, 1, R("NEG64"))
        self.asel(v3(self.NEG64), v3(self.NEG64), [[4, 16], [1, 4]], ALU.is_ge, NEGBIG, 0, -1, R("NEG64"))
        ms(self.sel[:], 1.0, R("sel"))
        self.asel(self.sel[:], self.sel[:], [[-4, 16]], ALU.is_ge, 0.0, 0, 1, R("sel"))
        self.asel(self.sel[:], self.sel[:], [[4, 16]], ALU.is_ge, 0.0, 3, -1, R("sel"))
        ms(self.sel3[:], 0.0, R("sel3"))
        self.asel(self.sel3[:], self.sel3[:], [[-4, 16]], ALU.not_equal, 1.0, -3, 1, R("sel3"))
        ms(self.maskall[:], 1.0, R("maskall"))
        self.asel(self.maskall[:], self.maskall[:], [[-4, 16], [1, 64]], ALU.is_ge, 0.0, 0, 0, R("maskall"))
        self.asel(self.maskall[:], self.maskall[:], [[4, 16], [-1, 64]], ALU.is_ge, 0.0, 3, 0, R("maskall"))
        ms(self.hhm[:], 1.0, R("hhm"))
        self.asel(self.hhm[:], self.hhm[:], [[0, 1]], ALU.is_ge, 0.0, -64, 1, R("hhm"))
        S.dma("sp", self.cst1[:], d["cst1"], writes=[R("cst1")])
        S.dma("sp", self.cst2[:], d["cst2"], writes=[R("cst2")])
        bk, br = self.bank()
        self.tp(bk[:, 0:96], self.cst1[:], self.ident_f[0:96, 0:96], [R("cst1"), R("ident_f")], [br])
        self.cp("dve", self.cvec[:], bk[:, 0:96], [br], [R("cvec")])
        bk, br = self.bank()
        self.tp(bk[:, 0:96], self.cst2[:], self.ident_f[0:96, 0:96], [R("cst2"), R("ident_f")], [br])
        self.cp("dve", self.cw[:], bk[:, 0:96], [br], [R("cw")])
        S.dma("sp", self.dtb[:], d["dtb"].rearrange("o h -> h o"), writes=[R("dtb")], allow_slow_non_contiguous=True)
        S.dma("sp", self.alog[:], d["alog"].rearrange("o h -> h o"), writes=[R("alog")], allow_slow_non_contiguous=True)
        self.act(self.aneg[:], self.alog[:], AF.Exp, [R("alog")], [R("aneg")])
        self.ts("dve", self.aneg[:], self.aneg[:], -1.0, None, ALU.mult, ALU.bypass, [R("aneg")], [R("aneg")])
        S.dma("sp", self.Dbc[:], d["dsk"].partition_broadcast(128), writes=[R("Dbc")])
        st = self.stg[0]
        S.dma("sp", st[0:32, :], d["sca"], writes=[R("stg", 0)])
        for c in range(8):
            bk, br = self.bank()
            self.tp(bk[:, 0:32], st[0:32, c * 128:(c + 1) * 128], self.ident_f[0:32, 0:32], [R("stg", 0), R("ident_f")], [br])
            self.cp("dve", self.stA[:, c, :], bk[:, 0:32], [br], [R("stA", c)])
        for part in range(3):
            st = self.stg[1]
            S.dma("sp", st[0:48, :], d["ssc"][:, part * 1024:(part + 1) * 1024], writes=[R("stg", 1)])
            for c in range(8):
                bk, br = self.bank()
                self.tp(bk[:, 0:48], st[0:48, c * 128:(c + 1) * 128], self.ident_f[0:48, 0:48], [R("stg", 1), R("ident_f")], [br])
                self.cp("dve", self.stB[:, part * 8 + c, :], bk[:, 0:48], [br], [R("stB", part * 8 + c)])
        S.op("pool", lambda e: e.memset(self.haloA[:], 0.0), [], [R("haloA", c) for c in range(8)])
        S.op("pool", lambda e: e.memset(self.haloB[:], 0.0), [], [R("haloB", c) for c in range(24)])
        ms(self.hT[:], 0.0, R("hT"))

    def nw(self, which, c):
        return self.cvec[:, which * 8 + c: which * 8 + c + 1]

    def caw(self, k, c):
        return self.cvec[:, 32 + k * 8 + c: 32 + k * 8 + c + 1]

    def scb(self, c):
        return self.cvec[:, 56 + c: 56 + c + 1]

    def snw(self, c):
        return self.cvec[:, 80 + c: 80 + c + 1]

    def scw(self, k, c):
        return self.cw[:, k * 24 + c: k * 24 + c + 1]

    def load_x(self, sti):
        S, d, R = self.S, self.d, self.R
        blocks = []
        if sti == 0:
            blocks.append(("sm", 0, 80))
            for i in range(4):
                blocks.append(("p", 80 + 128 * i, 128 * i))
        else:
            for i in range(4):
                blocks.append(("p", 128 * i, 512 * sti + 128 * i))
        for kind, c0, src0 in blocks:
            si = self.stg_i % 2
            self.stg_i += 1
            st = self.stg[si]
            if kind == "sm":
                S.dma("pool", st[0:64, :], d["xs"], writes=[R("stg", si)])
                S.dma("pool", st[64:80, :], d["meta"], writes=[R("stg", si)])
                n = 80
            else:
                S.dma("pool", st[:, :], d["xp"][src0:src0 + 128, :], writes=[R("stg", si)])
                n = 128
            for half in range(2):
                bk, br = self.bank()
                for j in range(4):
                    c = half * 4 + j
                    self.tp(bk[:, j * 128:j * 128 + n], st[0:n, c * 128:(c + 1) * 128], self.ident_f[0:n, 0:n],
                            [R("stg", si), R("ident_f")], [br])
                eng = "act" if half == 0 else "dve"
                self.cp(eng, self.r[:, half * 4:half * 4 + 4, c0:c0 + n],
                        bk[:].rearrange("p (j n) -> p j n", j=4)[:, :, 0:n], [br], [R("r", half * 4 + j_) for j_ in range(4)])

    def rmsnorm(self, which, c0, n, out_fn):
        R = self.R
        bk, br = self.bank()
        for c in range(8):
            self.act(self.ma[:, c, 0:n], self.r[:, c, c0:c0 + n], AF.Square, [R("r", c)], [R("ma", c)])
            self.mm(bk[:, 0:n], self.ones_b[:], self.ma[:, c, 0:n], c == 0, c == 7, [R("ones_b"), R("ma", c)], [br])
        self.act(self.t1[:, 0:n], bk[:, 0:n], AF.Ln, [br], [R("t1")], scale=1.0 / D, bias=EPS)
        self.act(self.t1[:, 0:n], self.t1[:, 0:n], AF.Exp, [R("t1")], [R("t1")], scale=-0.5)
        for c in range(8):
            dst, wres = out_fn(c)
            self.stt("dve", dst, self.r[:, c, c0:c0 + n], self.nw(which, c), self.t1[:, 0:n], ALU.mult, ALU.mult,
                     [R("r", c), R("t1"), R("cvec")], wres if isinstance(wres, list) else [wres])

    def norm_to_xn(self, which, tiles):
        for (c0, n) in tiles:
            self.rmsnorm(which, c0, n, lambda c: (self.xn[:, c, c0:c0 + n], self.R("xn", c)))

    def take_pages(self, k):
        ids = [(self.page_i + j) % 4 for j in range(k)]
        self.page_i += k
        return ids

    def wdma(self, dst, src, pid):
        self.S.dma("pool", dst, src, writes=[self.R("page", pid)])

    def wview(self, w):
        return w[0].rearrange("(k p) n -> p k n", p=128)

    def ffn_items(self, wgu, wd, tiles):
        items = []
        slabs = [(0, 4), (4, 4), (8, 4), (12, 4), (16, 4), (20, 2)]
        for (j0, nj) in slabs:
            def load(pids, j0=j0, nj=nj):
                pg, pd = self.pages[pids[0]], self.pages[pids[1]]
                gv = pg[:, 0:8 * 2 * nj * 128].rearrange("p (k s n) -> p k s n", k=8, s=2)
                src = self.wview(wgu)
                self.wdma(gv[:, :, 0, :], src[:, :, j0 * 128:(j0 + nj) * 128], pids[0])
                self.wdma(gv[:, :, 1, :], src[:, :, DFF + j0 * 128:DFF + (j0 + nj) * 128], pids[0])
                dv = pd[:, 0:nj * 1024].rearrange("p (j n) -> p j n", j=nj)
                self.wdma(dv, wd[0].rearrange("(j p) n -> p j n", p=128)[:, j0:j0 + nj, :], pids[1])

            def compute(pids, j0=j0, nj=nj):
                R = self.R
                pg, pd = self.pages[pids[0]], self.pages[pids[1]]
                gv = pg[:, 0:8 * 2 * nj * 128].rearrange("p (k s n) -> p k s n", k=8, s=2)
                dv = pd[:, 0:nj * 1024].rearrange("p (j n) -> p j n", j=nj)
                pr0, pr1 = R("page", pids[0]), R("page", pids[1])
                for (c0, n) in tiles:
                    for jj in range(nj):
                        bg, rg = self.bank()
                        for k in range(8):
                            self.mm(bg[:, 0:n], gv[:, k, 0, jj * 128:(jj + 1) * 128], self.xn[:, k, c0:c0 + n], k == 0, k == 7, [pr0, R("xn", k)], [rg])
                        bu, ru = self.bank()
                        for k in range(8):
                            self.mm(bu[:, 0:n], gv[:, k, 1, jj * 128:(jj + 1) * 128], self.xn[:, k, c0:c0 + n], k == 0, k == 7, [pr0, R("xn", k)], [ru])
                        sg = self.sg[jj % 2]
                        self.act(sg[:, 0:n], bg[:, 0:n], AF.Silu, [rg], [R("sg", jj % 2)])
                        self.tt("dve", self.hb[:, jj, 0:n], sg[:, 0:n], bu[:, 0:n], ALU.mult, [R("sg", jj % 2), ru], [R("hb", jj)])
                    for c in range(8):
                        bd, rd = self.bank()
                        for jj in range(nj):
                            self.mm(bd[:, 0:n], dv[:, jj, c * 128:(c + 1) * 128], self.hb[:, jj, 0:n], jj == 0, jj == nj - 1, [pr1, R("hb", jj)], [rd])
                        self.stt("dve", self.r[:, c, c0:c0 + n], bd[:, 0:n], 0.5, self.r[:, c, c0:c0 + n], ALU.mult, ALU.add,
                                 [rd, R("r", c)], [R("r", c)])
            items.append((2, load, compute))
        return items

    def conv(self, out3, src3, L, wcols, reads, writes, acc=False):
        W = len(wcols)
        eng = CONV_ENG
        if not acc:
            self.ts(eng, out3, src3[:, :, 0:L], wcols[0], None, ALU.mult, ALU.bypass, reads, writes)
        for k in range(0 if acc else 1, W):
            self.stt(eng, out3, src3[:, :, k:k + L], wcols[k], out3, ALU.mult, ALU.add, reads + writes, writes)

    def a_items(self, sti, w_in, tiles):
        items = []
        for s in range(4):
            def load(pids, s=s):
                pv = self.pages[pids[0]][:, 0:8 * 3 * 256].rearrange("p (k s n) -> p k s n", k=8, s=3)
                src = self.wview(w_in)
                for i, off in enumerate((OFF_AB, OFF_AC, OFF_AH)):
                    self.wdma(pv[:, :, i, :], src[:, :, off + s * 256: off + (s + 1) * 256], pids[0])

            def compute(pids, s=s):
                R = self.R
                pv = self.pages[pids[0]][:, 0:8 * 3 * 256].rearrange("p (k s n) -> p k s n", k=8, s=3)
                pr = R("page", pids[0])
                for ti, (c0, n) in enumerate(tiles):
                    for jj in range(2):
                        c = s * 2 + jj
                        bks = []
                        for i in range(3):
                            bk, br = self.bank()
                            for k in range(8):
                                self.mm(bk[:, 0:n], pv[:, k, i, jj * 128:(jj + 1) * 128], self.xn[:, k, c0:c0 + n], k == 0, k == 7, [pr, R("xn", k)], [br])
                            bks.append((bk, br))
                        (bb, rb), (bc, rc), (bh, rh) = bks
                        ach = self.pre[:, jj, :]
                        ra = R("pre", jj)
                        cva = self.xc[:, jj, :]
                        rcv = R("xc_s0", jj)
                        sg = self.sg[jj]
                        rsg = R("sg", jj)
                        if sti == 0 and ti == 0:
                            a3 = ach[:, 0:96].rearrange("p (b t) -> p b t", t=6)
                            self.cp("dve", a3[:, :, 0:2], self.stA[:, c, :].rearrange("p (b k) -> p b k", k=2), [R("stA", c)], [ra])
                            self.cp("act", sg[:, 0:80], bc[:, 0:80], [rc], [rsg])
                            self.tt("dve", a3[:, :, 2:6], sg[:, 0:64].rearrange("p (b t) -> p b t", t=4),
                                    bh[:, 0:64].rearrange("p (b t) -> p b t", t=4), ALU.mult, [rsg, rh], [ra])
                            self.S.op("dve", lambda e, ach=ach: e.memset(ach[:, 100:102], 0.0), [], [ra])
                            self.tt("dve", ach[:, 102:118], sg[:, 64:80], bh[:, 64:80], ALU.mult, [rsg, rh], [ra])
                            wc = [self.caw(k, c) for k in range(3)]
                            self.conv(cva[:, 0:64].rearrange("p (b t) -> p b t", t=4), a3, 4, wc, [ra, R("cvec")], [rcv])
                            self.conv(cva[:, 64:80].rearrange("p (b t) -> p b t", b=1), ach[:, 100:118].rearrange("p (b t) -> p b t", b=1), 16, wc, [ra, R("cvec")], [rcv])
                            self.tt("dve", self.Y[:, c, c0:c0 + n], cva[:, 0:n], bb[:, 0:n], ALU.mult, [rcv, rb], [R("Y", c)])
                            self.cp("act", self.outA[:, c, :].rearrange("p (b k) -> p b k", k=2), a3[:, :, 4:6], [ra], [R("stA", c)])
                            self.cp("act", self.haloA[:, c, :], ach[:, 116:118], [ra], [R("haloA", c)])
                        else:
                            self.cp("act", ach[:, 0:2], self.haloA[:, c, :], [R("haloA", c)], [ra])
                            self.cp("act", sg[:, 0:n], bc[:, 0:n], [rc], [rsg])
                            self.tt("dve", ach[:, 2:2 + n], sg[:, 0:n], bh[:, 0:n], ALU.mult, [rsg, rh], [ra])
                            wc = [self.caw(k, c) for k in range(3)]
                            self.conv(cva[:, 0:n].rearrange("p (b t) -> p b t", b=1), ach[:, 0:2 + n].rearrange("p (b t) -> p b t", b=1), n, wc, [ra, R("cvec")], [rcv])
                            self.tt("dve", self.Y[:, c, c0:c0 + n], cva[:, 0:n], bb[:, 0:n], ALU.mult, [rcv, rb], [R("Y", c)])
                            self.cp("act", self.haloA[:, c, :], ach[:, n:n + 2], [ra], [R("haloA", c)])
            items.append((1, load, compute))
        return items

    def atail_item(self, w_in, w_a_out, w_o, tiles):
        def load(pids):
            v = lambda i: self.pages[pids[i]][:, :].rearrange("p (k n) -> p k n", k=8)
            self.wdma(v(0), self.wview(w_a_out), pids[0])
            self.wdma(v(1), self.wview(w_in)[:, :, OFF_GA:OFF_GA + 1024], pids[1])
            self.wdma(v(2), self.wview(w_o), pids[2])

        def compute(pids):
            R = self.R
            v = lambda i: self.pages[pids[i]][:, :].rearrange("p (k n) -> p k n", k=8)
            pr = [R("page", p) for p in pids]
            for (c0, n) in tiles:
                for c in range(8):
                    by, ry = self.bank()
                    for k in range(8):
                        self.mm(by[:, 0:n], v(0)[:, k, c * 128:(c + 1) * 128], self.Y[:, k, c0:c0 + n], k == 0, k == 7, [pr[0], R("Y", k)], [ry])
                    bg, rg = self.bank()
                    for k in range(8):
                        self.mm(bg[:, 0:n], v(1)[:, k, c * 128:(c + 1) * 128], self.xn[:, k, c0:c0 + n], k == 0, k == 7, [pr[1], R("xn", k)], [rg])
                    sg = self.sg[c % 2]
                    self.act(sg[:, 0:n], bg[:, 0:n], AF.Sigmoid, [rg], [R("sg", c % 2)])
                    self.tt("dve", self.ma[:, c, 0:n], sg[:, 0:n], by[:, 0:n], ALU.mult, [R("sg", c % 2), ry], [R("ma", c)])
                self.wo_apply(v(2), pr[2], c0, n)
        return (3, load, compute)

    def wo_apply(self, wv, pr, c0, n):
        R = self.R
        for c2 in range(8):
            bm, rm = self.bank()
            for k in range(8):
                self.mm(bm[:, 0:n], wv[:, k, c2 * 128:(c2 + 1) * 128], self.ma[:, k, 0:n], k == 0, k == 7, [pr, R("ma", k)], [rm])
            self.tt("dve", self.r[:, c2, c0:c0 + n], bm[:, 0:n], self.r[:, c2, c0:c0 + n], ALU.add, [rm, R("r", c2)], [R("r", c2)])

    def group_norm(self, g, c0, n):
        R = self.R
        bk, br = self.bank()
        for q in range(4):
            k = 4 * g + q
            sq, rsq = (self.ynb, R("ynb")) if q % 2 == 0 else (self.xdt, R("xdt"))
            self.act(sq[:, 0:n], self.Y[:, k, c0:c0 + n], AF.Square, [R("Y", k)], [rsq])
            self.mm(bk[:, 0:n], self.ones_b[:], sq[:, 0:n], q == 0, q == 3, [R("ones_b"), rsq], [br])
        self.act(self.t1[:, 0:n], bk[:, 0:n], AF.Ln, [br], [R("t1")], scale=1.0 / 512, bias=EPS)
        self.act(self.t1[:, 0:n], self.t1[:, 0:n], AF.Exp, [R("t1")], [R("t1")], scale=-0.5)
        for q in range(4):
            k = 4 * g + q
            self.stt("dve", self.Y[:, k, c0:c0 + n], self.Y[:, k, c0:c0 + n], self.snw(k), self.t1[:, 0:n], ALU.mult, ALU.mult,
                     [R("Y", k), R("t1"), R("cvec")], [R("Y", k)])

    def y_normalize(self, sti, ti, c0, n):
        R = self.R
        chunks = self.chunks_of(sti, ti)
        lo, hi = min(ch[3] for ch in chunks) * 4, (max(ch[3] for ch in chunks) + 1) * 4
        rt = self.rstd_tab[:, lo:hi]
        self.ts("dve", rt, self.ss_tab[:, lo:hi], 1.0 / 512, EPS, ALU.mult, ALU.add, [R("ss_tab")], [R("rstd_tab")])
        self.S.op("dve", lambda e: e.reciprocal(out=rt, in_=rt), [R("rstd_tab")], [R("rstd_tab")])
        self.act(rt, rt, AF.Sqrt, [R("rstd_tab")], [R("rstd_tab")])
        for g in range(4):
            bk, br = self.bank()
            for (kind, cc0, TK, ci) in chunks:
                lc = cc0 - c0
                col = ci * 4 + g
                self.mm(bk[:, lc:lc + TK], self.rstd_tab[0:TK, col:col + 1].broadcast_to([TK, 128]), self.ident_f[0:TK, 0:TK], True, True,
                        [R("rstd_tab"), R("ident_f")], [br])
            for q in range(4):
                k = 4 * g + q
                self.stt("dve", self.Y[:, k, c0:c0 + n], self.Y[:, k, c0:c0 + n], self.snw(k), bk[:, 0:n], ALU.mult, ALU.mult,
                         [R("Y", k), br, R("cvec")], [R("Y", k)])

    def btail_item(self, sti, w_in, w_b_out, w_o, tiles):
        def load(pids):
            v = lambda i: self.pages[pids[i]][:, :].rearrange("p (k n) -> p k n", k=8)
            src = self.wview(w_b_out)
            self.wdma(v(0), src[:, 0:8, :], pids[0])
            self.wdma(v(1), src[:, 8:16, :], pids[1])
            self.wdma(v(2), self.wview(w_in)[:, :, OFF_GB:OFF_GB + 1024], pids[2])
            self.wdma(v(3), self.wview(w_o), pids[3])

        def compute(pids):
            R = self.R
            v = lambda i: self.pages[pids[i]][:, :].rearrange("p (k n) -> p k n", k=8)
            pr = [R("page", p) for p in pids]
            for ti, (c0, n) in enumerate(tiles):
                for c in range(8):
                    by, ry = self.bank()
                    for k in range(16):
                        self.mm(by[:, 0:n], v(k // 8)[:, k % 8, c * 128:(c + 1) * 128], self.Y[:, k, c0:c0 + n], k == 0, k == 15, [pr[k // 8], R("Y", k)], [ry])
                    bg, rg = self.bank()
                    for k in range(8):
                        self.mm(bg[:, 0:n], v(2)[:, k, c * 128:(c + 1) * 128], self.xn[:, k, c0:c0 + n], k == 0, k == 7, [pr[2], R("xn", k)], [rg])
                    sg = self.sg[c % 2]
                    self.act(sg[:, 0:n], bg[:, 0:n], AF.Sigmoid, [rg], [R("sg", c % 2)])
                    self.tt("dve", self.ma[:, c, 0:n], sg[:, 0:n], by[:, 0:n], ALU.mult, [R("sg", c % 2), ry], [R("ma", c)])
                self.wo_apply(v(3), pr[3], c0, n)
        return (4, load, compute)

    def bview(self, bk):
        return bk[:].bitcast(BF16)

    def bc_hp(self, t, off, TK, pstep):
        return bass.AP(t, off, [[pstep, TK], [1, 8], [0, 64]])

    def chunks_of(self, sti, ti):
        if sti == 0:
            if ti == 0:
                return [("s", 0, 64, 0), ("m", 64, 16, 1)]
            return [("p", 80 + 128 * i, 128, 2 + i) for i in range(4)]
        return [("p", 128 * i, 128, i) for i in range(4)]

    def chunk_pre(self, kind, cc0, TK, ci):
        R = self.R
        NCH32 = 6 * 32
        BTx = self.BT64 if kind == "s" else self.BT
        BOx = self.BO64 if kind == "s" else self.ones_f
        bk, br = self.bank()
        self.tp(bk[0:TK, 0:32], self.dtfm[:, cc0:cc0 + TK], self.ident_f[0:32, 0:32], [R("dtfm"), R("ident_f")], [br])
        self.tp(bk[0:TK, 32:64], self.dafm[:, cc0:cc0 + TK], self.ident_f[0:32, 0:32], [R("dafm"), R("ident_f")], [br])
        self.tp(bk[0:TK, 64:96], self.ldtfm[:, cc0:cc0 + TK], self.ident_f[0:32, 0:32], [R("ldtfm"), R("ident_f")], [br])
        self.cp("dve", self.dt_t[0:TK, ci, :], bk[0:TK, 0:32], [br], [R("dt_t", ci)])
        self.cp("dve", self.da_t[0:TK, ci, :], bk[0:TK, 32:64], [br], [R("da_t", ci)])
        b2, r2 = self.bank()
        self.mm(b2[0:TK, 0:32], BTx[0:TK, 0:TK], self.da_t[0:TK, ci, :], True, True, [R("da_t", ci), R("BT"), R("BT64")], [r2])
        self.mm(b2[0:TK, 32:64], BOx[0:TK, 0:TK], self.da_t[0:TK, ci, :], True, True, [R("da_t", ci), R("BO64"), R("ones_f")], [r2])
        self.ts("dve", self.nacs_t[0:TK, ci, :], b2[0:TK, 0:32], -1.0, None, ALU.mult, ALU.bypass, [r2], [R("nacs", ci)])
        self.act(self.eacs_t[0:TK, ci, :], b2[0:TK, 0:32], AF.Exp, [r2], [R("eacs", ci)])
        self.act(self.etot_t[0:TK, ci, :], b2[0:TK, 32:64], AF.Exp, [r2], [R("etot", ci)])
        self.tt("dve", self.tmp32[0:TK, :], b2[0:TK, 32:64], self.nacs_t[0:TK, ci, :], ALU.add, [r2, R("nacs", ci)], [R("tmp32")])
        self.act(self.tmp32[0:TK, :], self.tmp32[0:TK, :], AF.Exp, [R("tmp32")], [R("tmp32")])
        self.tt("dve", self.wdec_t[0:TK, ci, :], self.tmp32[0:TK, :], self.dt_t[0:TK, ci, :], ALU.mult, [R("tmp32"), R("dt_t", ci)], [R("wdec", ci)])
        self.tt("dve", self.nacs_t[0:TK, ci, :], self.nacs_t[0:TK, ci, :], bk[0:TK, 64:96], ALU.add, [R("nacs", ci), br], [R("nacs", ci)])
        if kind == "s":
            in0 = bass.AP(self.etot_t, ci * 32, [[NCH32, 64], [0, 16], [1, 32]])
            in1 = bass.AP(self.sel3, 0, [[16, 64], [1, 16], [0, 32]])
            self.tt("dve", self.yt[0:64, :].rearrange("p (b h) -> p b h", b=16), in0, in1, ALU.mult, [R("etot", ci), R("sel3")], [R("yt")])
            b3, r3 = self.bank()
            self.mm(b3[:, 0:512], self.ones64[0:64, 0:128], self.yt[0:64, :], True, True, [R("yt"), R("ones64")], [r3])
            self.cp("dve", self.t1[:], b3[:], [r3], [R("t1")])
            tv = self.t1[:].rearrange("p (x two) -> p x two", two=2)
            ev = self.sz[:, 0:256]
            self.tt("dve", ev, tv[:, :, 1], tv[:, :, 0], ALU.subtract, [R("t1")], [R("sz")])
            self.stt("dve", self.etn[:].rearrange("p b i -> p (b i)"), ev, self.hhm[:, 0:1], tv[:, :, 0], ALU.mult, ALU.add,
                     [R("sz"), R("t1"), R("hhm")], [R("etn")])

    def bufset(self, bs):
        R = self.R
        if bs == 0:
            par = {"xc": [], "gz": [], "Bc": [], "Cc": []}
            bufs = dict(xc=self.xc, Bc=self.Bc, Cc=self.Cc, gz=self.gz)
        elif bs == "t0":
            pm = [R("ma", c) for c in range(3)]
            par = {"xc": pm, "gz": pm, "Bc": pm, "Cc": pm}
            mb = self.ma[:, :, :].rearrange("p c n -> p (c n)")
            mf = mb.bitcast(F32)
            bufs = dict(xc=mf[:, 0:320].rearrange("p (q n) -> p q n", q=4), Bc=mb[:, 640:720], Cc=mb[:, 720:800],
                        gz=mb[:, 800:1120].rearrange("p (q n) -> p q n", q=4))
        else:
            par = {"xc": [R("ma", c) for c in range(8)], "gz": [R("hb", c) for c in range(4)], "Bc": [R("sg", 0)], "Cc": [R("sg", 0)]}
            xc1 = self.ma[:, :, :].rearrange("p c n -> p (c n)").bitcast(F32).rearrange("p (q n) -> p q n", q=4)
            s0 = self.sg[0][:, :].bitcast(BF16)
            bufs = dict(xc=xc1, Bc=s0[:, 0:512], Cc=s0[:, 512:1024], gz=self.hb)
        B = dict(bufs)
        B["bs"] = bs
        B["rd"] = lambda name, q=0: [R(name + "_s%s" % bs, q)] + par[name]
        B["wr"] = lambda name, q=0: ([R(name + "_s%s" % bs, q)], par[name])
        return B

    def inproj_gen(self, sti, g, ti, c0, n, pids, B):
        R = self.R
        pa = self.pages[pids[0]][:, 0:8 * 800].rearrange("p (k n) -> p k n", k=8)
        pz = self.pages[pids[1]][:, 0:8 * 512].rearrange("p (k n) -> p k n", k=8)
        pr0, pr1 = R("page", pids[0]), R("page", pids[1])
        special = (sti == 0 and ti == 0)

        def cidof(q):
            return (4 * g + q) if q < 4 else (16 + g if q == 4 else 20 + g)

        def scratch(pq):
            return (self.sz, R("sz")) if pq == 0 else (self.sg[1], R("sg", 1))

        def stageA(q):
            cid = cidof(q)
            bk, br = self.bank()
            for k in range(8):
                self.mm(bk[:, 0:n], pa[:, k, q * 128:(q + 1) * 128], self.xn[:, k, c0:c0 + n], k == 0, k == 7, [pr0, R("xn", k)], [br])
            pq = q % 2
            rp = R("pre", pq)
            if special:
                p3 = self.pre[:, pq, 0:112].rearrange("p (b t) -> p b t", t=7)
                self.cp("dve", p3[:, :, 0:3], self.stB[:, cid, :].rearrange("p (b k) -> p b k", k=3), [R("stB", cid)], [rp])
                self.cp("act", p3[:, :, 3:7], bk[:, 0:64].rearrange("p (b t) -> p b t", t=4), [br], [rp])
                self.S.op("dve", lambda e, pq=pq: e.memset(self.pre[:, pq, 120:123], 0.0), [], [rp])
                self.cp("act", self.pre[:, pq, 123:139], bk[:, 64:80], [br], [rp])
            else:
                self.cp("dve", self.pre[:, pq, 0:3], self.haloB[:, cid, :], [R("haloB", cid)], [rp])
                self.cp("act", self.pre[:, pq, 3:3 + n], bk[:, 0:n], [br], [rp])

        def stageB(q):
            cid = cidof(q)
            pq = q % 2
            rp = R("pre", pq)
            cvb, rcv = scratch(pq)
            wc = [self.scw(k, cid) for k in range(4)]
            if q < 4:
                dst, (wdst, pdst) = B["xc"][:, q, 0:n], B["wr"]("xc", q)
            elif q == 4:
                dst, (wdst, pdst) = B["Bc"][:, 0:n], B["wr"]("Bc")
            else:
                dst, (wdst, pdst) = B["Cc"][:, 0:n], B["wr"]("Cc")
            if special:
                p3 = self.pre[:, pq, 0:112].rearrange("p (b t) -> p b t", t=7)
                self.conv(cvb[:, 0:64].rearrange("p (b t) -> p b t", t=4), p3, 4, wc, [rp, R("cw")], [rcv])
                self.conv(cvb[:, 64:80].rearrange("p (b t) -> p b t", b=1),
                          self.pre[:, pq, 120:139].rearrange("p (b t) -> p b t", b=1), 16, wc, [rp, R("cw")], [rcv])
                self.cp("act", self.outB[:, cid, :].rearrange("p (b k) -> p b k", k=3), p3[:, :, 4:7], [rp], [R("stB", cid)])
                self.cp("act", self.haloB[:, cid, :], self.pre[:, pq, 136:139], [rp], [R("haloB", cid)])
            else:
                self.conv(cvb[:, 0:n].rearrange("p (b t) -> p b t", b=1),
                          self.pre[:, pq, 0:3 + n].rearrange("p (b t) -> p b t", b=1), n, wc, [rp, R("cw")], [rcv])
                self.cp("act", self.haloB[:, cid, :], self.pre[:, pq, n:n + 3], [rp], [R("haloB", cid)])
            self.act(dst, cvb[:, 0:n], AF.Silu, [rcv, R("cvec")] + pdst, wdst, bias=self.scb(cid))

        def gate(q):
            bk, br = self.bank()
            for k in range(8):
                self.mm(bk[:, 0:n], pz[:, k, q * 128:(q + 1) * 128], self.xn[:, k, c0:c0 + n], k == 0, k == 7, [pr1, R("xn", k)], [br])
            wg, pg = B["wr"]("gz", q)
            self.act(B["gz"][:, q, 0:n], bk[:, 0:n], AF.Silu, [br] + pg, wg)

        stageA(0)
        yield "A"
        for q in range(6):
            if q + 1 < 6:
                stageA(q + 1)
                yield "A"
            stageB(q)
            if q < 4:
                gate(q)
            yield "B"
        if g == 0:
            bk, br = self.bank()
            for k in range(8):
                self.mm(bk[0:32, 0:n], pa[:, k, 768:800], self.xn[:, k, c0:c0 + n], k == 0, k == 7, [pr0, R("xn", k)], [br])
            self.act(self.dtfm[:, c0:c0 + n], bk[0:32, 0:n], AF.Exp, [br, R("dtb")], [R("dtfm")], bias=self.dtb[:, 0:1])
            self.act(self.dtfm[:, c0:c0 + n], self.dtfm[:, c0:c0 + n], AF.Ln, [R("dtfm")], [R("dtfm")], bias=1.0)
            self.ts("dve", self.dafm[:, c0:c0 + n], self.dtfm[:, c0:c0 + n], self.aneg[:, 0:1], None, ALU.mult, ALU.bypass,
                    [R("dtfm"), R("aneg")], [R("dafm")])
            self.act(self.ldtfm[:, c0:c0 + n], self.dtfm[:, c0:c0 + n], AF.Ln, [R("dtfm")], [R("ldtfm")])
            for ch in self.chunks_of(sti, ti):
                self.chunk_pre(*ch)
                yield

    def b_items(self, sti, w_in, tiles):
        items = []
        self.bp = {}

        def drain(gen):
            for _ in gen:
                pass
        for g in range(4):
            def load(pids, g=g):
                self.bp[g] = pids
                pa = self.pages[pids[0]][:, 0:8 * 800].rearrange("p (k n) -> p k n", k=8)
                pz = self.pages[pids[1]][:, 0:8 * 512].rearrange("p (k n) -> p k n", k=8)
                src = self.wview(w_in)
                self.wdma(pa[:, :, 0:512], src[:, :, OFF_X + g * 512: OFF_X + (g + 1) * 512], pids[0])
                self.wdma(pa[:, :, 512:640], src[:, :, OFF_B + g * 128: OFF_B + (g + 1) * 128], pids[0])
                self.wdma(pa[:, :, 640:768], src[:, :, OFF_C + g * 128: OFF_C + (g + 1) * 128], pids[0])
                if g == 0:
                    self.wdma(pa[:, :, 768:800], src[:, :, OFF_DT: OFF_DT + 32], pids[0])
                self.wdma(pz, src[:, :, OFF_Z + g * 512: OFF_Z + (g + 1) * 512], pids[1])

            def merge(ssd, inp, policy):
                ngap = 0
                state = {"inp": inp}

                def adv_in(nb):
                    while state["inp"] is not None and nb > 0:
                        try:
                            if next(state["inp"]) == "B":
                                nb -= 1
                        except StopIteration:
                            state["inp"] = None
                for tag in ssd:
                    if tag == "gap":
                        ngap += 1
                        adv_in(policy(ngap))
                adv_in(100)

            def compute(pids, g=g):
                R = self.R
                steady = lambda k: 2 if k <= 2 else 1
                if sti == 0:
                    (c00, n0), (c01, n1) = tiles
                    BT0, B0 = self.bufset("t0"), self.bufset(0)
                    if g == 0:
                        drain(self.inproj_gen(sti, 0, 0, c00, n0, pids, BT0))
                    merge(self.ssd_tile(sti, g, 0, c00, self.chunks_of(sti, 0), BT0),
                          self.inproj_gen(sti, g, 1, c01, n1, pids, B0), lambda k: 1)
                    self.group_norm(g, c00, n0)
                    nxt = None
                    if g + 1 < 4:
                        assert self.bp.get(g + 1) is not None, "next group's weights not prefetched"
                        nxt = self.inproj_gen(sti, g + 1, 0, c00, n0, self.bp[g + 1], BT0)
                    merge(self.ssd_tile(sti, g, 1, c01, self.chunks_of(sti, 1), B0), nxt, steady)
                    self.group_norm(g, c01, n1)
                    return
                (c0, n) = tiles[0]
                if g == 0:
                    drain(self.inproj_gen(sti, 0, 0, c0, n, pids, self.bufset(0)))
                nxt = None
                if g + 1 < 4:
                    assert self.bp.get(g + 1) is not None, "next group's weights not prefetched"
                    nxt = self.inproj_gen(sti, g + 1, 0, c0, n, self.bp[g + 1], self.bufset((g + 1) % 2))
                merge(self.ssd_tile(sti, g, 0, c0, self.chunks_of(sti, 0), self.bufset(g % 2)), nxt, steady)
                self.group_norm(g, c0, n)
                if sti == 3:
                    self.state_out(g)
            items.append((2, load, compute))
        return items

    def bankp(self):
        bk, br = self.bank()
        i = self.last_bank
        self.pinned.add(i)
        return bk, br, i

    def ssd_front(self, sti, g, c0, ch, seq, fr, B):
        R, S, d = self.R, self.S, self.d
        kind, cc0, TK, ci = ch
        lc = cc0 - c0
        NCH32 = 6 * 32
        off = ci * 32 + 8 * g
        pb = seq % 2
        fr["pb"] = pb
        xw, xD, Btok = self.xw2[pb], self.xD2[pb], self.Btok2[pb]
        BTx = self.BT64 if kind == "s" else self.BT
        NEGx = self.NEG64 if kind == "s" else self.NEG
        bx, rx, bx_i = self.bankp()
        for q in range(4):
            self.tp(bx[0:TK, q * 128:(q + 1) * 128], B["xc"][:, q, lc:lc + TK], self.ident_f[:, :], B["rd"]("xc", q) + [R("ident_f")], [rx])
        bB, rB, bB_i = self.bankp()
        bBv = self.bview(bB)
        self.tp(bBv[0:TK, 0:128], B["Bc"][:, lc:lc + TK], self.ident_b[:, :], B["rd"]("Bc") + [R("ident_b")], [rB])
        yield
        x3 = bx[0:TK, :].rearrange("p (h j) -> p h j", h=8)
        v3 = lambda t: t[0:TK, :].rearrange("p (h j) -> p h j", h=8)
        self.cp("act", self.xdt[0:TK, :], bx[0:TK, :], [rx], [R("xdt")])
        self.tt("dve", v3(xw), x3, self.bc_hp(self.wdec_t, off, TK, NCH32), ALU.mult, [rx, R("wdec", ci)], [R("xw", pb)])
        self.tt("dve", v3(xD), x3, self.bc_hp(self.Dbc, 8 * g, TK, 32), ALU.mult, [rx, R("Dbc")], [R("xD", pb)])
        self.cp("act", Btok[0:TK, :], bBv[0:TK, 0:128], [rB], [R("Btok", pb)])
        self.pinned -= {bx_i, bB_i}
        segs = []
        for half in range(2):
            bs, rs, bs_i = self.bankp()
            segs.append((bs, rs, bs_i))
            for j in range(4):
                hh = half * 4 + j
                h = 8 * g + hh
                o = bs[0:TK, j * 128:j * 128 + TK]
                self.mm(o, self.da_t[0:TK, ci, h:h + 1].broadcast_to([TK, TK]), BTx[0:TK, 0:TK], True, False, [R("da_t", ci), R("BT"), R("BT64")], [rs])
                self.mm(o, self.ident_b[0:TK, 0:TK], NEGx[0:TK, 0:TK], False, True, [R("ident_b"), R("NEG"), R("NEG64")], [rs])
        bc, rc, bc_i = self.bankp()
        self.mm(bc[0:TK, 0:TK], B["Bc"][:, lc:lc + TK], B["Cc"][:, lc:lc + TK], True, True, B["rd"]("Bc") + B["rd"]("Cc"), [rc])
        yield
        self.cp("act", self.cbT[0:TK, 0:TK], bc[0:TK, 0:TK], [rc], [R("cbT")])
        for half in range(2):
            bs, rs, bs_i = segs[half]
            for j in range(4):
                hh = half * 4 + j
                h = 8 * g + hh
                self.act(self.LT[0:TK, hh, 0:TK], bs[0:TK, j * 128:j * 128 + TK], AF.Exp, [rs, R("nacs", ci)], [R("LT", hh)],
                         bias=self.nacs_t[0:TK, ci, h:h + 1])
        self.pinned -= {bc_i, segs[0][2], segs[1][2]}
        yield "gap"
        self.tt("dve", self.MT[0:TK, :, 0:TK], self.LT[0:TK, :, 0:TK], bass.AP(self.cbT, 0, [[128, TK], [0, 8], [1, TK]]), ALU.mult,
                [R("LT", h_) for h_ in range(8)] + [R("cbT")], [R("MT")])
        byd, ryd, byd_i = self.bankp()
        for hh in range(8):
            self.mm(byd[0:TK, hh * 64:(hh + 1) * 64], self.MT[0:TK, hh, 0:TK], self.xdt[0:TK, hh * 64:(hh + 1) * 64], True, True,
                    [R("MT"), R("xdt")], [ryd])
        fr["byd"] = (byd, ryd, byd_i)
        yield

    def ssd_back(self, sti, g, c0, ch, fr, B):
        R, S, d = self.R, self.S, self.d
        kind, cc0, TK, ci = ch
        lc = cc0 - c0
        NCH32 = 6 * 32
        off = ci * 32 + 8 * g
        byd, ryd, byd_i = fr["byd"]
        pb = fr["pb"]
        xw, xD, Btok = self.xw2[pb], self.xD2[pb], self.Btok2[pb]
        rxw, rxD, rBtok = R("xw", pb), R("xD", pb), R("Btok", pb)
        v3 = lambda t: t[0:TK, :].rearrange("p (h j) -> p h j", h=8)
        hv = self.hT[:, g * 512:(g + 1) * 512]
        byo = ryo = byo_i = None
        if kind == "m":
            self.tt("dve", self.ynb[0:TK, :], byd[0:TK, :], xD[0:TK, :], ALU.add, [ryd, rxD], [R("ynb")])
        else:
            self.tt("dve", self.yt[0:TK, :], byd[0:TK, :], xD[0:TK, :], ALU.add, [ryd, rxD], [R("yt")])
        self.pinned.discard(byd_i)
        if kind == "p":
            byo, ryo, byo_i = self.bankp()
            self.mm(byo[0:TK, :], B["Cc"][:, lc:lc + TK], self.hTb[:], True, True, B["rd"]("Cc") + [R("hTb")], [ryo])
        if kind != "s":
            bS, rS, bS_i = self.bankp()
            self.mm(bS[:, :], Btok[0:TK, :], xw[0:TK, :], True, True, [rBtok, rxw], [rS])
        yield
        if kind == "m":
            self.cp("act", hv, bS[:, :], [rS], [R("hT")])
            self.cp("act", self.hTb[:], hv, [R("hT")], [R("hTb")])
            self.pinned.discard(bS_i)
        elif kind == "p":
            self.tt("dve", hv.rearrange("p (h j) -> p h j", h=8), hv.rearrange("p (h j) -> p h j", h=8),
                    self.bc_hp(self.etot_t, off, 128, NCH32), ALU.mult, [R("hT"), R("etot", ci)], [R("hT")])
            self.tt("dve", hv, hv, bS[:, :], ALU.add, [R("hT"), rS], [R("hT")])
            self.cp("act", self.hTb[:], hv, [R("hT")], [R("hTb")])
            self.pinned.discard(bS_i)
        else:
            Cmv = self.MT[:, :, :].rearrange("p a b -> p (a b)").rearrange("p (b t) -> p b t", b=16)
            Btmv = self.hb[:, :, :].rearrange("p c n -> p (c n)")[0:64, 0:2048].rearrange("p (b n) -> p b n", b=16)
            rBtm = [R("hb", c_) for c_ in range(4)]
            self.tt("dve", Cmv, B["Cc"][:, 0:64].unsqueeze(1).broadcast_to([128, 16, 64]), self.maskall[:], ALU.mult,
                    B["rd"]("Cc") + [R("maskall")], [R("MT")])
            self.tt("dve", Btmv, bass.AP(Btok, 0, [[128, 64], [0, 16], [1, 128]]),
                    bass.AP(self.sel, 0, [[16, 64], [1, 16], [0, 128]]), ALU.mult, [rBtok, R("sel")], rBtm)
            byo, ryo, byo_i = self.bankp()
            h0v = lambda i: self.stg[i // 2][:, (i % 2) * 512:(i % 2 + 1) * 512].rearrange("p (i n) -> p i n", i=4)
            rpar = lambda i: R("stg", i // 2)
            src = lambda b: d["sst"][b, 8 * g:8 * g + 8].rearrange("(i hh) p n -> (hh p) i n", hh=2)

            def ld(b):
                S.dma("pool", h0v(b % 4), src(b), reads=[rpar(b % 4)], writes=[R("h0", b % 4)])

            def tpc(b):
                h0, rh0 = h0v(b % 4), R("h0", b % 4)
                bt, rt, bt_i = self.bankp()
                for i in range(4):
                    self.tp(bt[:, i * 128:(i + 1) * 128], h0[:, i, :], self.ident_f[:, :], [rh0, rpar(b % 4), R("ident_f")], [rt])
                hTbb = self.hTb2[b % 2]
                rhTbb = R("hTb") if b % 2 == 0 else R("hTb2", 1)
                self.cp("act", hTbb[:], bt[:], [rt], [rhTbb])
                self.pinned.discard(bt_i)
            ld(0)
            ld(1)
            ld(2)
            ld(3)
            tpc(0)
            for b in range(16):
                if b + 1 < 16:
                    tpc(b + 1)
                h0, rh0 = h0v(b % 4), R("h0", b % 4)
                hTbb = self.hTb2[b % 2]
                rhTbb = R("hTb") if b % 2 == 0 else R("hTb2", 1)
                self.mm(byo[0:64, :], Cmv[:, b, :], hTbb[:], b == 0, b == 15, [R("MT"), rhTbb], [ryo])
                bsn, rsn, bsn_i = self.bankp()
                for i in range(4):
                    self.mm(bsn[:, i * 128:(i + 1) * 128], xw[0:64, i * 128:(i + 1) * 128], Btmv[:, b, :], True, True,
                            [rxw] + rBtm, [rsn])
                self.tt("dve", h0, h0, bass.AP(self.etn, b * 16 + 4 * g, [[256, 128], [1, 4], [0, 128]]), ALU.mult,
                        [rh0, R("etn"), rpar(b % 4)], [rh0])
                self.tt("dve", h0, h0, bsn[:].rearrange("p (i n) -> p i n", i=4), ALU.add, [rh0, rsn, rpar(b % 4)], [rh0])
                self.pinned.discard(bsn_i)
                self.out_toks.append(S.dma("sp", d["sst_o"][b, 8 * g:8 * g + 8].rearrange("(i hh) p n -> (hh p) i n", hh=2), h0,
                                           reads=[rh0, rpar(b % 4)]))
                if b + 4 < 16:
                    ld(b + 4)
                yield "gap"
        if kind != "m":
            self.tt("dve", v3(self.t1), byo[0:TK, :].rearrange("p (h j) -> p h j", h=8), self.bc_hp(self.eacs_t, off, TK, NCH32), ALU.mult,
                    [ryo, R("eacs", ci)], [R("t1")])
            self.tt("dve", self.ynb[0:TK, :], self.yt[0:TK, :], self.t1[0:TK, :], ALU.add, [R("yt"), R("t1")], [R("ynb")])
            self.pinned.discard(byo_i)
        yield
        bv, rv, bv_i = self.bankp()
        bvv = self.bview(bv)
        for q in range(4):
            self.tp(bvv[:, q * 128:q * 128 + TK], self.ynb[0:TK, q * 128:(q + 1) * 128], self.ident_b[0:TK, 0:TK], [R("ynb"), R("ident_b")], [rv])
        self.tt("dve", self.Y[:, 4 * g:4 * g + 4, cc0:cc0 + TK], bvv[:, 0:512].rearrange("p (q t) -> p q t", q=4)[:, :, 0:TK],
                B["gz"][:, :, lc:lc + TK], ALU.mult, [rv] + [r_ for q_ in range(4) for r_ in B["rd"]("gz", q_)], [R("Y", 4 * g + q_) for q_ in range(4)])
        self.pinned.discard(bv_i)
        yield

    def ssd_tile(self, sti, g, ti, c0, chunks, B):
        R = self.R
        if chunks[0][0] == "p" and ti == 0:
            self.cp("act", self.hTb[:], self.hT[:, g * 512:(g + 1) * 512], [R("hT")], [R("hTb")])
        frs = [dict() for _ in chunks]
        for t_ in self.ssd_front(sti, g, c0, chunks[0], 0, frs[0], B):
            yield t_
        for i, ch in enumerate(chunks):
            bgen = self.ssd_back(sti, g, c0, ch, frs[i], B)
            if i + 1 < len(chunks):
                fgen = self.ssd_front(sti, g, c0, chunks[i + 1], i + 1, frs[i + 1], B)
                if ch[0] == "p" and chunks[i + 1][0] == "p":
                    alive = [fgen, bgen]
                    while alive:
                        for gen in list(alive):
                            try:
                                yield next(gen)
                            except StopIteration:
                                alive.remove(gen)
                else:
                    for t_ in fgen:
                        yield t_
                    for t_ in bgen:
                        yield t_
            else:
                for t_ in bgen:
                    yield t_

    def state_out(self, g):
        R, S, d = self.R, self.S, self.d
        bk, br = self.bank()
        for i in range(4):
            self.tp(bk[:, i * 128:(i + 1) * 128], self.hT[:, g * 512 + i * 128: g * 512 + (i + 1) * 128], self.ident_f[:, :], [R("hT"), R("ident_f")], [br])
        hn = self.stg[1][:, 0:512]
        self.cp("act", hn, bk[:], [br], [R("stg", 1)])
        self.out_toks.append(S.dma("sp", d["pss"][8 * g:8 * g + 8].rearrange("(i hh) p n -> (hh p) i n", hh=2),
                                   hn.rearrange("p (i n) -> p i n", i=4), reads=[R("stg", 1)]))

    def final_out(self, sti, tiles):
        R, S, d = self.R, self.S, self.d
        Yf = self.Y[:, :, :].rearrange("p c n -> p (c n)").bitcast(F32).rearrange("p (c n) -> p c n", c=8)
        rY = lambda c: [R("Y", 2 * c), R("Y", 2 * c + 1)]
        for (c0, n) in tiles:
            self.rmsnorm(3, c0, n, lambda c: (Yf[:, c, c0:c0 + n], rY(c)))
        blocks = []
        if sti == 0:
            blocks.append((0, 64, d["ys"][:, :]))
            for i in range(4):
                blocks.append((80 + 128 * i, 128, d["yp"][128 * i:128 * (i + 1), :]))
        else:
            for i in range(4):
                blocks.append((128 * i, 128, d["yp"][512 * sti + 128 * i: 512 * sti + 128 * (i + 1), :]))
        for (cb, nb, dst) in blocks:
            si = self.stg_i % 2
            self.stg_i += 1
            st = self.stg[si]
            for half in range(2):
                bk, br = self.bank()
                for j in range(4):
                    c = half * 4 + j
                    self.tp(bk[0:nb, j * 128:(j + 1) * 128], Yf[:, c, cb:cb + nb], self.ident_f[:, :], rY(c) + [R("ident_f")], [br])
                self.cp("act" if half == 0 else "dve", st[0:nb, half * 512:(half + 1) * 512], bk[0:nb, :], [br], [R("stg", si)])
            self.out_toks.append(S.dma("sp", dst, st[0:nb, :], reads=[R("stg", si)]))

    def conv_state_out(self):
        R, S, d = self.R, self.S, self.d

        def emit(src_fn, nchunks, rows, dst_fn, rsrc):
            for part in range(nchunks // 8):
                si = self.stg_i % 2
                self.stg_i += 1
                st = self.stg[si]
                for half in range(2):
                    bk, br = self.bank()
                    for j in range(4):
                        c = part * 8 + half * 4 + j
                        self.tp(bk[0:rows, j * 128:(j + 1) * 128], src_fn(c), self.ident_f[:, :], [rsrc(c), R("ident_f")], [br])
                    self.cp("act" if half == 0 else "dve", st[0:rows, half * 512:(half + 1) * 512], bk[0:rows, :], [br], [R("stg", si)])
                self.out_toks.append(S.dma("sp", dst_fn(part), st[0:rows, :], reads=[R("stg", si)]))
        emit(lambda c: self.haloA[:, c, :], 8, 2, lambda part: d["pca"][:, :], lambda c: R("haloA", c))
        emit(lambda c: self.haloB[:, c, :], 24, 3, lambda part: d["psc"][:, part * 1024:(part + 1) * 1024], lambda c: R("haloB", c))
        emit(lambda c: self.outA[:, c, :], 8, 32, lambda part: d["sca_o"][:, :], lambda c: R("stA", c))
        emit(lambda c: self.outB[:, c, :], 24, 48, lambda part: d["ssc_o"][:, part * 1024:(part + 1) * 1024], lambda c: R("stB", c))

    def run(self):
        d = self.d
        self.out_toks = []
        items = []
        for sti in range(4):
            tiles = [(0, 80), (80, 512)] if sti == 0 else [(0, 512)]
            items.append((0, None, lambda pids, sti=sti, tiles=tiles: (self.load_x(sti), self.norm_to_xn(0, tiles))))
            items += self.ffn_items(d["w1gu"], d["w1d"], tiles)
            items.append((0, None, lambda pids, tiles=tiles: self.norm_to_xn(1, tiles)))
            items += self.a_items(sti, d["w_in"], tiles)
            items.append(self.atail_item(d["w_in"], d["w_a_out"], d["w_o"], tiles))
            items += self.b_items(sti, d["w_in"], tiles)
            items.append(self.btail_item(sti, d["w_in"], d["w_b_out"], d["w_o"], tiles))
            items.append((0, None, lambda pids, tiles=tiles: self.norm_to_xn(2, tiles)))
            items += self.ffn_items(d["w2gu"], d["w2d"], tiles)
            items.append((0, None, lambda pids, sti=sti, tiles=tiles: self.final_out(sti, tiles)))
        items.append((0, None, lambda pids: self.conv_state_out()))
        N = len(items)
        pids = [None] * N
        loaded = [False] * N

        def do_load(j):
            k, load, _ = items[j]
            pids[j] = self.take_pages(k)
            if load is not None:
                load(pids[j])
            loaded[j] = True

        do_load(0)
        do_load(1)
        do_load(2)
        self.setup()
        for i in range(N):
            if not loaded[i]:
                do_load(i)
            j = i + 1
            if j < N and not loaded[j]:
                k = items[j][0]
                cand = [(self.page_i + x) % 4 for x in range(k)]
                if not (set(cand) & set(pids[i])):
                    do_load(j)
            items[i][2](pids[i])
        self.S.wait_all("sp", self.out_toks)


IN_SPECS = [
    ("xp", [2048, 1024]), ("xs", [64, 1024]), ("meta", [16, 1024]), ("sca", [32, 1024]), ("ssc", [48, 3072]),
    ("sst", [16, 32, 64, 128]), ("cst1", [96, 128]), ("cst2", [96, 128]), ("dtb", [1, 32]), ("alog", [1, 32]), ("dsk", [1, 32]),
    ("w1gu", [1, 1024, 5632]), ("w1d", [1, 2816, 1024]), ("w_in", [1, 1024, 10272]), ("w_a_out", [1, 1024, 1024]),
    ("w_b_out", [1, 2048, 1024]), ("w_o", [1, 1024, 1024]), ("w2gu", [1, 1024, 5632]), ("w2d", [1, 2816, 1024]),
]
OUT_SPECS = [
    ("yp", [2048, 1024]), ("ys", [64, 1024]), ("pca", [2, 1024]), ("psc", [3, 3072]), ("pss", [32, 64, 128]),
    ("sca_o", [32, 1024]), ("ssc_o", [48, 3072]), ("sst_o", [16, 32, 64, 128]),
]


def build_nc():
    nc = bass.Bass("TRN2", target_bir_lowering=False)
    d = {}
    for name, shape in IN_SPECS:
        d[name] = nc.dram_tensor(name, shape, F32, kind="ExternalInput").ap()
    for name, shape in OUT_SPECS:
        d[name] = nc.dram_tensor(name, shape, F32, kind="ExternalOutput").ap()
    with ExitStack() as st:
        S = Sched(nc, st)
        K = Kern(S, nc, d)
        K.alloc()
        K.ones_f = S.sb([128, 128], F32, name="ones_f")
        S.op("pool", lambda e: e.memset(K.ones_f[:], 1.0), [], [K.R("ones_f")])
        K.run()
        S.emit()
    return nc


_NC_CACHE = {}


def kernel(x_prompt, x_sample, state_conv_a, state_ssm_conv, state_ssm, meta_tokens,
           norm_ffn1, ffn1_w_gu, ffn1_w_down, norm_mix, w_in, conv_a_w, w_a_out,
           ssm_conv_w, ssm_conv_b, dt_bias, a_log, d_skip, ssm_norm_w, w_b_out, w_o,
           norm_ffn2, ffn2_w_gu, ffn2_w_down, norm_final):
    f = lambda a: np.ascontiguousarray(np.asarray(a, dtype=np.float32))
    n = 8
    if "nc" not in _NC_CACHE:
        _NC_CACHE["nc"] = build_nc()
    nc = _NC_CACHE["nc"]
    cst1 = np.concatenate([f(norm_ffn1).reshape(8, 128), f(norm_mix).reshape(8, 128), f(norm_ffn2).reshape(8, 128),
                           f(norm_final).reshape(8, 128), f(conv_a_w).reshape(24, 128), f(ssm_conv_b).reshape(24, 128),
                           f(ssm_norm_w).reshape(16, 128)], axis=0)
    cst2 = f(ssm_conv_w).reshape(96, 128)
    shared = {
        "meta": f(meta_tokens), "cst1": f(cst1), "cst2": cst2, "dtb": f(dt_bias).reshape(1, 32), "alog": f(a_log).reshape(1, 32),
        "dsk": f(d_skip).reshape(1, 32), "w1gu": f(ffn1_w_gu), "w1d": f(ffn1_w_down), "w_in": f(w_in), "w_a_out": f(w_a_out),
        "w_b_out": f(w_b_out), "w_o": f(w_o), "w2gu": f(ffn2_w_gu), "w2d": f(ffn2_w_down),
    }
    xp, xs = f(x_prompt), f(x_sample)
    sca, ssc, sst = f(state_conv_a), f(state_ssm_conv), f(state_ssm)
    in_maps = []
    for c in range(n):
        m = dict(shared)
        m["xp"] = xp[c]
        m["xs"] = xs[16 * c:16 * (c + 1)].reshape(64, 1024)
        m["sca"] = sca[0, 16 * c:16 * (c + 1)].reshape(32, 1024)
        m["ssc"] = ssc[0, 16 * c:16 * (c + 1)].reshape(48, 3072)
        m["sst"] = sst[0, 16 * c:16 * (c + 1)]
        in_maps.append(m)
    res = run_bass_kernel_spmd(nc, in_maps, core_ids=list(range(n)))
    rs = res.results
    y_prompt = np.stack([rs[c]["yp"] for c in range(n)], axis=0)
    y_sample = np.concatenate([rs[c]["ys"].reshape(16, 4, 1024) for c in range(n)], axis=0)
    p_ca = np.stack([rs[c]["pca"] for c in range(n)], axis=0)[None]
    p_sc = np.stack([rs[c]["psc"] for c in range(n)], axis=0)[None]
    p_ss = np.stack([rs[c]["pss"] for c in range(n)], axis=0)[None]
    s_ca = np.concatenate([rs[c]["sca_o"].reshape(16, 2, 1024) for c in range(n)], axis=0)[None]
    s_sc = np.concatenate([rs[c]["ssc_o"].reshape(16, 3, 3072) for c in range(n)], axis=0)[None]
    s_ss = np.concatenate([rs[c]["sst_o"] for c in range(n)], axis=0)[None]
    return tuple(np.ascontiguousarray(a, dtype=np.float32) for a in (y_prompt, y_sample, p_ca, p_sc, p_ss, s_ca, s_sc, s_ss))
```

```python
import numpy as np
from contextlib import ExitStack
import concourse.bass as bass
import concourse.mybir as mybir
from concourse.bass_utils import run_bass_kernel_spmd

F32 = mybir.dt.float32
BF16 = mybir.dt.bfloat16
AF = mybir.ActivationFunctionType
ALU = mybir.AluOpType
AX = mybir.AxisListType

ENGS = ("pe", "act", "dve", "pool", "sp")
SAME_SYNC = {"pe": False, "act": True, "dve": True, "pool": True, "sp": False}
N_DMA_SEMS = 24


class Res:
    __slots__ = ("name", "w", "r", "excl")

    def __init__(self, name="", excl=False):
        self.name = name
        self.w = None
        self.r = {}
        self.excl = excl


class Sched:
    def __init__(self, nc, st):
        self.nc = nc
        self.st = st
        self.prog = {e: [] for e in ENGS}
        self.cnt = {e: 0 for e in ENGS}
        self.seen = {e: {} for e in ENGS}
        self.dma_val = {("d", i): 0 for i in range(N_DMA_SEMS)}
        self.dma_i = 0
        self.nsb = 0

    def sb(self, shape, dtype, name=None):
        self.nsb += 1
        return self.st.enter_context(self.nc.sbuf_tensor("s_" + (name or f"sb{self.nsb}"), list(shape), dtype))

    def ps(self, shape, dtype=F32, name=None):
        self.nsb += 1
        return self.st.enter_context(self.nc.psum_tensor("p_" + (name or f"ps{self.nsb}"), list(shape), dtype))

    def _need(self, eng, deps):
        need = {}
        seen = self.seen[eng]
        for key, val in deps:
            if key == eng and not SAME_SYNC[eng]:
                continue
            if seen.get(key, 0) >= val:
                continue
            if need.get(key, 0) < val:
                need[key] = val
        for key, val in need.items():
            seen[key] = val
        return list(need.items())

    @staticmethod
    def _deps(reads, writes, eng=None):
        deps = []
        for r in reads:
            if r.w is not None:
                deps.append(r.w)
            if r.excl:
                deps.extend((k, v) for k, v in r.r.items() if k != eng)
        for w in writes:
            if w.w is not None:
                deps.append(w.w)
            deps.extend(w.r.items())
        return deps

    @staticmethod
    def _mark(tok, reads, writes):
        for r in reads:
            if r.r.get(tok[0], 0) < tok[1]:
                r.r[tok[0]] = tok[1]
        for w in writes:
            w.w = tok
            w.r = {}

    def op(self, eng, fn, reads=(), writes=()):
        waits = self._need(eng, self._deps(reads, writes, eng))
        self.cnt[eng] += 1
        tok = (eng, self.cnt[eng])
        self.prog[eng].append((waits, fn, tok))
        self._mark(tok, reads, writes)
        return tok

    def dma(self, q, out, in_, reads=(), writes=(), **kw):
        key = ("d", self.dma_i % N_DMA_SEMS)
        self.dma_i += 1
        prev = self.dma_val[key]
        deps = self._deps(reads, writes, q)
        if prev:
            deps.append((key, prev))
        waits = self._need(q, deps)
        self.dma_val[key] = prev + 16
        tok = (key, prev + 16)
        self.prog[q].append((waits, lambda e: e.dma_start(out=out, in_=in_, **kw), tok))
        self._mark(tok, reads, writes)
        return tok

    def wait_all(self, eng, toks):
        waits = self._need(eng, toks)
        self.prog[eng].append((waits, None, None))

    def emit(self):
        nc = self.nc
        marked = {e: set() for e in ENGS}
        for e in ENGS:
            for waits, fn, tok in self.prog[e]:
                for key, val in waits:
                    if key in marked:
                        marked[key].add(val)
        rank = {}
        for e in ENGS:
            for i, v in enumerate(sorted(marked[e])):
                rank[(e, v)] = i + 1
        sems = {}
        for e in ENGS:
            sems[e] = self.st.enter_context(nc.semaphore(f"sem_{e}"))
        for i in range(N_DMA_SEMS):
            sems[("d", i)] = self.st.enter_context(nc.semaphore(f"sem_d{i}"))
        handles = {"pe": "tensor", "act": "scalar", "dve": "vector", "pool": "gpsimd", "sp": "sync"}

        def run(e, eng_handle):
            for waits, fn, tok in self.prog[e]:
                for key, val in waits:
                    v = rank[(key, val)] if key in marked else val
                    eng_handle.wait_ge(sems[key], v)
                if fn is None:
                    continue
                inst = fn(eng_handle)
                if tok[0] in marked:
                    if tok[1] in marked[tok[0]]:
                        inst.then_inc(sems[tok[0]], 1)
                else:
                    inst.then_inc(sems[tok[0]], 16)

        with nc.Block() as block:
            for e in ENGS:
                getattr(block, handles[e])(lambda h, e=e: run(e, h))


D = 1024
DFF = 2816
NFF = 22
DSSM = 2048
DXBC = 3072
DIN = 10272
OFF_AB, OFF_AC, OFF_AH, OFF_Z, OFF_X, OFF_B, OFF_C, OFF_DT, OFF_GA, OFF_GB = (
    0, 1024, 2048, 3072, 5120, 7168, 7680, 8192, 8224, 9248)
EPS = 1e-6
TSMAX = 592
NEGBIG = -30000.0
CONV_ENG = "dve"
USE_HILO = False
USE_YCOPY = False
USE_WSCALE = False


class Kern:
    def __init__(self, S, nc, d):
        self.S = S
        self.nc = nc
        self.d = d
        self.res = {}
        self.bank_i = 0
        self.pinned = set()
        self.page_i = 0

    def R(self, *key):
        r = self.res.get(key)
        if r is None:
            r = self.res[key] = Res(str(key), excl=(key[0] == "bank"))
        return r

    def alloc(self):
        S = self.S
        self.banks = [S.ps([128, 512], F32, name=f"bank{i}") for i in range(8)]
        self.pages = [S.sb([128, 8192], BF16, name=f"page{i}") for i in range(4)]
        self.r = S.sb([128, 8, TSMAX], F32, name="r")
        self.xn = S.sb([128, 8, TSMAX], BF16, name="xn")
        self.Y = S.sb([128, 16, TSMAX], BF16, name="Y")
        self.ident_f = S.sb([128, 128], F32, name="ident_f")
        self.ident_b = S.sb([128, 128], BF16, name="ident_b")
        self.ones_b = S.sb([128, 128], BF16, name="ones_b")
        self.BT = S.sb([128, 128], F32, name="BT")
        self.NEG = S.sb([128, 128], BF16, name="NEGm")
        self.BT64 = S.sb([64, 64], F32, name="BT64")
        self.BO64 = S.sb([64, 64], F32, name="BO64")
        self.NEG64 = S.sb([64, 64], BF16, name="NEG64")
        self.ones64 = S.sb([64, 128], F32, name="ones64")
        self.sel = S.sb([64, 16], F32, name="sel")
        self.sel3 = S.sb([64, 16], F32, name="sel3")
        self.maskall = S.sb([128, 16, 64], BF16, name="maskall")
        self.hhm = S.sb([128, 1], F32, name="hhm")
        self.cvec = S.sb([128, 96], F32, name="cvec")
        self.cw = S.sb([128, 96], F32, name="cw")
        self.cst1 = S.sb([96, 128], F32, name="cst1")
        self.cst2 = S.sb([96, 128], F32, name="cst2")
        self.dtb = S.sb([32, 1], F32, name="dtb")
        self.alog = S.sb([32, 1], F32, name="alog")
        self.aneg = S.sb([32, 1], F32, name="aneg")
        self.Dbc = S.sb([128, 32], F32, name="Dbc")
        self.stg = [S.sb([128, 1024], F32, name=f"stg{i}") for i in range(2)]
        self.stg_i = 0
        self.sg = [S.sb([128, 512], F32, name=f"sg{i}") for i in range(2)]
        self.hb = S.sb([128, 4, 512], BF16, name="hb")
        self.ma = S.sb([128, 8, 512], BF16, name="ma")
        self.haloA = S.sb([128, 8, 2], F32, name="haloA")
        self.stA = S.sb([128, 8, 32], F32, name="stA")
        self.outA = self.stA
        self.pre = S.sb([128, 2, 520], F32, name="pre")
        self.xc = S.sb([128, 4, 512], F32, name="xc")
        self.Bc = S.sb([128, 512], BF16, name="Bc")
        self.Cc = S.sb([128, 512], BF16, name="Cc")
        self.haloB = S.sb([128, 24, 3], F32, name="haloB")
        self.stB = S.sb([128, 24, 48], F32, name="stB")
        self.outB = self.stB
        self.dtfm = S.sb([32, TSMAX], F32, name="dtfm")
        self.dafm = S.sb([32, TSMAX], F32, name="dafm")
        self.ldtfm = S.sb([32, TSMAX], F32, name="ldtfm")
        NCH = 6
        self.dt_t = S.sb([128, NCH, 32], F32, name="dt_t")
        self.da_t = S.sb([128, NCH, 32], F32, name="da_t")
        self.nacs_t = S.sb([128, NCH, 32], F32, name="nacs_t")
        self.wdec_t = S.sb([128, NCH, 32], F32, name="wdec_t")
        self.eacs_t = S.sb([128, NCH, 32], F32, name="eacs_t")
        self.etot_t = S.sb([128, NCH, 32], F32, name="etot_t")
        self.tmp32 = S.sb([128, 32], F32, name="tmp32")
        self.gz = S.sb([128, 4, 512], BF16, name="gz")
        self.xdt = S.sb([128, 512], BF16, name="xdt")
        self.xw2 = [S.sb([128, 512], BF16, name=f"xw{i}") for i in range(2)]
        self.xD2 = [S.sb([128, 512], F32, name=f"xD{i}") for i in range(2)]
        self.Btok2 = [S.sb([128, 128], BF16, name=f"Btok{i}") for i in range(2)]
        self.cbT = S.sb([128, 128], BF16, name="cbT")
        self.LT = S.sb([128, 8, 128], BF16, name="LT")
        self.MT = S.sb([128, 8, 128], BF16, name="MT")
        self.t1 = S.sb([128, 512], F32, name="t1")
        self.yt = S.sb([128, 512], F32, name="yt")
        self.sz = S.sb([128, 512], F32, name="sz")
        self.ynb = S.sb([128, 512], BF16, name="ynb")
        self.hT = S.sb([128, 2048], F32, name="hT")
        self.hTb = S.sb([128, 512], BF16, name="hTb")
        self.hTb2 = [self.hTb, S.sb([128, 512], BF16, name="hTb_b")]
        self.etn = S.sb([128, 16, 16], F32, name="etn")

    def bank(self):
        while True:
            i = self.bank_i % 8
            self.bank_i += 1
            if i not in self.pinned:
                break
        self.last_bank = i
        return self.banks[i], self.R("bank", i)

    def mm(self, out, lhsT, rhs, start, stop, reads, writes):
        self.S.op("pe", lambda e: e.matmul(out, lhsT=lhsT, rhs=rhs, start=start, stop=stop), reads, writes)

    def tp(self, out, in_, ident, reads, writes):
        self.S.op("pe", lambda e: e.transpose(out, in_, ident), reads, writes)

    def act(self, out, in_, func, reads, writes, **kw):
        self.S.op("act", lambda e: e.activation(out=out, in_=in_, func=func, **kw), reads, writes)

    def tt(self, eng, out, in0, in1, op, reads, writes):
        self.S.op(eng, lambda e: e.tensor_tensor(out=out, in0=in0, in1=in1, op=op), reads, writes)

    def ts(self, eng, out, in0, s1, s2, op0, op1, reads, writes):
        self.S.op(eng, lambda e: e.tensor_scalar(out=out, in0=in0, scalar1=s1, scalar2=s2, op0=op0, op1=op1), reads, writes)

    def stt(self, eng, out, in0, scalar, in1, op0, op1, reads, writes):
        self.S.op(eng, lambda e: e.scalar_tensor_tensor(out=out, in0=in0, scalar=scalar, in1=in1, op0=op0, op1=op1), reads, writes)

    def cp(self, eng, out, in_, reads, writes):
        if eng == "act":
            self.S.op("act", lambda e: e.copy(out=out, in_=in_), reads, writes)
        else:
            self.S.op(eng, lambda e: e.tensor_copy(out=out, in_=in_), reads, writes)

    def asel(self, out, in_, pattern, cmp, fill, base, cm, res):
        self.S.op("pool", lambda e: e.affine_select(out=out, in_=in_, pattern=pattern, compare_op=cmp,
                                                    fill=fill, base=base, channel_multiplier=cm), [res], [res])

    def setup(self):
        S, d = self.S, self.d
        R = self.R
        ms = lambda t, v, res: S.op("pool", lambda e: e.memset(t, v), [], [res])
        ms(self.ident_f[:], 0.0, R("ident_f"))
        self.asel(self.ident_f[:], self.ident_f[:], [[-1, 128]], ALU.not_equal, 1.0, 0, 1, R("ident_f"))
        self.cp("dve", self.ident_b[:], self.ident_f[:], [R("ident_f")], [R("ident_b")])
        ms(self.ones_b[:], 1.0, R("ones_b"))
        ms(self.ones64[:], 1.0, R("ones64"))
        ms(self.BT[:], 1.0, R("BT"))
        self.asel(self.BT[:], self.BT[:], [[1, 128]], ALU.is_ge, 0.0, 0, -1, R("BT"))
        ms(self.NEG[:], 0.0, R("NEG"))
        self.asel(self.NEG[:], self.NEG[:], [[1, 128]], ALU.is_ge, NEGBIG, 0, -1, R("NEG"))
        v3 = lambda t: t[:].rearrange("p (b t) -> p b t", t=4)
        ms(self.BO64[:], 1.0, R("BO64"))
        self.asel(v3(self.BO64), v3(self.BO64), [[-4, 16], [0, 4]], ALU.is_ge, 0.0, 0, 1, R("BO64"))
        self.asel(v3(self.BO64), v3(self.BO64), [[4, 16], [0, 4]], ALU.is_ge, 0.0, 3, -1, R("BO64"))
        ms(self.BT64[:], 1.0, R("BT64"))
        self.asel(v3(self.BT64), v3(self.BT64), [[-4, 16], [0, 4]], ALU.is_ge, 0.0, 0, 1, R("BT64"))
        self.asel(v3(self.BT64), v3(self.BT64), [[4, 16], [1, 4]], ALU.is_ge, 0.0, 0, -1, R("BT64"))
        ms(self.NEG64[:], 0.0, R("NEG64"))
        self.asel(v3(self.NEG64), v3(self.NEG64), [[-4, 16], [0, 4]], ALU.is_ge, NEGBIG, 0, 1, R("NEG64"))
        self.asel(v3(self.NEG64), v3(self.NEG64), [[4, 16], [1, 4]], ALU.is_ge, NEGBIG, 0, -1, R("NEG64"))
        ms(self.sel[:], 1.0, R("sel"))
        self.asel(self.sel[:], self.sel[:], [[-4, 16]], ALU.is_ge, 0.0, 0, 1, R("sel"))
        self.asel(self.sel[:], self.sel[:], [[4, 16]], ALU.is_ge, 0.0, 3, -1, R("sel"))
        ms(self.sel3[:], 0.0, R("sel3"))
        self.asel(self.sel3[:], self.sel3[:], [[-4, 16]], ALU.not_equal, 1.0, -3, 1, R("sel3"))
        ms(self.maskall[:], 1.0, R("maskall"))
        self.asel(self.maskall[:], self.maskall[:], [[-4, 16], [1, 64]], ALU.is_ge, 0.0, 0, 0, R("maskall"))
        self.asel(self.maskall[:], self.maskall[:], [[4, 16], [-1, 64]], ALU.is_ge, 0.0, 3, 0, R("maskall"))
        ms(self.hhm[:], 1.0, R("hhm"))
        self.asel(self.hhm[:], self.hhm[:], [[0, 1]], ALU.is_ge, 0.0, -64, 1, R("hhm"))
        S.dma("sp", self.cst1[:], d["cst1"], writes=[R("cst1")])
        S.dma("sp", self.cst2[:], d["cst2"], writes=[R("cst2")])
        bk, br = self.bank()
        self.tp(bk[:, 0:96], self.cst1[:], self.ident_f[0:96, 0:96], [R("cst1"), R("ident_f")], [br])
        self.cp("dve", self.cvec[:], bk[:, 0:96], [br], [R("cvec")])
        bk, br = self.bank()
        self.tp(bk[:, 0:96], self.cst2[:], self.ident_f[0:96, 0:96], [R("cst2"), R("ident_f")], [br])
        self.cp("dve", self.cw[:], bk[:, 0:96], [br], [R("cw")])
        S.dma("sp", self.dtb[:], d["dtb"].rearrange("o h -> h o"), writes=[R("dtb")], allow_slow_non_contiguous=True)
        S.dma("sp", self.alog[:], d["alog"].rearrange("o h -> h o"), writes=[R("alog")], allow_slow_non_contiguous=True)
        self.act(self.aneg[:], self.alog[:], AF.Exp, [R("alog")], [R("aneg")])
        self.ts("dve", self.aneg[:], self.aneg[:], -1.0, None, ALU.mult, ALU.bypass, [R("aneg")], [R("aneg")])
        S.dma("sp", self.Dbc[:], d["dsk"].partition_broadcast(128), writes=[R("Dbc")])
        st = self.stg[0]
        S.dma("sp", st[0:32, :], d["sca"], writes=[R("stg", 0)])
        for c in range(8):
            bk, br = self.bank()
            self.tp(bk[:, 0:32], st[0:32, c * 128:(c + 1) * 128], self.ident_f[0:32, 0:32], [R("stg", 0), R("ident_f")], [br])
            self.cp("dve", self.stA[:, c, :], bk[:, 0:32], [br], [R("stA", c)])
        for part in range(3):
            st = self.stg[1]
            S.dma("sp", st[0:48, :], d["ssc"][:, part * 1024:(part + 1) * 1024], writes=[R("stg", 1)])
            for c in range(8):
                bk, br = self.bank()
                self.tp(bk[:, 0:48], st[0:48, c * 128:(c + 1) * 128], self.ident_f[0:48, 0:48], [R("stg", 1), R("ident_f")], [br])
                self.cp("dve", self.stB[:, part * 8 + c, :], bk[:, 0:48], [br], [R("stB", part * 8 + c)])
        S.op("pool", lambda e: e.memset(self.haloA[:], 0.0), [], [R("haloA", c) for c in range(8)])
        S.op("pool", lambda e: e.memset(self.haloB[:], 0.0), [], [R("haloB", c) for c in range(24)])
        ms(self.hT[:], 0.0, R("hT"))

    def nw(self, which, c):
        return self.cvec[:, which * 8 + c: which * 8 + c + 1]

    def caw(self, k, c):
        return self.cvec[:, 32 + k * 8 + c: 32 + k * 8 + c + 1]

    def scb(self, c):
        return self.cvec[:, 56 + c: 56 + c + 1]

    def snw(self, c):
        return self.cvec[:, 80 + c: 80 + c + 1]

    def scw(self, k, c):
        return self.cw[:, k * 24 + c: k * 24 + c + 1]

    def load_x(self, sti):
        S, d, R = self.S, self.d, self.R
        blocks = []
        if sti == 0:
            blocks.append(("sm", 0, 80))
            for i in range(4):
                blocks.append(("p", 80 + 128 * i, 128 * i))
        else:
            for i in range(4):
                blocks.append(("p", 128 * i, 512 * sti + 128 * i))
        for kind, c0, src0 in blocks:
            si = self.stg_i % 2
            self.stg_i += 1
            st = self.stg[si]
            if kind == "sm":
                S.dma("pool", st[0:64, :], d["xs"], writes=[R("stg", si)])
                S.dma("pool", st[64:80, :], d["meta"], writes=[R("stg", si)])
                n = 80
            else:
                S.dma("pool", st[:, :], d["xp"][src0:src0 + 128, :], writes=[R("stg", si)])
                n = 128
            for half in range(2):
                bk, br = self.bank()
                for j in range(4):
                    c = half * 4 + j
                    self.tp(bk[:, j * 128:j * 128 + n], st[0:n, c * 128:(c + 1) * 128], self.ident_f[0:n, 0:n],
                            [R("stg", si), R("ident_f")], [br])
                eng = "act" if half == 0 else "dve"
                self.cp(eng, self.r[:, half * 4:half * 4 + 4, c0:c0 + n],
                        bk[:].rearrange("p (j n) -> p j n", j=4)[:, :, 0:n], [br], [R("r", half * 4 + j_) for j_ in range(4)])

    def rmsnorm(self, which, c0, n, out_fn):
        R = self.R
        bk, br = self.bank()
        for c in range(8):
            self.act(self.ma[:, c, 0:n], self.r[:, c, c0:c0 + n], AF.Square, [R("r", c)], [R("ma", c)])
            self.mm(bk[:, 0:n], self.ones_b[:], self.ma[:, c, 0:n], c == 0, c == 7, [R("ones_b"), R("ma", c)], [br])
        self.act(self.t1[:, 0:n], bk[:, 0:n], AF.Ln, [br], [R("t1")], scale=1.0 / D, bias=EPS)
        self.act(self.t1[:, 0:n], self.t1[:, 0:n], AF.Exp, [R("t1")], [R("t1")], scale=-0.5)
        for c in range(8):
            dst, wres = out_fn(c)
            self.stt("dve", dst, self.r[:, c, c0:c0 + n], self.nw(which, c), self.t1[:, 0:n], ALU.mult, ALU.mult,
                     [R("r", c), R("t1"), R("cvec")], wres if isinstance(wres, list) else [wres])

    def norm_to_xn(self, which, tiles):
        for (c0, n) in tiles:
            self.rmsnorm(which, c0, n, lambda c: (self.xn[:, c, c0:c0 + n], self.R("xn", c)))

    def take_pages(self, k):
        ids = [(self.page_i + j) % 4 for j in range(k)]
        self.page_i += k
        return ids

    def wdma(self, dst, src, pid):
        self.S.dma("pool", dst, src, writes=[self.R("page", pid)])

    def wview(self, w):
        return w[0].rearrange("(k p) n -> p k n", p=128)

    def ffn_items(self, wgu, wd, tiles):
        items = []
        slabs = [(0, 4), (4, 4), (8, 4), (12, 4), (16, 4), (20, 2)]
        for (j0, nj) in slabs:
            def load(pids, j0=j0, nj=nj):
                pg, pd = self.pages[pids[0]], self.pages[pids[1]]
                gv = pg[:, 0:8 * 2 * nj * 128].rearrange("p (k s n) -> p k s n", k=8, s=2)
                src = self.wview(wgu)
                self.wdma(gv[:, :, 0, :], src[:, :, j0 * 128:(j0 + nj) * 128], pids[0])
                self.wdma(gv[:, :, 1, :], src[:, :, DFF + j0 * 128:DFF + (j0 + nj) * 128], pids[0])
                dv = pd[:, 0:nj * 1024].rearrange("p (j n) -> p j n", j=nj)
                self.wdma(dv, wd[0].rearrange("(j p) n -> p j n", p=128)[:, j0:j0 + nj, :], pids[1])

            def compute(pids, j0=j0, nj=nj):
                R = self.R
                pg, pd = self.pages[pids[0]], self.pages[pids[1]]
                gv = pg[:, 0:8 * 2 * nj * 128].rearrange("p (k s n) -> p k s n", k=8, s=2)
                dv = pd[:, 0:nj * 1024].rearrange("p (j n) -> p j n", j=nj)
                pr0, pr1 = R("page", pids[0]), R("page", pids[1])
                for (c0, n) in tiles:
                    for jj in range(nj):
                        bg, rg = self.bank()
                        for k in range(8):
                            self.mm(bg[:, 0:n], gv[:, k, 0, jj * 128:(jj + 1) * 128], self.xn[:, k, c0:c0 + n], k == 0, k == 7, [pr0, R("xn", k)], [rg])
                        bu, ru = self.bank()
                        for k in range(8):
                            self.mm(bu[:, 0:n], gv[:, k, 1, jj * 128:(jj + 1) * 128], self.xn[:, k, c0:c0 + n], k == 0, k == 7, [pr0, R("xn", k)], [ru])
                        sg = self.sg[jj % 2]
                        self.act(sg[:, 0:n], bg[:, 0:n], AF.Silu, [rg], [R("sg", jj % 2)])
                        self.tt("dve", self.hb[:, jj, 0:n], sg[:, 0:n], bu[:, 0:n], ALU.mult, [R("sg", jj % 2), ru], [R("hb", jj)])
                    for c in range(8):
                        bd, rd = self.bank()
                        for jj in range(nj):
                            self.mm(bd[:, 0:n], dv[:, jj, c * 128:(c + 1) * 128], self.hb[:, jj, 0:n], jj == 0, jj == nj - 1, [pr1, R("hb", jj)], [rd])
                        self.stt("dve", self.r[:, c, c0:c0 + n], bd[:, 0:n], 0.5, self.r[:, c, c0:c0 + n], ALU.mult, ALU.add,
                                 [rd, R("r", c)], [R("r", c)])
            items.append((2, load, compute))
        return items

    def conv(self, out3, src3, L, wcols, reads, writes, acc=False):
        W = len(wcols)
        eng = CONV_ENG
        if not acc:
            self.ts(eng, out3, src3[:, :, 0:L], wcols[0], None, ALU.mult, ALU.bypass, reads, writes)
        for k in range(0 if acc else 1, W):
            self.stt(eng, out3, src3[:, :, k:k + L], wcols[k], out3, ALU.mult, ALU.add, reads + writes, writes)

    def a_items(self, sti, w_in, tiles):
        items = []
        for s in range(4):
            def load(pids, s=s):
                pv = self.pages[pids[0]][:, 0:8 * 3 * 256].rearrange("p (k s n) -> p k s n", k=8, s=3)
                src = self.wview(w_in)
                for i, off in enumerate((OFF_AB, OFF_AC, OFF_AH)):
                    self.wdma(pv[:, :, i, :], src[:, :, off + s * 256: off + (s + 1) * 256], pids[0])

            def compute(pids, s=s):
                R = self.R
                pv = self.pages[pids[0]][:, 0:8 * 3 * 256].rearrange("p (k s n) -> p k s n", k=8, s=3)
                pr = R("page", pids[0])
                for ti, (c0, n) in enumerate(tiles):
                    for jj in range(2):
                        c = s * 2 + jj
                        bks = []
                        for i in range(3):
                            bk, br = self.bank()
                            for k in range(8):
                                self.mm(bk[:, 0:n], pv[:, k, i, jj * 128:(jj + 1) * 128], self.xn[:, k, c0:c0 + n], k == 0, k == 7, [pr, R("xn", k)], [br])
                            bks.append((bk, br))
                        (bb, rb), (bc, rc), (bh, rh) = bks
                        ach = self.pre[:, jj, :]
                        ra = R("pre", jj)
                        cva = self.xc[:, jj, :]
                        rcv = R("xc_s0", jj)
                        sg = self.sg[jj]
                        rsg = R("sg", jj)
                        if sti == 0 and ti == 0:
                            a3 = ach[:, 0:96].rearrange("p (b t) -> p b t", t=6)
                            self.cp("dve", a3[:, :, 0:2], self.stA[:, c, :].rearrange("p (b k) -> p b k", k=2), [R("stA", c)], [ra])
                            self.cp("act", sg[:, 0:80], bc[:, 0:80], [rc], [rsg])
                            self.tt("dve", a3[:, :, 2:6], sg[:, 0:64].rearrange("p (b t) -> p b t", t=4),
                                    bh[:, 0:64].rearrange("p (b t) -> p b t", t=4), ALU.mult, [rsg, rh], [ra])
                            self.S.op("dve", lambda e, ach=ach: e.memset(ach[:, 100:102], 0.0), [], [ra])
                            self.tt("dve", ach[:, 102:118], sg[:, 64:80], bh[:, 64:80], ALU.mult, [rsg, rh], [ra])
                            wc = [self.caw(k, c) for k in range(3)]
                            self.conv(cva[:, 0:64].rearrange("p (b t) -> p b t", t=4), a3, 4, wc, [ra, R("cvec")], [rcv])
                            self.conv(cva[:, 64:80].rearrange("p (b t) -> p b t", b=1), ach[:, 100:118].rearrange("p (b t) -> p b t", b=1), 16, wc, [ra, R("cvec")], [rcv])
                            self.tt("dve", self.Y[:, c, c0:c0 + n], cva[:, 0:n], bb[:, 0:n], ALU.mult, [rcv, rb], [R("Y", c)])
                            self.cp("act", self.outA[:, c, :].rearrange("p (b k) -> p b k", k=2), a3[:, :, 4:6], [ra], [R("stA", c)])
                            self.cp("act", self.haloA[:, c, :], ach[:, 116:118], [ra], [R("haloA", c)])
                        else:
                            self.cp("act", ach[:, 0:2], self.haloA[:, c, :], [R("haloA", c)], [ra])
                            self.cp("act", sg[:, 0:n], bc[:, 0:n], [rc], [rsg])
                            self.tt("dve", ach[:, 2:2 + n], sg[:, 0:n], bh[:, 0:n], ALU.mult, [rsg, rh], [ra])
                            wc = [self.caw(k, c) for k in range(3)]
                            self.conv(cva[:, 0:n].rearrange("p (b t) -> p b t", b=1), ach[:, 0:2 + n].rearrange("p (b t) -> p b t", b=1), n, wc, [ra, R("cvec")], [rcv])
                            self.tt("dve", self.Y[:, c, c0:c0 + n], cva[:, 0:n], bb[:, 0:n], ALU.mult, [rcv, rb], [R("Y", c)])
                            self.cp("act", self.haloA[:, c, :], ach[:, n:n + 2], [ra], [R("haloA", c)])
            items.append((1, load, compute))
        return items

    def atail_item(self, w_in, w_a_out, w_o, tiles):
        def load(pids):
            v = lambda i: self.pages[pids[i]][:, :].rearrange("p (k n) -> p k n", k=8)
            self.wdma(v(0), self.wview(w_a_out), pids[0])
            self.wdma(v(1), self.wview(w_in)[:, :, OFF_GA:OFF_GA + 1024], pids[1])
            self.wdma(v(2), self.wview(w_o), pids[2])

        def compute(pids):
            R = self.R
            v = lambda i: self.pages[pids[i]][:, :].rearrange("p (k n) -> p k n", k=8)
            pr = [R("page", p) for p in pids]
            for (c0, n) in tiles:
                for c in range(8):
                    by, ry = self.bank()
                    for k in range(8):
                        self.mm(by[:, 0:n], v(0)[:, k, c * 128:(c + 1) * 128], self.Y[:, k, c0:c0 + n], k == 0, k == 7, [pr[0], R("Y", k)], [ry])
                    bg, rg = self.bank()
                    for k in range(8):
                        self.mm(bg[:, 0:n], v(1)[:, k, c * 128:(c + 1) * 128], self.xn[:, k, c0:c0 + n], k == 0, k == 7, [pr[1], R("xn", k)], [rg])
                    sg = self.sg[c % 2]
                    self.act(sg[:, 0:n], bg[:, 0:n], AF.Sigmoid, [rg], [R("sg", c % 2)])
                    self.tt("dve", self.ma[:, c, 0:n], sg[:, 0:n], by[:, 0:n], ALU.mult, [R("sg", c % 2), ry], [R("ma", c)])
                self.wo_apply(v(2), pr[2], c0, n)
        return (3, load, compute)

    def wo_apply(self, wv, pr, c0, n):
        R = self.R
        for c2 in range(8):
            bm, rm = self.bank()
            for k in range(8):
                self.mm(bm[:, 0:n], wv[:, k, c2 * 128:(c2 + 1) * 128], self.ma[:, k, 0:n], k == 0, k == 7, [pr, R("ma", k)], [rm])
            self.tt("dve", self.r[:, c2, c0:c0 + n], bm[:, 0:n], self.r[:, c2, c0:c0 + n], ALU.add, [rm, R("r", c2)], [R("r", c2)])

    def group_norm(self, g, c0, n):
        R = self.R
        bk, br = self.bank()
        for q in range(4):
            k = 4 * g + q
            sq, rsq = (self.ynb, R("ynb")) if q % 2 == 0 else (self.xdt, R("xdt"))
            self.act(sq[:, 0:n], self.Y[:, k, c0:c0 + n], AF.Square, [R("Y", k)], [rsq])
            self.mm(bk[:, 0:n], self.ones_b[:], sq[:, 0:n], q == 0, q == 3, [R("ones_b"), rsq], [br])
        self.act(self.t1[:, 0:n], bk[:, 0:n], AF.Ln, [br], [R("t1")], scale=1.0 / 512, bias=EPS)
        self.act(self.t1[:, 0:n], self.t1[:, 0:n], AF.Exp, [R("t1")], [R("t1")], scale=-0.5)
        for q in range(4):
            k = 4 * g + q
            self.stt("dve", self.Y[:, k, c0:c0 + n], self.Y[:, k, c0:c0 + n], self.snw(k), self.t1[:, 0:n], ALU.mult, ALU.mult,
                     [R("Y", k), R("t1"), R("cvec")], [R("Y", k)])

    def gn_a(self, g, c0, n):
        R = self.R
        sqv = self.stg[0][:, :].bitcast(BF16)
        bk, br = self.bank()
        for q in range(4):
            k = 4 * g + q
            sq = sqv[:, q * 512:q * 512 + n]
            self.act(sq, self.Y[:, k, c0:c0 + n], AF.Square, [R("Y", k), R("stg", 0)], [R("gnsq", q)])
            self.mm(bk[:, 0:n], self.ones_b[:], sq, q == 0, q == 3, [R("ones_b"), R("gnsq", q), R("stg", 0)], [br])
        rs = self.stg[1][:, 512:512 + n]
        self.act(rs, bk[:, 0:n], AF.Ln, [br, R("stg", 1)], [R("gnr")], scale=1.0 / 512, bias=EPS)
        self.act(rs, rs, AF.Exp, [R("gnr"), R("stg", 1)], [R("gnr")], scale=-0.5)

    def gn_b(self, g, c0, n):
        R = self.R
        rs = self.stg[1][:, 512:512 + n]
        for q in range(4):
            k = 4 * g + q
            self.stt("dve", self.Y[:, k, c0:c0 + n], self.Y[:, k, c0:c0 + n], self.snw(k), rs, ALU.mult, ALU.mult,
                     [R("Y", k), R("gnr"), R("stg", 1), R("cvec")], [R("Y", k)])

    def y_normalize(self, sti, ti, c0, n):
        R = self.R
        chunks = self.chunks_of(sti, ti)
        lo, hi = min(ch[3] for ch in chunks) * 4, (max(ch[3] for ch in chunks) + 1) * 4
        rt = self.rstd_tab[:, lo:hi]
        self.ts("dve", rt, self.ss_tab[:, lo:hi], 1.0 / 512, EPS, ALU.mult, ALU.add, [R("ss_tab")], [R("rstd_tab")])
        self.S.op("dve", lambda e: e.reciprocal(out=rt, in_=rt), [R("rstd_tab")], [R("rstd_tab")])
        self.act(rt, rt, AF.Sqrt, [R("rstd_tab")], [R("rstd_tab")])
        for g in range(4):
            bk, br = self.bank()
            for (kind, cc0, TK, ci) in chunks:
                lc = cc0 - c0
                col = ci * 4 + g
                self.mm(bk[:, lc:lc + TK], self.rstd_tab[0:TK, col:col + 1].broadcast_to([TK, 128]), self.ident_f[0:TK, 0:TK], True, True,
                        [R("rstd_tab"), R("ident_f")], [br])
            for q in range(4):
                k = 4 * g + q
                self.stt("dve", self.Y[:, k, c0:c0 + n], self.Y[:, k, c0:c0 + n], self.snw(k), bk[:, 0:n], ALU.mult, ALU.mult,
                         [R("Y", k), br, R("cvec")], [R("Y", k)])

    def btail_item(self, sti, w_in, w_b_out, w_o, tiles):
        def load(pids):
            v = lambda i: self.pages[pids[i]][:, :].rearrange("p (k n) -> p k n", k=8)
            src = self.wview(w_b_out)
            self.wdma(v(0), src[:, 0:8, :], pids[0])
            self.wdma(v(1), src[:, 8:16, :], pids[1])
            self.wdma(v(2), self.wview(w_in)[:, :, OFF_GB:OFF_GB + 1024], pids[2])
            self.wdma(v(3), self.wview(w_o), pids[3])

        def compute(pids):
            R = self.R
            v = lambda i: self.pages[pids[i]][:, :].rearrange("p (k n) -> p k n", k=8)
            pr = [R("page", p) for p in pids]
            for ti, (c0, n) in enumerate(tiles):
                for c in range(8):
                    by, ry = self.bank()
                    for k in range(16):
                        self.mm(by[:, 0:n], v(k // 8)[:, k % 8, c * 128:(c + 1) * 128], self.Y[:, k, c0:c0 + n], k == 0, k == 15, [pr[k // 8], R("Y", k)], [ry])
                    bg, rg = self.bank()
                    for k in range(8):
                        self.mm(bg[:, 0:n], v(2)[:, k, c * 128:(c + 1) * 128], self.xn[:, k, c0:c0 + n], k == 0, k == 7, [pr[2], R("xn", k)], [rg])
                    sg = self.sg[c % 2]
                    self.act(sg[:, 0:n], bg[:, 0:n], AF.Sigmoid, [rg], [R("sg", c % 2)])
                    self.tt("dve", self.ma[:, c, 0:n], sg[:, 0:n], by[:, 0:n], ALU.mult, [R("sg", c % 2), ry], [R("ma", c)])
                self.wo_apply(v(3), pr[3], c0, n)
        return (4, load, compute)

    def bview(self, bk):
        return bk[:].bitcast(BF16)

    def bc_hp(self, t, off, TK, pstep):
        return bass.AP(t, off, [[pstep, TK], [1, 8], [0, 64]])

    def chunks_of(self, sti, ti):
        if sti == 0:
            if ti == 0:
                return [("s", 0, 64, 0), ("m", 64, 16, 1)]
            return [("p", 80 + 128 * i, 128, 2 + i) for i in range(4)]
        return [("p", 128 * i, 128, i) for i in range(4)]

    def chunk_pre(self, kind, cc0, TK, ci):
        R = self.R
        NCH32 = 6 * 32
        BTx = self.BT64 if kind == "s" else self.BT
        BOx = self.BO64 if kind == "s" else self.ones_f
        bk, br = self.bank()
        self.tp(bk[0:TK, 0:32], self.dtfm[:, cc0:cc0 + TK], self.ident_f[0:32, 0:32], [R("dtfm"), R("ident_f")], [br])
        self.tp(bk[0:TK, 32:64], self.dafm[:, cc0:cc0 + TK], self.ident_f[0:32, 0:32], [R("dafm"), R("ident_f")], [br])
        self.tp(bk[0:TK, 64:96], self.ldtfm[:, cc0:cc0 + TK], self.ident_f[0:32, 0:32], [R("ldtfm"), R("ident_f")], [br])
        self.cp("dve", self.dt_t[0:TK, ci, :], bk[0:TK, 0:32], [br], [R("dt_t", ci)])
        self.cp("dve", self.da_t[0:TK, ci, :], bk[0:TK, 32:64], [br], [R("da_t", ci)])
        b2, r2 = self.bank()
        self.mm(b2[0:TK, 0:32], BTx[0:TK, 0:TK], self.da_t[0:TK, ci, :], True, True, [R("da_t", ci), R("BT"), R("BT64")], [r2])
        self.mm(b2[0:TK, 32:64], BOx[0:TK, 0:TK], self.da_t[0:TK, ci, :], True, True, [R("da_t", ci), R("BO64"), R("ones_f")], [r2])
        self.ts("dve", self.nacs_t[0:TK, ci, :], b2[0:TK, 0:32], -1.0, None, ALU.mult, ALU.bypass, [r2], [R("nacs", ci)])
        self.act(self.eacs_t[0:TK, ci, :], b2[0:TK, 0:32], AF.Exp, [r2], [R("eacs", ci)])
        self.act(self.etot_t[0:TK, ci, :], b2[0:TK, 32:64], AF.Exp, [r2], [R("etot", ci)])
        self.tt("dve", self.tmp32[0:TK, :], b2[0:TK, 32:64], self.nacs_t[0:TK, ci, :], ALU.add, [r2, R("nacs", ci)], [R("tmp32")])
        self.act(self.tmp32[0:TK, :], self.tmp32[0:TK, :], AF.Exp, [R("tmp32")], [R("tmp32")])
        self.tt("dve", self.wdec_t[0:TK, ci, :], self.tmp32[0:TK, :], self.dt_t[0:TK, ci, :], ALU.mult, [R("tmp32"), R("dt_t", ci)], [R("wdec", ci)])
        self.tt("dve", self.nacs_t[0:TK, ci, :], self.nacs_t[0:TK, ci, :], bk[0:TK, 64:96], ALU.add, [R("nacs", ci), br], [R("nacs", ci)])
        if kind == "s":
            in0 = bass.AP(self.etot_t, ci * 32, [[NCH32, 64], [0, 16], [1, 32]])
            in1 = bass.AP(self.sel3, 0, [[16, 64], [1, 16], [0, 32]])
            self.tt("dve", self.yt[0:64, :].rearrange("p (b h) -> p b h", b=16), in0, in1, ALU.mult, [R("etot", ci), R("sel3")], [R("yt")])
            b3, r3 = self.bank()
            self.mm(b3[:, 0:512], self.ones64[0:64, 0:128], self.yt[0:64, :], True, True, [R("yt"), R("ones64")], [r3])
            self.cp("dve", self.t1[:], b3[:], [r3], [R("t1")])
            tv = self.t1[:].rearrange("p (x two) -> p x two", two=2)
            ev = self.sz[:, 0:256]
            self.tt("dve", ev, tv[:, :, 1], tv[:, :, 0], ALU.subtract, [R("t1")], [R("sz")])
            self.stt("dve", self.etn[:].rearrange("p b i -> p (b i)"), ev, self.hhm[:, 0:1], tv[:, :, 0], ALU.mult, ALU.add,
                     [R("sz"), R("t1"), R("hhm")], [R("etn")])

    def bufset(self, bs):
        R = self.R
        if bs == 0:
            par = {"xc": [], "gz": [], "Bc": [], "Cc": []}
            bufs = dict(xc=self.xc, Bc=self.Bc, Cc=self.Cc, gz=self.gz)
        elif bs == "t0":
            pm = [R("ma", c) for c in range(3)]
            par = {"xc": pm, "gz": pm, "Bc": pm, "Cc": pm}
            mb = self.ma[:, :, :].rearrange("p c n -> p (c n)")
            mf = mb.bitcast(F32)
            bufs = dict(xc=mf[:, 0:320].rearrange("p (q n) -> p q n", q=4), Bc=mb[:, 640:720], Cc=mb[:, 720:800],
                        gz=mb[:, 800:1120].rearrange("p (q n) -> p q n", q=4))
        else:
            par = {"xc": [R("ma", c) for c in range(8)], "gz": [R("hb", c) for c in range(4)], "Bc": [R("sg", 0)], "Cc": [R("sg", 0)]}
            xc1 = self.ma[:, :, :].rearrange("p c n -> p (c n)").bitcast(F32).rearrange("p (q n) -> p q n", q=4)
            s0 = self.sg[0][:, :].bitcast(BF16)
            bufs = dict(xc=xc1, Bc=s0[:, 0:512], Cc=s0[:, 512:1024], gz=self.hb)
        B = dict(bufs)
        B["bs"] = bs
        B["rd"] = lambda name, q=0: [R(name + "_s%s" % bs, q)] + par[name]
        B["wr"] = lambda name, q=0: ([R(name + "_s%s" % bs, q)], par[name])
        return B

    def inproj_gen(self, sti, g, ti, c0, n, pids, B):
        R = self.R
        pa = self.pages[pids[0]][:, 0:8 * 800].rearrange("p (k n) -> p k n", k=8)
        pz = self.pages[pids[1]][:, 0:8 * 512].rearrange("p (k n) -> p k n", k=8)
        pr0, pr1 = R("page", pids[0]), R("page", pids[1])
        special = (sti == 0 and ti == 0)

        def cidof(q):
            return (4 * g + q) if q < 4 else (16 + g if q == 4 else 20 + g)

        def scratch(pq):
            return (self.sz, R("sz")) if pq == 0 else (self.sg[1], R("sg", 1))

        def stageA(q):
            cid = cidof(q)
            bk, br = self.bank()
            for k in range(8):
                self.mm(bk[:, 0:n], pa[:, k, q * 128:(q + 1) * 128], self.xn[:, k, c0:c0 + n], k == 0, k == 7, [pr0, R("xn", k)], [br])
            pq = q % 2
            rp = R("pre", pq)
            if special:
                p3 = self.pre[:, pq, 0:112].rearrange("p (b t) -> p b t", t=7)
                self.cp("dve", p3[:, :, 0:3], self.stB[:, cid, :].rearrange("p (b k) -> p b k", k=3), [R("stB", cid)], [rp])
                self.cp("act", p3[:, :, 3:7], bk[:, 0:64].rearrange("p (b t) -> p b t", t=4), [br], [rp])
                self.S.op("dve", lambda e, pq=pq: e.memset(self.pre[:, pq, 120:123], 0.0), [], [rp])
                self.cp("act", self.pre[:, pq, 123:139], bk[:, 64:80], [br], [rp])
            else:
                self.cp("dve", self.pre[:, pq, 0:3], self.haloB[:, cid, :], [R("haloB", cid)], [rp])
                self.cp("act", self.pre[:, pq, 3:3 + n], bk[:, 0:n], [br], [rp])

        def stageB(q):
            cid = cidof(q)
            pq = q % 2
            rp = R("pre", pq)
            cvb, rcv = scratch(pq)
            wc = [self.scw(k, cid) for k in range(4)]
            if q < 4:
                dst, (wdst, pdst) = B["xc"][:, q, 0:n], B["wr"]("xc", q)
            elif q == 4:
                dst, (wdst, pdst) = B["Bc"][:, 0:n], B["wr"]("Bc")
            else:
                dst, (wdst, pdst) = B["Cc"][:, 0:n], B["wr"]("Cc")
            if special:
                p3 = self.pre[:, pq, 0:112].rearrange("p (b t) -> p b t", t=7)
                self.conv(cvb[:, 0:64].rearrange("p (b t) -> p b t", t=4), p3, 4, wc, [rp, R("cw")], [rcv])
                self.conv(cvb[:, 64:80].rearrange("p (b t) -> p b t", b=1),
                          self.pre[:, pq, 120:139].rearrange("p (b t) -> p b t", b=1), 16, wc, [rp, R("cw")], [rcv])
                self.cp("act", self.outB[:, cid, :].rearrange("p (b k) -> p b k", k=3), p3[:, :, 4:7], [rp], [R("stB", cid)])
                self.cp("act", self.haloB[:, cid, :], self.pre[:, pq, 136:139], [rp], [R("haloB", cid)])
            else:
                self.conv(cvb[:, 0:n].rearrange("p (b t) -> p b t", b=1),
                          self.pre[:, pq, 0:3 + n].rearrange("p (b t) -> p b t", b=1), n, wc, [rp, R("cw")], [rcv])
                self.cp("act", self.haloB[:, cid, :], self.pre[:, pq, n:n + 3], [rp], [R("haloB", cid)])
            self.act(dst, cvb[:, 0:n], AF.Silu, [rcv, R("cvec")] + pdst, wdst, bias=self.scb(cid))

        def gate(q):
            bk, br = self.bank()
            for k in range(8):
                self.mm(bk[:, 0:n], pz[:, k, q * 128:(q + 1) * 128], self.xn[:, k, c0:c0 + n], k == 0, k == 7, [pr1, R("xn", k)], [br])
            wg, pg = B["wr"]("gz", q)
            self.act(B["gz"][:, q, 0:n], bk[:, 0:n], AF.Silu, [br] + pg, wg)

        stageA(0)
        yield "A"
        for q in range(6):
            if q + 1 < 6:
                stageA(q + 1)
                yield "A"
            stageB(q)
            if q < 4:
                gate(q)
            yield "B"
        if g == 0:
            bk, br = self.bank()
            for k in range(8):
                self.mm(bk[0:32, 0:n], pa[:, k, 768:800], self.xn[:, k, c0:c0 + n], k == 0, k == 7, [pr0, R("xn", k)], [br])
            self.act(self.dtfm[:, c0:c0 + n], bk[0:32, 0:n], AF.Exp, [br, R("dtb")], [R("dtfm")], bias=self.dtb[:, 0:1])
            self.act(self.dtfm[:, c0:c0 + n], self.dtfm[:, c0:c0 + n], AF.Ln, [R("dtfm")], [R("dtfm")], bias=1.0)
            self.ts("dve", self.dafm[:, c0:c0 + n], self.dtfm[:, c0:c0 + n], self.aneg[:, 0:1], None, ALU.mult, ALU.bypass,
                    [R("dtfm"), R("aneg")], [R("dafm")])
            self.act(self.ldtfm[:, c0:c0 + n], self.dtfm[:, c0:c0 + n], AF.Ln, [R("dtfm")], [R("ldtfm")])
            for ch in self.chunks_of(sti, ti):
                self.chunk_pre(*ch)
                yield

    def b_items(self, sti, w_in, tiles):
        items = []
        self.bp = {}

        def drain(gen):
            for _ in gen:
                pass
        for g in range(4):
            def load(pids, g=g):
                self.bp[g] = pids
                pa = self.pages[pids[0]][:, 0:8 * 800].rearrange("p (k n) -> p k n", k=8)
                pz = self.pages[pids[1]][:, 0:8 * 512].rearrange("p (k n) -> p k n", k=8)
                src = self.wview(w_in)
                self.wdma(pa[:, :, 0:512], src[:, :, OFF_X + g * 512: OFF_X + (g + 1) * 512], pids[0])
                self.wdma(pa[:, :, 512:640], src[:, :, OFF_B + g * 128: OFF_B + (g + 1) * 128], pids[0])
                self.wdma(pa[:, :, 640:768], src[:, :, OFF_C + g * 128: OFF_C + (g + 1) * 128], pids[0])
                if g == 0:
                    self.wdma(pa[:, :, 768:800], src[:, :, OFF_DT: OFF_DT + 32], pids[0])
                self.wdma(pz, src[:, :, OFF_Z + g * 512: OFF_Z + (g + 1) * 512], pids[1])

            def merge(ssd, inp, policy, hook=None):
                ngap = 0
                state = {"inp": inp}

                def adv_in(nb):
                    while state["inp"] is not None and nb > 0:
                        try:
                            if next(state["inp"]) == "B":
                                nb -= 1
                        except StopIteration:
                            state["inp"] = None
                for tag in ssd:
                    if tag == "gap":
                        ngap += 1
                        if ngap == 1 and hook is not None:
                            hook()
                        adv_in(policy(ngap))
                adv_in(100)

            def compute(pids, g=g):
                R = self.R
                steady = lambda k: 2 if k <= 2 else 1
                if sti == 0:
                    (c00, n0), (c01, n1) = tiles
                    BT0, B0 = self.bufset("t0"), self.bufset(0)
                    if g == 0:
                        drain(self.inproj_gen(sti, 0, 0, c00, n0, pids, BT0))
                    merge(self.ssd_tile(sti, g, 0, c00, self.chunks_of(sti, 0), BT0),
                          self.inproj_gen(sti, g, 1, c01, n1, pids, B0), lambda k: 1)
                    self.group_norm(g, c00, n0)
                    nxt = None
                    if g + 1 < 4:
                        assert self.bp.get(g + 1) is not None, "next group's weights not prefetched"
                        nxt = self.inproj_gen(sti, g + 1, 0, c00, n0, self.bp[g + 1], BT0)
                    merge(self.ssd_tile(sti, g, 1, c01, self.chunks_of(sti, 1), B0), nxt, steady)
                    self.group_norm(g, c01, n1)
                    return
                (c0, n) = tiles[0]
                if g == 0:
                    drain(self.inproj_gen(sti, 0, 0, c0, n, pids, self.bufset(0)))
                nxt = None
                if g + 1 < 4:
                    assert self.bp.get(g + 1) is not None, "next group's weights not prefetched"
                    nxt = self.inproj_gen(sti, g + 1, 0, c0, n, self.bp[g + 1], self.bufset((g + 1) % 2))
                if g > 0:
                    self.gn_a(g - 1, c0, n)
                merge(self.ssd_tile(sti, g, 0, c0, self.chunks_of(sti, 0), self.bufset(g % 2)), nxt, steady,
                      hook=(lambda: self.gn_b(g - 1, c0, n)) if g > 0 else None)
                if g == 3:
                    self.group_norm(g, c0, n)
                if sti == 3:
                    self.state_out(g)
            items.append((2, load, compute))
        return items

    def bankp(self):
        bk, br = self.bank()
        i = self.last_bank
        self.pinned.add(i)
        return bk, br, i

    def ssd_front(self, sti, g, c0, ch, seq, fr, B):
        R, S, d = self.R, self.S, self.d
        kind, cc0, TK, ci = ch
        lc = cc0 - c0
        NCH32 = 6 * 32
        off = ci * 32 + 8 * g
        pb = seq % 2
        fr["pb"] = pb
        xw, xD, Btok = self.xw2[pb], self.xD2[pb], self.Btok2[pb]
        BTx = self.BT64 if kind == "s" else self.BT
        NEGx = self.NEG64 if kind == "s" else self.NEG
        bx, rx, bx_i = self.bankp()
        for q in range(4):
            self.tp(bx[0:TK, q * 128:(q + 1) * 128], B["xc"][:, q, lc:lc + TK], self.ident_f[:, :], B["rd"]("xc", q) + [R("ident_f")], [rx])
        bB, rB, bB_i = self.bankp()
        bBv = self.bview(bB)
        self.tp(bBv[0:TK, 0:128], B["Bc"][:, lc:lc + TK], self.ident_b[:, :], B["rd"]("Bc") + [R("ident_b")], [rB])
        yield
        x3 = bx[0:TK, :].rearrange("p (h j) -> p h j", h=8)
        v3 = lambda t: t[0:TK, :].rearrange("p (h j) -> p h j", h=8)
        self.cp("act", self.xdt[0:TK, :], bx[0:TK, :], [rx], [R("xdt")])
        self.tt("dve", v3(xw), x3, self.bc_hp(self.wdec_t, off, TK, NCH32), ALU.mult, [rx, R("wdec", ci)], [R("xw", pb)])
        self.tt("dve", v3(xD), x3, self.bc_hp(self.Dbc, 8 * g, TK, 32), ALU.mult, [rx, R("Dbc")], [R("xD", pb)])
        self.cp("act", Btok[0:TK, :], bBv[0:TK, 0:128], [rB], [R("Btok", pb)])
        self.pinned -= {bx_i, bB_i}
        segs = []
        for half in range(2):
            bs, rs, bs_i = self.bankp()
            segs.append((bs, rs, bs_i))
            for j in range(4):
                hh = half * 4 + j
                h = 8 * g + hh
                o = bs[0:TK, j * 128:j * 128 + TK]
                self.mm(o, self.da_t[0:TK, ci, h:h + 1].broadcast_to([TK, TK]), BTx[0:TK, 0:TK], True, False, [R("da_t", ci), R("BT"), R("BT64")], [rs])
                self.mm(o, self.ident_b[0:TK, 0:TK], NEGx[0:TK, 0:TK], False, True, [R("ident_b"), R("NEG"), R("NEG64")], [rs])
        bc, rc, bc_i = self.bankp()
        self.mm(bc[0:TK, 0:TK], B["Bc"][:, lc:lc + TK], B["Cc"][:, lc:lc + TK], True, True, B["rd"]("Bc") + B["rd"]("Cc"), [rc])
        yield
        self.cp("act", self.cbT[0:TK, 0:TK], bc[0:TK, 0:TK], [rc], [R("cbT")])
        for half in range(2):
            bs, rs, bs_i = segs[half]
            for j in range(4):
                hh = half * 4 + j
                h = 8 * g + hh
                self.act(self.LT[0:TK, hh, 0:TK], bs[0:TK, j * 128:j * 128 + TK], AF.Exp, [rs, R("nacs", ci)], [R("LT", hh)],
                         bias=self.nacs_t[0:TK, ci, h:h + 1])
        self.pinned -= {bc_i, segs[0][2], segs[1][2]}
        yield "gap"
        self.tt("dve", self.MT[0:TK, :, 0:TK], self.LT[0:TK, :, 0:TK], bass.AP(self.cbT, 0, [[128, TK], [0, 8], [1, TK]]), ALU.mult,
                [R("LT", h_) for h_ in range(8)] + [R("cbT")], [R("MT")])
        byd, ryd, byd_i = self.bankp()
        for hh in range(8):
            self.mm(byd[0:TK, hh * 64:(hh + 1) * 64], self.MT[0:TK, hh, 0:TK], self.xdt[0:TK, hh * 64:(hh + 1) * 64], True, True,
                    [R("MT"), R("xdt")], [ryd])
        fr["byd"] = (byd, ryd, byd_i)
        yield

    def ssd_back(self, sti, g, c0, ch, fr, B):
        R, S, d = self.R, self.S, self.d
        kind, cc0, TK, ci = ch
        lc = cc0 - c0
        NCH32 = 6 * 32
        off = ci * 32 + 8 * g
        byd, ryd, byd_i = fr["byd"]
        pb = fr["pb"]
        xw, xD, Btok = self.xw2[pb], self.xD2[pb], self.Btok2[pb]
        rxw, rxD, rBtok = R("xw", pb), R("xD", pb), R("Btok", pb)
        v3 = lambda t: t[0:TK, :].rearrange("p (h j) -> p h j", h=8)
        hv = self.hT[:, g * 512:(g + 1) * 512]
        byo = ryo = byo_i = None
        if kind == "m":
            self.tt("dve", self.ynb[0:TK, :], byd[0:TK, :], xD[0:TK, :], ALU.add, [ryd, rxD], [R("ynb")])
        else:
            self.tt("dve", self.yt[0:TK, :], byd[0:TK, :], xD[0:TK, :], ALU.add, [ryd, rxD], [R("yt")])
        self.pinned.discard(byd_i)
        if kind == "p":
            byo, ryo, byo_i = self.bankp()
            self.mm(byo[0:TK, :], B["Cc"][:, lc:lc + TK], self.hTb[:], True, True, B["rd"]("Cc") + [R("hTb")], [ryo])
        if kind != "s":
            bS, rS, bS_i = self.bankp()
            self.mm(bS[:, :], Btok[0:TK, :], xw[0:TK, :], True, True, [rBtok, rxw], [rS])
        yield
        if kind == "m":
            self.cp("act", hv, bS[:, :], [rS], [R("hT")])
            self.cp("act", self.hTb[:], hv, [R("hT")], [R("hTb")])
            self.pinned.discard(bS_i)
        elif kind == "p":
            self.tt("dve", hv.rearrange("p (h j) -> p h j", h=8), hv.rearrange("p (h j) -> p h j", h=8),
                    self.bc_hp(self.etot_t, off, 128, NCH32), ALU.mult, [R("hT"), R("etot", ci)], [R("hT")])
            self.tt("dve", hv, hv, bS[:, :], ALU.add, [R("hT"), rS], [R("hT")])
            self.cp("act", self.hTb[:], hv, [R("hT")], [R("hTb")])
            self.pinned.discard(bS_i)
        else:
            Cmv = self.MT[:, :, :].rearrange("p a b -> p (a b)").rearrange("p (b t) -> p b t", b=16)
            Btmv = self.hb[:, :, :].rearrange("p c n -> p (c n)")[0:64, 0:2048].rearrange("p (b n) -> p b n", b=16)
            rBtm = [R("hb", c_) for c_ in range(4)]
            self.tt("dve", Cmv, B["Cc"][:, 0:64].unsqueeze(1).broadcast_to([128, 16, 64]), self.maskall[:], ALU.mult,
                    B["rd"]("Cc") + [R("maskall")], [R("MT")])
            self.tt("dve", Btmv, bass.AP(Btok, 0, [[128, 64], [0, 16], [1, 128]]),
                    bass.AP(self.sel, 0, [[16, 64], [1, 16], [0, 128]]), ALU.mult, [rBtok, R("sel")], rBtm)
            byo, ryo, byo_i = self.bankp()
            h0v = lambda i: self.stg[i // 2][:, (i % 2) * 512:(i % 2 + 1) * 512].rearrange("p (i n) -> p i n", i=4)
            rpar = lambda i: R("stg", i // 2)
            src = lambda b: d["sst"][b, 8 * g:8 * g + 8].rearrange("(i hh) p n -> (hh p) i n", hh=2)

            def ld(b):
                S.dma("pool", h0v(b % 4), src(b), reads=[rpar(b % 4)], writes=[R("h0", b % 4)])

            def tpc(b):
                h0, rh0 = h0v(b % 4), R("h0", b % 4)
                bt, rt, bt_i = self.bankp()
                for i in range(4):
                    self.tp(bt[:, i * 128:(i + 1) * 128], h0[:, i, :], self.ident_f[:, :], [rh0, rpar(b % 4), R("ident_f")], [rt])
                hTbb = self.hTb2[b % 2]
                rhTbb = R("hTb") if b % 2 == 0 else R("hTb2", 1)
                self.cp("act", hTbb[:], bt[:], [rt], [rhTbb])
                self.pinned.discard(bt_i)
            ld(0)
            ld(1)
            ld(2)
            ld(3)
            tpc(0)
            for b in range(16):
                if b + 1 < 16:
                    tpc(b + 1)
                h0, rh0 = h0v(b % 4), R("h0", b % 4)
                hTbb = self.hTb2[b % 2]
                rhTbb = R("hTb") if b % 2 == 0 else R("hTb2", 1)
                self.mm(byo[0:64, :], Cmv[:, b, :], hTbb[:], b == 0, b == 15, [R("MT"), rhTbb], [ryo])
                bsn, rsn, bsn_i = self.bankp()
                for i in range(4):
                    self.mm(bsn[:, i * 128:(i + 1) * 128], xw[0:64, i * 128:(i + 1) * 128], Btmv[:, b, :], True, True,
                            [rxw] + rBtm, [rsn])
                self.tt("dve", h0, h0, bass.AP(self.etn, b * 16 + 4 * g, [[256, 128], [1, 4], [0, 128]]), ALU.mult,
                        [rh0, R("etn"), rpar(b % 4)], [rh0])
                self.tt("dve", h0, h0, bsn[:].rearrange("p (i n) -> p i n", i=4), ALU.add, [rh0, rsn, rpar(b % 4)], [rh0])
                self.pinned.discard(bsn_i)
                self.out_toks.append(S.dma("sp", d["sst_o"][b, 8 * g:8 * g + 8].rearrange("(i hh) p n -> (hh p) i n", hh=2), h0,
                                           reads=[rh0, rpar(b % 4)]))
                if b + 4 < 16:
                    ld(b + 4)
                yield "gap"
        if kind != "m":
            self.tt("dve", v3(self.t1), byo[0:TK, :].rearrange("p (h j) -> p h j", h=8), self.bc_hp(self.eacs_t, off, TK, NCH32), ALU.mult,
                    [ryo, R("eacs", ci)], [R("t1")])
            self.tt("dve", self.ynb[0:TK, :], self.yt[0:TK, :], self.t1[0:TK, :], ALU.add, [R("yt"), R("t1")], [R("ynb")])
            self.pinned.discard(byo_i)
        yield
        bv, rv, bv_i = self.bankp()
        bvv = self.bview(bv)
        for q in range(4):
            self.tp(bvv[:, q * 128:q * 128 + TK], self.ynb[0:TK, q * 128:(q + 1) * 128], self.ident_b[0:TK, 0:TK], [R("ynb"), R("ident_b")], [rv])
        self.tt("dve", self.Y[:, 4 * g:4 * g + 4, cc0:cc0 + TK], bvv[:, 0:512].rearrange("p (q t) -> p q t", q=4)[:, :, 0:TK],
                B["gz"][:, :, lc:lc + TK], ALU.mult, [rv] + [r_ for q_ in range(4) for r_ in B["rd"]("gz", q_)], [R("Y", 4 * g + q_) for q_ in range(4)])
        self.pinned.discard(bv_i)
        yield

    def ssd_tile(self, sti, g, ti, c0, chunks, B):
        R = self.R
        if chunks[0][0] == "p" and ti == 0:
            self.cp("act", self.hTb[:], self.hT[:, g * 512:(g + 1) * 512], [R("hT")], [R("hTb")])
        frs = [dict() for _ in chunks]
        for t_ in self.ssd_front(sti, g, c0, chunks[0], 0, frs[0], B):
            yield t_
        for i, ch in enumerate(chunks):
            bgen = self.ssd_back(sti, g, c0, ch, frs[i], B)
            if i + 1 < len(chunks):
                fgen = self.ssd_front(sti, g, c0, chunks[i + 1], i + 1, frs[i + 1], B)
                if ch[0] == "p" and chunks[i + 1][0] == "p":
                    alive = [fgen, bgen]
                    while alive:
                        for gen in list(alive):
                            try:
                                yield next(gen)
                            except StopIteration:
                                alive.remove(gen)
                else:
                    for t_ in fgen:
                        yield t_
                    for t_ in bgen:
                        yield t_
            else:
                for t_ in bgen:
                    yield t_

    def state_out(self, g):
        R, S, d = self.R, self.S, self.d
        bk, br = self.bank()
        for i in range(4):
            self.tp(bk[:, i * 128:(i + 1) * 128], self.hT[:, g * 512 + i * 128: g * 512 + (i + 1) * 128], self.ident_f[:, :], [R("hT"), R("ident_f")], [br])
        hn = self.stg[1][:, 0:512]
        self.cp("act", hn, bk[:], [br], [R("stg", 1)])
        self.out_toks.append(S.dma("sp", d["pss"][8 * g:8 * g + 8].rearrange("(i hh) p n -> (hh p) i n", hh=2),
                                   hn.rearrange("p (i n) -> p i n", i=4), reads=[R("stg", 1)]))

    def final_out(self, sti, tiles):
        R, S, d = self.R, self.S, self.d
        Yf = self.Y[:, :, :].rearrange("p c n -> p (c n)").bitcast(F32).rearrange("p (c n) -> p c n", c=8)
        rY = lambda c: [R("Y", 2 * c), R("Y", 2 * c + 1)]
        for (c0, n) in tiles:
            self.rmsnorm(3, c0, n, lambda c: (Yf[:, c, c0:c0 + n], rY(c)))
        blocks = []
        if sti == 0:
            blocks.append((0, 64, d["ys"][:, :]))
            for i in range(4):
                blocks.append((80 + 128 * i, 128, d["yp"][128 * i:128 * (i + 1), :]))
        else:
            for i in range(4):
                blocks.append((128 * i, 128, d["yp"][512 * sti + 128 * i: 512 * sti + 128 * (i + 1), :]))
        for (cb, nb, dst) in blocks:
            si = self.stg_i % 2
            self.stg_i += 1
            st = self.stg[si]
            for half in range(2):
                bk, br = self.bank()
                for j in range(4):
                    c = half * 4 + j
                    self.tp(bk[0:nb, j * 128:(j + 1) * 128], Yf[:, c, cb:cb + nb], self.ident_f[:, :], rY(c) + [R("ident_f")], [br])
                self.cp("act" if half == 0 else "dve", st[0:nb, half * 512:(half + 1) * 512], bk[0:nb, :], [br], [R("stg", si)])
            self.out_toks.append(S.dma("sp", dst, st[0:nb, :], reads=[R("stg", si)]))

    def conv_state_out(self):
        R, S, d = self.R, self.S, self.d

        def emit(src_fn, nchunks, rows, dst_fn, rsrc):
            for part in range(nchunks // 8):
                si = self.stg_i % 2
                self.stg_i += 1
                st = self.stg[si]
                for half in range(2):
                    bk, br = self.bank()
                    for j in range(4):
                        c = part * 8 + half * 4 + j
                        self.tp(bk[0:rows, j * 128:(j + 1) * 128], src_fn(c), self.ident_f[:, :], [rsrc(c), R("ident_f")], [br])
                    self.cp("act" if half == 0 else "dve", st[0:rows, half * 512:(half + 1) * 512], bk[0:rows, :], [br], [R("stg", si)])
                self.out_toks.append(S.dma("sp", dst_fn(part), st[0:rows, :], reads=[R("stg", si)]))
        emit(lambda c: self.haloA[:, c, :], 8, 2, lambda part: d["pca"][:, :], lambda c: R("haloA", c))
        emit(lambda c: self.haloB[:, c, :], 24, 3, lambda part: d["psc"][:, part * 1024:(part + 1) * 1024], lambda c: R("haloB", c))
        emit(lambda c: self.outA[:, c, :], 8, 32, lambda part: d["sca_o"][:, :], lambda c: R("stA", c))
        emit(lambda c: self.outB[:, c, :], 24, 48, lambda part: d["ssc_o"][:, part * 1024:(part + 1) * 1024], lambda c: R("stB", c))

    def run(self):
        d = self.d
        self.out_toks = []
        items = []
        for sti in range(4):
            tiles = [(0, 80), (80, 512)] if sti == 0 else [(0, 512)]
            items.append((0, None, lambda pids, sti=sti, tiles=tiles: (self.load_x(sti), self.norm_to_xn(0, tiles))))
            items += self.ffn_items(d["w1gu"], d["w1d"], tiles)
            items.append((0, None, lambda pids, tiles=tiles: self.norm_to_xn(1, tiles)))
            items += self.a_items(sti, d["w_in"], tiles)
            items.append(self.atail_item(d["w_in"], d["w_a_out"], d["w_o"], tiles))
            items += self.b_items(sti, d["w_in"], tiles)
            items.append(self.btail_item(sti, d["w_in"], d["w_b_out"], d["w_o"], tiles))
            items.append((0, None, lambda pids, tiles=tiles: self.norm_to_xn(2, tiles)))
            items += self.ffn_items(d["w2gu"], d["w2d"], tiles)
            items.append((0, None, lambda pids, sti=sti, tiles=tiles: self.final_out(sti, tiles)))
        items.append((0, None, lambda pids: self.conv_state_out()))
        N = len(items)
        pids = [None] * N
        loaded = [False] * N

        def do_load(j):
            k, load, _ = items[j]
            pids[j] = self.take_pages(k)
            if load is not None:
                load(pids[j])
            loaded[j] = True

        for i in range(N):
            if not loaded[i]:
                do_load(i)
            j = i + 1
            if j < N and not loaded[j]:
                k = items[j][0]
                cand = [(self.page_i + x) % 4 for x in range(k)]
                if not (set(cand) & set(pids[i])):
                    do_load(j)
            items[i][2](pids[i])
        self.S.wait_all("sp", self.out_toks)


IN_SPECS = [
    ("xp", [2048, 1024]), ("xs", [64, 1024]), ("meta", [16, 1024]), ("sca", [32, 1024]), ("ssc", [48, 3072]),
    ("sst", [16, 32, 64, 128]), ("cst1", [96, 128]), ("cst2", [96, 128]), ("dtb", [1, 32]), ("alog", [1, 32]), ("dsk", [1, 32]),
    ("w1gu", [1, 1024, 5632]), ("w1d", [1, 2816, 1024]), ("w_in", [1, 1024, 10272]), ("w_a_out", [1, 1024, 1024]),
    ("w_b_out", [1, 2048, 1024]), ("w_o", [1, 1024, 1024]), ("w2gu", [1, 1024, 5632]), ("w2d", [1, 2816, 1024]),
]
OUT_SPECS = [
    ("yp", [2048, 1024]), ("ys", [64, 1024]), ("pca", [2, 1024]), ("psc", [3, 3072]), ("pss", [32, 64, 128]),
    ("sca_o", [32, 1024]), ("ssc_o", [48, 3072]), ("sst_o", [16, 32, 64, 128]),
]


def build_nc():
    nc = bass.Bass("TRN2", target_bir_lowering=False)
    d = {}
    for name, shape in IN_SPECS:
        d[name] = nc.dram_tensor(name, shape, F32, kind="ExternalInput").ap()
    for name, shape in OUT_SPECS:
        d[name] = nc.dram_tensor(name, shape, F32, kind="ExternalOutput").ap()
    with ExitStack() as st:
        S = Sched(nc, st)
        K = Kern(S, nc, d)
        K.alloc()
        K.ones_f = S.sb([128, 128], F32, name="ones_f")
        S.op("pool", lambda e: e.memset(K.ones_f[:], 1.0), [], [K.R("ones_f")])
        K.setup()
        K.run()
        S.emit()
    return nc


_NC_CACHE = {}


def kernel(x_prompt, x_sample, state_conv_a, state_ssm_conv, state_ssm, meta_tokens,
           norm_ffn1, ffn1_w_gu, ffn1_w_down, norm_mix, w_in, conv_a_w, w_a_out,
           ssm_conv_w, ssm_conv_b, dt_bias, a_log, d_skip, ssm_norm_w, w_b_out, w_o,
           norm_ffn2, ffn2_w_gu, ffn2_w_down, norm_final):
    f = lambda a: np.ascontiguousarray(np.asarray(a, dtype=np.float32))
    n = 8
    if "nc" not in _NC_CACHE:
        _NC_CACHE["nc"] = build_nc()
    nc = _NC_CACHE["nc"]
    cst1 = np.concatenate([f(norm_ffn1).reshape(8, 128), f(norm_mix).reshape(8, 128), f(norm_ffn2).reshape(8, 128),
                           f(norm_final).reshape(8, 128), f(conv_a_w).reshape(24, 128), f(ssm_conv_b).reshape(24, 128),
                           f(ssm_norm_w).reshape(16, 128)], axis=0)
    cst2 = f(ssm_conv_w).reshape(96, 128)
    shared = {
        "meta": f(meta_tokens), "cst1": f(cst1), "cst2": cst2, "dtb": f(dt_bias).reshape(1, 32), "alog": f(a_log).reshape(1, 32),
        "dsk": f(d_skip).reshape(1, 32), "w1gu": f(ffn1_w_gu), "w1d": f(ffn1_w_down), "w_in": f(w_in), "w_a_out": f(w_a_out),
        "w_b_out": f(w_b_out), "w_o": f(w_o), "w2gu": f(ffn2_w_gu), "w2d": f(ffn2_w_down),
    }
    xp, xs = f(x_prompt), f(x_sample)
    sca, ssc, sst = f(state_conv_a), f(state_ssm_conv), f(state_ssm)
    in_maps = []
    for c in range(n):
        m = dict(shared)
        m["xp"] = xp[c]
        m["xs"] = xs[16 * c:16 * (c + 1)].reshape(64, 1024)
        m["sca"] = sca[0, 16 * c:16 * (c + 1)].reshape(32, 1024)
        m["ssc"] = ssc[0, 16 * c:16 * (c + 1)].reshape(48, 3072)
        m["sst"] = sst[0, 16 * c:16 * (c + 1)]
        in_maps.append(m)
    res = run_bass_kernel_spmd(nc, in_maps, core_ids=list(range(n)))
    rs = res.results
    y_prompt = np.stack([rs[c]["yp"] for c in range(n)], axis=0)
    y_sample = np.concatenate([rs[c]["ys"].reshape(16, 4, 1024) for c in range(n)], axis=0)
    p_ca = np.stack([rs[c]["pca"] for c in range(n)], axis=0)[None]
    p_sc = np.stack([rs[c]["psc"] for c in range(n)], axis=0)[None]
    p_ss = np.stack([rs[c]["pss"] for c in range(n)], axis=0)[None]
    s_ca = np.concatenate([rs[c]["sca_o"].reshape(16, 2, 1024) for c in range(n)], axis=0)[None]
    s_sc = np.concatenate([rs[c]["ssc_o"].reshape(16, 3, 3072) for c in range(n)], axis=0)[None]
    s_ss = np.concatenate([rs[c]["sst_o"] for c in range(n)], axis=0)[None]
    return tuple(np.ascontiguousarray(a, dtype=np.float32) for a in (y_prompt, y_sample, p_ca, p_sc, p_ss, s_ca, s_sc, s_ss))
```

```python
import numpy as np
from contextlib import ExitStack
import concourse.bass as bass
import concourse.mybir as mybir
from concourse.bass_utils import run_bass_kernel_spmd

F32 = mybir.dt.float32
BF16 = mybir.dt.bfloat16
AF = mybir.ActivationFunctionType
ALU = mybir.AluOpType
AX = mybir.AxisListType

ENGS = ("pe", "act", "dve", "pool", "sp")
SAME_SYNC = {"pe": False, "act": True, "dve": True, "pool": True, "sp": False}
N_DMA_SEMS = 24


class Res:
    __slots__ = ("name", "w", "r", "excl")

    def __init__(self, name="", excl=False):
        self.name = name
        self.w = None
        self.r = {}
        self.excl = excl


class Sched:
    def __init__(self, nc, st):
        self.nc = nc
        self.st = st
        self.prog = {e: [] for e in ENGS}
        self.cnt = {e: 0 for e in ENGS}
        self.seen = {e: {} for e in ENGS}
        self.dma_val = {("d", i): 0 for i in range(N_DMA_SEMS)}
        self.dma_i = 0
        self.nsb = 0

    def sb(self, shape, dtype, name=None):
        self.nsb += 1
        return self.st.enter_context(self.nc.sbuf_tensor("s_" + (name or f"sb{self.nsb}"), list(shape), dtype))

    def ps(self, shape, dtype=F32, name=None):
        self.nsb += 1
        return self.st.enter_context(self.nc.psum_tensor("p_" + (name or f"ps{self.nsb}"), list(shape), dtype))

    def _need(self, eng, deps):
        need = {}
        seen = self.seen[eng]
        for key, val in deps:
            if key == eng and not SAME_SYNC[eng]:
                continue
            if seen.get(key, 0) >= val:
                continue
            if need.get(key, 0) < val:
                need[key] = val
        for key, val in need.items():
            seen[key] = val
        return list(need.items())

    @staticmethod
    def _deps(reads, writes, eng=None):
        deps = []
        for r in reads:
            if r.w is not None:
                deps.append(r.w)
            if r.excl:
                deps.extend((k, v) for k, v in r.r.items() if k != eng)
        for w in writes:
            if w.w is not None:
                deps.append(w.w)
            deps.extend(w.r.items())
        return deps

    @staticmethod
    def _mark(tok, reads, writes):
        for r in reads:
            if r.r.get(tok[0], 0) < tok[1]:
                r.r[tok[0]] = tok[1]
        for w in writes:
            w.w = tok
            w.r = {}

    def op(self, eng, fn, reads=(), writes=()):
        waits = self._need(eng, self._deps(reads, writes, eng))
        self.cnt[eng] += 1
        tok = (eng, self.cnt[eng])
        self.prog[eng].append((waits, fn, tok))
        self._mark(tok, reads, writes)
        return tok

    def dma(self, q, out, in_, reads=(), writes=(), **kw):
        key = ("d", self.dma_i % N_DMA_SEMS)
        self.dma_i += 1
        prev = self.dma_val[key]
        deps = self._deps(reads, writes, q)
        if prev:
            deps.append((key, prev))
        waits = self._need(q, deps)
        self.dma_val[key] = prev + 16
        tok = (key, prev + 16)
        self.prog[q].append((waits, lambda e: e.dma_start(out=out, in_=in_, **kw), tok))
        self._mark(tok, reads, writes)
        return tok

    def wait_all(self, eng, toks):
        waits = self._need(eng, toks)
        self.prog[eng].append((waits, None, None))

    def emit(self):
        nc = self.nc
        marked = {e: set() for e in ENGS}
        for e in ENGS:
            for waits, fn, tok in self.prog[e]:
                for key, val in waits:
                    if key in marked:
                        marked[key].add(val)
        rank = {}
        for e in ENGS:
            for i, v in enumerate(sorted(marked[e])):
                rank[(e, v)] = i + 1
        sems = {}
        for e in ENGS:
            sems[e] = self.st.enter_context(nc.semaphore(f"sem_{e}"))
        for i in range(N_DMA_SEMS):
            sems[("d", i)] = self.st.enter_context(nc.semaphore(f"sem_d{i}"))
        handles = {"pe": "tensor", "act": "scalar", "dve": "vector", "pool": "gpsimd", "sp": "sync"}

        def run(e, eng_handle):
            for waits, fn, tok in self.prog[e]:
                for key, val in waits:
                    v = rank[(key, val)] if key in marked else val
                    eng_handle.wait_ge(sems[key], v)
                if fn is None:
                    continue
                inst = fn(eng_handle)
                if tok[0] in marked:
                    if tok[1] in marked[tok[0]]:
                        inst.then_inc(sems[tok[0]], 1)
                else:
                    inst.then_inc(sems[tok[0]], 16)

        with nc.Block() as block:
            for e in ENGS:
                getattr(block, handles[e])(lambda h, e=e: run(e, h))


D = 1024
DFF = 2816
NFF = 22
DSSM = 2048
DXBC = 3072
DIN = 10272
OFF_AB, OFF_AC, OFF_AH, OFF_Z, OFF_X, OFF_B, OFF_C, OFF_DT, OFF_GA, OFF_GB = (
    0, 1024, 2048, 3072, 5120, 7168, 7680, 8192, 8224, 9248)
EPS = 1e-6
TSMAX = 592
NEGBIG = -30000.0
CONV_ENG = "dve"
USE_HILO = False
USE_YCOPY = False
USE_WSCALE = False


class Kern:
    def __init__(self, S, nc, d):
        self.S = S
        self.nc = nc
        self.d = d
        self.res = {}
        self.bank_i = 0
        self.pinned = set()
        self.page_i = 0

    def R(self, *key):
        r = self.res.get(key)
        if r is None:
            r = self.res[key] = Res(str(key), excl=(key[0] == "bank"))
        return r

    def alloc(self):
        S = self.S
        self.banks = [S.ps([128, 512], F32, name=f"bank{i}") for i in range(8)]
        self.pages = [S.sb([128, 8192], BF16, name=f"page{i}") for i in range(4)]
        self.r = S.sb([128, 8, TSMAX], F32, name="r")
        self.xn = S.sb([128, 8, TSMAX], BF16, name="xn")
        self.Y = S.sb([128, 16, TSMAX], BF16, name="Y")
        self.ident_f = S.sb([128, 128], F32, name="ident_f")
        self.ident_b = S.sb([128, 128], BF16, name="ident_b")
        self.ones_b = S.sb([128, 128], BF16, name="ones_b")
        self.BT = S.sb([128, 128], F32, name="BT")
        self.NEG = S.sb([128, 128], BF16, name="NEGm")
        self.BT64 = S.sb([64, 64], F32, name="BT64")
        self.BO64 = S.sb([64, 64], F32, name="BO64")
        self.NEG64 = S.sb([64, 64], BF16, name="NEG64")
        self.ones64 = S.sb([64, 128], F32, name="ones64")
        self.sel = S.sb([64, 16], F32, name="sel")
        self.sel3 = S.sb([64, 16], F32, name="sel3")
        self.maskall = S.sb([128, 16, 64], BF16, name="maskall")
        self.hhm = S.sb([128, 1], F32, name="hhm")
        self.cvec = S.sb([128, 96], F32, name="cvec")
        self.cw = S.sb([128, 96], F32, name="cw")
        self.cst1 = S.sb([96, 128], F32, name="cst1")
        self.cst2 = S.sb([96, 128], F32, name="cst2")
        self.dtb = S.sb([32, 1], F32, name="dtb")
        self.alog = S.sb([32, 1], F32, name="alog")
        self.aneg = S.sb([32, 1], F32, name="aneg")
        self.Dbc = S.sb([128, 32], F32, name="Dbc")
        self.stg = [S.sb([128, 1024], F32, name=f"stg{i}") for i in range(2)]
        self.stg_i = 0
        self.sg = [S.sb([128, 512], F32, name=f"sg{i}") for i in range(2)]
        self.hb = S.sb([128, 4, 512], BF16, name="hb")
        self.ma = S.sb([128, 8, 512], BF16, name="ma")
        self.haloA = S.sb([128, 8, 2], F32, name="haloA")
        self.stA = S.sb([128, 8, 32], F32, name="stA")
        self.outA = self.stA
        self.pre = S.sb([128, 2, 520], F32, name="pre")
        self.xc = S.sb([128, 4, 512], F32, name="xc")
        self.Bc = S.sb([128, 512], BF16, name="Bc")
        self.Cc = S.sb([128, 512], BF16, name="Cc")
        self.haloB = S.sb([128, 24, 3], F32, name="haloB")
        self.stB = S.sb([128, 24, 48], F32, name="stB")
        self.outB = self.stB
        self.dtfm = S.sb([32, TSMAX], F32, name="dtfm")
        self.dafm = S.sb([32, TSMAX], F32, name="dafm")
        self.ldtfm = S.sb([32, TSMAX], F32, name="ldtfm")
        NCH = 6
        self.dt_t = S.sb([128, NCH, 32], F32, name="dt_t")
        self.da_t = S.sb([128, NCH, 32], F32, name="da_t")
        self.nacs_t = S.sb([128, NCH, 32], F32, name="nacs_t")
        self.wdec_t = S.sb([128, NCH, 32], F32, name="wdec_t")
        self.eacs_t = S.sb([128, NCH, 32], F32, name="eacs_t")
        self.etot_t = S.sb([128, NCH, 32], F32, name="etot_t")
        self.tmp32 = S.sb([128, 32], F32, name="tmp32")
        self.gz = S.sb([128, 4, 512], BF16, name="gz")
        self.xdt = S.sb([128, 512], BF16, name="xdt")
        self.xw2 = [S.sb([128, 512], BF16, name=f"xw{i}") for i in range(2)]
        self.xD2 = [S.sb([128, 512], F32, name=f"xD{i}") for i in range(2)]
        self.Btok2 = [S.sb([128, 128], BF16, name=f"Btok{i}") for i in range(2)]
        self.cbT = S.sb([128, 128], BF16, name="cbT")
        self.LT = S.sb([128, 8, 128], BF16, name="LT")
        self.MT = S.sb([128, 8, 128], BF16, name="MT")
        self.t1 = S.sb([128, 512], F32, name="t1")
        self.yt = S.sb([128, 512], F32, name="yt")
        self.sz = S.sb([128, 512], F32, name="sz")
        self.ynb = S.sb([128, 512], BF16, name="ynb")
        self.hT = S.sb([128, 2048], F32, name="hT")
        self.hTb = S.sb([128, 512], BF16, name="hTb")
        self.hTb2 = [self.hTb, S.sb([128, 512], BF16, name="hTb_b")]
        self.etn = S.sb([128, 16, 16], F32, name="etn")

    def bank(self):
        while True:
            i = self.bank_i % 8
            self.bank_i += 1
            if i not in self.pinned:
                break
        self.last_bank = i
        return self.banks[i], self.R("bank", i)

    def mm(self, out, lhsT, rhs, start, stop, reads, writes):
        self.S.op("pe", lambda e: e.matmul(out, lhsT=lhsT, rhs=rhs, start=start, stop=stop), reads, writes)

    def tp(self, out, in_, ident, reads, writes):
        self.S.op("pe", lambda e: e.transpose(out, in_, ident), reads, writes)

    def act(self, out, in_, func, reads, writes, **kw):
        self.S.op("act", lambda e: e.activation(out=out, in_=in_, func=func, **kw), reads, writes)

    def tt(self, eng, out, in0, in1, op, reads, writes):
        self.S.op(eng, lambda e: e.tensor_tensor(out=out, in0=in0, in1=in1, op=op), reads, writes)

    def ts(self, eng, out, in0, s1, s2, op0, op1, reads, writes):
        self.S.op(eng, lambda e: e.tensor_scalar(out=out, in0=in0, scalar1=s1, scalar2=s2, op0=op0, op1=op1), reads, writes)

    def stt(self, eng, out, in0, scalar, in1, op0, op1, reads, writes):
        self.S.op(eng, lambda e: e.scalar_tensor_tensor(out=out, in0=in0, scalar=scalar, in1=in1, op0=op0, op1=op1), reads, writes)

    def cp(self, eng, out, in_, reads, writes):
        if eng == "act":
            self.S.op("act", lambda e: e.copy(out=out, in_=in_), reads, writes)
        else:
            self.S.op(eng, lambda e: e.tensor_copy(out=out, in_=in_), reads, writes)

    def asel(self, out, in_, pattern, cmp, fill, base, cm, res):
        self.S.op("pool", lambda e: e.affine_select(out=out, in_=in_, pattern=pattern, compare_op=cmp,
                                                    fill=fill, base=base, channel_multiplier=cm), [res], [res])

    def setup(self):
        S, d = self.S, self.d
        R = self.R
        ms = lambda t, v, res: S.op("pool", lambda e: e.memset(t, v), [], [res])
        ms(self.ident_f[:], 0.0, R("ident_f"))
        self.asel(self.ident_f[:], self.ident_f[:], [[-1, 128]], ALU.not_equal, 1.0, 0, 1, R("ident_f"))
        self.cp("dve", self.ident_b[:], self.ident_f[:], [R("ident_f")], [R("ident_b")])
        ms(self.ones_b[:], 1.0, R("ones_b"))
        ms(self.ones64[:], 1.0, R("ones64"))
        ms(self.BT[:], 1.0, R("BT"))
        self.asel(self.BT[:], self.BT[:], [[1, 128]], ALU.is_ge, 0.0, 0, -1, R("BT"))
        ms(self.NEG[:], 0.0, R("NEG"))
        self.asel(self.NEG[:], self.NEG[:], [[1, 128]], ALU.is_ge, NEGBIG, 0, -1, R("NEG"))
        v3 = lambda t: t[:].rearrange("p (b t) -> p b t", t=4)
        ms(self.BO64[:], 1.0, R("BO64"))
        self.asel(v3(self.BO64), v3(self.BO64), [[-4, 16], [0, 4]], ALU.is_ge, 0.0, 0, 1, R("BO64"))
        self.asel(v3(self.BO64), v3(self.BO64), [[4, 16], [0, 4]], ALU.is_ge, 0.0, 3, -1, R("BO64"))
        ms(self.BT64[:], 1.0, R("BT64"))
        self.asel(v3(self.BT64), v3(self.BT64), [[-4, 16], [0, 4]], ALU.is_ge, 0.0, 0, 1, R("BT64"))
        self.asel(v3(self.BT64), v3(self.BT64), [[4, 16], [1, 4]], ALU.is_ge, 0.0, 0, -1, R("BT64"))
        ms(self.NEG64[:], 0.0, R("NEG64"))
        self.asel(v3(self.NEG64), v3(self.NEG64), [[-4, 16], [0, 4]], ALU.is_ge, NEGBIG, 0, 1, R("NEG64"))
        self.asel(v3(self.NEG64), v3(self.NEG64), [[4, 16], [1, 4]], ALU.is_ge, NEGBIG, 0, -1, R("NEG64"))
        ms(self.sel[:], 1.0, R("sel"))
        self.asel(self.sel[:], self.sel[:], [[-4, 16]], ALU.is_ge, 0.0, 0, 1, R("sel"))
        self.asel(self.sel[:], self.sel[:], [[4, 16]], ALU.is_ge, 0.0, 3, -1, R("sel"))
        ms(self.sel3[:], 0.0, R("sel3"))
        self.asel(self.sel3[:], self.sel3[:], [[-4, 16]], ALU.not_equal, 1.0, -3, 1, R("sel3"))
        ms(self.maskall[:], 1.0, R("maskall"))
        self.asel(self.maskall[:], self.maskall[:], [[-4, 16], [1, 64]], ALU.is_ge, 0.0, 0, 0, R("maskall"))
        self.asel(self.maskall[:], self.maskall[:], [[4, 16], [-1, 64]], ALU.is_ge, 0.0, 3, 0, R("maskall"))
        ms(self.hhm[:], 1.0, R("hhm"))
        self.asel(self.hhm[:], self.hhm[:], [[0, 1]], ALU.is_ge, 0.0, -64, 1, R("hhm"))
        S.dma("sp", self.cst1[:], d["cst1"], writes=[R("cst1")])
        S.dma("sp", self.cst2[:], d["cst2"], writes=[R("cst2")])
        bk, br = self.bank()
        self.tp(bk[:, 0:96], self.cst1[:], self.ident_f[0:96, 0:96], [R("cst1"), R("ident_f")], [br])
        self.cp("dve", self.cvec[:], bk[:, 0:96], [br], [R("cvec")])
        bk, br = self.bank()
        self.tp(bk[:, 0:96], self.cst2[:], self.ident_f[0:96, 0:96], [R("cst2"), R("ident_f")], [br])
        self.cp("dve", self.cw[:], bk[:, 0:96], [br], [R("cw")])
        S.dma("sp", self.dtb[:], d["dtb"].rearrange("o h -> h o"), writes=[R("dtb")], allow_slow_non_contiguous=True)
        S.dma("sp", self.alog[:], d["alog"].rearrange("o h -> h o"), writes=[R("alog")], allow_slow_non_contiguous=True)
        self.act(self.aneg[:], self.alog[:], AF.Exp, [R("alog")], [R("aneg")])
        self.ts("dve", self.aneg[:], self.aneg[:], -1.0, None, ALU.mult, ALU.bypass, [R("aneg")], [R("aneg")])
        S.dma("sp", self.Dbc[:], d["dsk"].partition_broadcast(128), writes=[R("Dbc")])
        st = self.stg[0]
        S.dma("sp", st[0:32, :], d["sca"], writes=[R("stg", 0)])
        for c in range(8):
            bk, br = self.bank()
            self.tp(bk[:, 0:32], st[0:32, c * 128:(c + 1) * 128], self.ident_f[0:32, 0:32], [R("stg", 0), R("ident_f")], [br])
            self.cp("dve", self.stA[:, c, :], bk[:, 0:32], [br], [R("stA", c)])
        for part in range(3):
            st = self.stg[1]
            S.dma("sp", st[0:48, :], d["ssc"][:, part * 1024:(part + 1) * 1024], writes=[R("stg", 1)])
            for c in range(8):
                bk, br = self.bank()
                self.tp(bk[:, 0:48], st[0:48, c * 128:(c + 1) * 128], self.ident_f[0:48, 0:48], [R("stg", 1), R("ident_f")], [br])
                self.cp("dve", self.stB[:, part * 8 + c, :], bk[:, 0:48], [br], [R("stB", part * 8 + c)])
        S.op("pool", lambda e: e.memset(self.haloA[:], 0.0), [], [R("haloA", c) for c in range(8)])
        S.op("pool", lambda e: e.memset(self.haloB[:], 0.0), [], [R("haloB", c) for c in range(24)])
        ms(self.hT[:], 0.0, R("hT"))

    def nw(self, which, c):
        return self.cvec[:, which * 8 + c: which * 8 + c + 1]

    def caw(self, k, c):
        return self.cvec[:, 32 + k * 8 + c: 32 + k * 8 + c + 1]

    def scb(self, c):
        return self.cvec[:, 56 + c: 56 + c + 1]

    def snw(self, c):
        return self.cvec[:, 80 + c: 80 + c + 1]

    def scw(self, k, c):
        return self.cw[:, k * 24 + c: k * 24 + c + 1]

    def x_blocks(self, sti):
        blocks = []
        if sti == 0:
            blocks.append(("sm", 0, 80))
            for i in range(4):
                blocks.append(("p", 80 + 128 * i, 128 * i))
        else:
            for i in range(4):
                blocks.append(("p", 128 * i, 512 * sti + 128 * i))
        return blocks

    def x_staging(self):
        R = self.R
        return [
            (self.xc[:, 0:2, :].rearrange("p q n -> p (q n)"), [R("xc_s0", 0), R("xc_s0", 1)]),
            (self.xc[:, 2:4, :].rearrange("p q n -> p (q n)"), [R("xc_s0", 2), R("xc_s0", 3)]),
            (self.gz[:, :, :].rearrange("p q n -> p (q n)").bitcast(F32), [R("gz_s0", q) for q in range(4)]),
            (self.pre[:, :, :].rearrange("p a n -> p (a n)")[:, 0:1024], [R("pre", 0), R("pre", 1)]),
            (self.stg[0][:, :], [R("stg", 0)]),
        ]

    def load_x_issue(self, sti):
        S, d = self.S, self.d
        stgs = self.x_staging()
        for bi, (kind, c0, src0) in enumerate(self.x_blocks(sti)):
            st, rs = stgs[bi]
            if kind == "sm":
                S.dma("pool", st[0:64, :], d["xs"], writes=rs)
                S.dma("pool", st[64:80, :], d["meta"], writes=rs)
            else:
                S.dma("pool", st[:, :], d["xp"][src0:src0 + 128, :], writes=rs)

    def load_x(self, sti):
        S, d, R = self.S, self.d, self.R
        stgs = self.x_staging()
        for bi, (kind, c0, src0) in enumerate(self.x_blocks(sti)):
            st, rs = stgs[bi]
            n = 80 if kind == "sm" else 128
            for half in range(2):
                bk, br = self.bank()
                for j in range(4):
                    c = half * 4 + j
                    self.tp(bk[:, j * 128:j * 128 + n], st[0:n, c * 128:(c + 1) * 128], self.ident_f[0:n, 0:n],
                            rs + [R("ident_f")], [br])
                eng = "act" if half == 0 else "dve"
                self.cp(eng, self.r[:, half * 4:half * 4 + 4, c0:c0 + n],
                        bk[:].rearrange("p (j n) -> p j n", j=4)[:, :, 0:n], [br], [R("r", half * 4 + j_) for j_ in range(4)])

    def rmsnorm(self, which, c0, n, out_fn):
        R = self.R
        bk, br = self.bank()
        for c in range(8):
            self.act(self.ma[:, c, 0:n], self.r[:, c, c0:c0 + n], AF.Square, [R("r", c)], [R("ma", c)])
            self.mm(bk[:, 0:n], self.ones_b[:], self.ma[:, c, 0:n], c == 0, c == 7, [R("ones_b"), R("ma", c)], [br])
        self.act(self.t1[:, 0:n], bk[:, 0:n], AF.Ln, [br], [R("t1")], scale=1.0 / D, bias=EPS)
        self.act(self.t1[:, 0:n], self.t1[:, 0:n], AF.Exp, [R("t1")], [R("t1")], scale=-0.5)
        for c in range(8):
            dst, wres = out_fn(c)
            self.stt("dve", dst, self.r[:, c, c0:c0 + n], self.nw(which, c), self.t1[:, 0:n], ALU.mult, ALU.mult,
                     [R("r", c), R("t1"), R("cvec")], wres if isinstance(wres, list) else [wres])

    def norm_to_xn(self, which, tiles):
        for (c0, n) in tiles:
            self.rmsnorm(which, c0, n, lambda c: (self.xn[:, c, c0:c0 + n], self.R("xn", c)))

    def take_pages(self, k):
        ids = [(self.page_i + j) % 4 for j in range(k)]
        self.page_i += k
        return ids

    def wdma(self, dst, src, pid):
        self.S.dma("pool", dst, src, writes=[self.R("page", pid)])

    def wview(self, w):
        return w[0].rearrange("(k p) n -> p k n", p=128)

    def ffn_items(self, wgu, wd, tiles):
        items = []
        slabs = [(0, 4), (4, 4), (8, 4), (12, 4), (16, 4), (20, 2)]
        for (j0, nj) in slabs:
            def load(pids, j0=j0, nj=nj):
                pg, pd = self.pages[pids[0]], self.pages[pids[1]]
                gv = pg[:, 0:8 * 2 * nj * 128].rearrange("p (k s n) -> p k s n", k=8, s=2)
                src = self.wview(wgu)
                self.wdma(gv[:, :, 0, :], src[:, :, j0 * 128:(j0 + nj) * 128], pids[0])
                self.wdma(gv[:, :, 1, :], src[:, :, DFF + j0 * 128:DFF + (j0 + nj) * 128], pids[0])
                dv = pd[:, 0:nj * 1024].rearrange("p (j n) -> p j n", j=nj)
                self.wdma(dv, wd[0].rearrange("(j p) n -> p j n", p=128)[:, j0:j0 + nj, :], pids[1])

            def compute(pids, j0=j0, nj=nj):
                R = self.R
                pg, pd = self.pages[pids[0]], self.pages[pids[1]]
                gv = pg[:, 0:8 * 2 * nj * 128].rearrange("p (k s n) -> p k s n", k=8, s=2)
                dv = pd[:, 0:nj * 1024].rearrange("p (j n) -> p j n", j=nj)
                pr0, pr1 = R("page", pids[0]), R("page", pids[1])
                for (c0, n) in tiles:
                    for jj in range(nj):
                        bg, rg = self.bank()
                        for k in range(8):
                            self.mm(bg[:, 0:n], gv[:, k, 0, jj * 128:(jj + 1) * 128], self.xn[:, k, c0:c0 + n], k == 0, k == 7, [pr0, R("xn", k)], [rg])
                        bu, ru = self.bank()
                        for k in range(8):
                            self.mm(bu[:, 0:n], gv[:, k, 1, jj * 128:(jj + 1) * 128], self.xn[:, k, c0:c0 + n], k == 0, k == 7, [pr0, R("xn", k)], [ru])
                        sg = self.sg[jj % 2]
                        self.act(sg[:, 0:n], bg[:, 0:n], AF.Silu, [rg], [R("sg", jj % 2)])
                        self.tt("dve", self.hb[:, jj, 0:n], sg[:, 0:n], bu[:, 0:n], ALU.mult, [R("sg", jj % 2), ru], [R("hb", jj)])
                    for c in range(8):
                        bd, rd = self.bank()
                        for jj in range(nj):
                            self.mm(bd[:, 0:n], dv[:, jj, c * 128:(c + 1) * 128], self.hb[:, jj, 0:n], jj == 0, jj == nj - 1, [pr1, R("hb", jj)], [rd])
                        self.stt("dve", self.r[:, c, c0:c0 + n], bd[:, 0:n], 0.5, self.r[:, c, c0:c0 + n], ALU.mult, ALU.add,
                                 [rd, R("r", c)], [R("r", c)])
            items.append((2, load, compute))
        return items

    def conv(self, out3, src3, L, wcols, reads, writes, acc=False):
        W = len(wcols)
        eng = CONV_ENG
        if not acc:
            self.ts(eng, out3, src3[:, :, 0:L], wcols[0], None, ALU.mult, ALU.bypass, reads, writes)
        for k in range(0 if acc else 1, W):
            self.stt(eng, out3, src3[:, :, k:k + L], wcols[k], out3, ALU.mult, ALU.add, reads + writes, writes)

    def a_items(self, sti, w_in, tiles):
        items = []
        for s in range(4):
            def load(pids, s=s):
                pv = self.pages[pids[0]][:, 0:8 * 3 * 256].rearrange("p (k s n) -> p k s n", k=8, s=3)
                src = self.wview(w_in)
                for i, off in enumerate((OFF_AB, OFF_AC, OFF_AH)):
                    self.wdma(pv[:, :, i, :], src[:, :, off + s * 256: off + (s + 1) * 256], pids[0])

            def compute(pids, s=s):
                R = self.R
                pv = self.pages[pids[0]][:, 0:8 * 3 * 256].rearrange("p (k s n) -> p k s n", k=8, s=3)
                pr = R("page", pids[0])
                for ti, (c0, n) in enumerate(tiles):
                    for jj in range(2):
                        c = s * 2 + jj
                        bks = []
                        for i in range(3):
                            bk, br = self.bank()
                            for k in range(8):
                                self.mm(bk[:, 0:n], pv[:, k, i, jj * 128:(jj + 1) * 128], self.xn[:, k, c0:c0 + n], k == 0, k == 7, [pr, R("xn", k)], [br])
                            bks.append((bk, br))
                        (bb, rb), (bc, rc), (bh, rh) = bks
                        ach = self.pre[:, jj, :]
                        ra = R("pre", jj)
                        cva = self.xc[:, jj, :]
                        rcv = R("xc_s0", jj)
                        sg = self.sg[jj]
                        rsg = R("sg", jj)
                        if sti == 0 and ti == 0:
                            a3 = ach[:, 0:96].rearrange("p (b t) -> p b t", t=6)
                            self.cp("dve", a3[:, :, 0:2], self.stA[:, c, :].rearrange("p (b k) -> p b k", k=2), [R("stA", c)], [ra])
                            self.cp("act", sg[:, 0:80], bc[:, 0:80], [rc], [rsg])
                            self.tt("dve", a3[:, :, 2:6], sg[:, 0:64].rearrange("p (b t) -> p b t", t=4),
                                    bh[:, 0:64].rearrange("p (b t) -> p b t", t=4), ALU.mult, [rsg, rh], [ra])
                            self.S.op("dve", lambda e, ach=ach: e.memset(ach[:, 100:102], 0.0), [], [ra])
                            self.tt("dve", ach[:, 102:118], sg[:, 64:80], bh[:, 64:80], ALU.mult, [rsg, rh], [ra])
                            wc = [self.caw(k, c) for k in range(3)]
                            self.conv(cva[:, 0:64].rearrange("p (b t) -> p b t", t=4), a3, 4, wc, [ra, R("cvec")], [rcv])
                            self.conv(cva[:, 64:80].rearrange("p (b t) -> p b t", b=1), ach[:, 100:118].rearrange("p (b t) -> p b t", b=1), 16, wc, [ra, R("cvec")], [rcv])
                            self.tt("dve", self.Y[:, c, c0:c0 + n], cva[:, 0:n], bb[:, 0:n], ALU.mult, [rcv, rb], [R("Y", c)])
                            self.cp("act", self.outA[:, c, :].rearrange("p (b k) -> p b k", k=2), a3[:, :, 4:6], [ra], [R("stA", c)])
                            self.cp("act", self.haloA[:, c, :], ach[:, 116:118], [ra], [R("haloA", c)])
                        else:
                            self.cp("act", ach[:, 0:2], self.haloA[:, c, :], [R("haloA", c)], [ra])
                            self.cp("act", sg[:, 0:n], bc[:, 0:n], [rc], [rsg])
                            self.tt("dve", ach[:, 2:2 + n], sg[:, 0:n], bh[:, 0:n], ALU.mult, [rsg, rh], [ra])
                            wc = [self.caw(k, c) for k in range(3)]
                            self.conv(cva[:, 0:n].rearrange("p (b t) -> p b t", b=1), ach[:, 0:2 + n].rearrange("p (b t) -> p b t", b=1), n, wc, [ra, R("cvec")], [rcv])
                            self.tt("dve", self.Y[:, c, c0:c0 + n], cva[:, 0:n], bb[:, 0:n], ALU.mult, [rcv, rb], [R("Y", c)])
                            self.cp("act", self.haloA[:, c, :], ach[:, n:n + 2], [ra], [R("haloA", c)])
            items.append((1, load, compute))
        return items

    def atail_item(self, w_in, w_a_out, w_o, tiles):
        def load(pids):
            v = lambda i: self.pages[pids[i]][:, :].rearrange("p (k n) -> p k n", k=8)
            self.wdma(v(0), self.wview(w_a_out), pids[0])
            self.wdma(v(1), self.wview(w_in)[:, :, OFF_GA:OFF_GA + 1024], pids[1])
            self.wdma(v(2), self.wview(w_o), pids[2])

        def compute(pids):
            R = self.R
            v = lambda i: self.pages[pids[i]][:, :].rearrange("p (k n) -> p k n", k=8)
            pr = [R("page", p) for p in pids]
            for (c0, n) in tiles:
                for c in range(8):
                    by, ry = self.bank()
                    for k in range(8):
                        self.mm(by[:, 0:n], v(0)[:, k, c * 128:(c + 1) * 128], self.Y[:, k, c0:c0 + n], k == 0, k == 7, [pr[0], R("Y", k)], [ry])
                    bg, rg = self.bank()
                    for k in range(8):
                        self.mm(bg[:, 0:n], v(1)[:, k, c * 128:(c + 1) * 128], self.xn[:, k, c0:c0 + n], k == 0, k == 7, [pr[1], R("xn", k)], [rg])
                    sg = self.sg[c % 2]
                    self.act(sg[:, 0:n], bg[:, 0:n], AF.Sigmoid, [rg], [R("sg", c % 2)])
                    self.tt("dve", self.ma[:, c, 0:n], sg[:, 0:n], by[:, 0:n], ALU.mult, [R("sg", c % 2), ry], [R("ma", c)])
                self.wo_apply(v(2), pr[2], c0, n)
        return (3, load, compute)

    def wo_apply(self, wv, pr, c0, n):
        R = self.R
        for c2 in range(8):
            bm, rm = self.bank()
            for k in range(8):
                self.mm(bm[:, 0:n], wv[:, k, c2 * 128:(c2 + 1) * 128], self.ma[:, k, 0:n], k == 0, k == 7, [pr, R("ma", k)], [rm])
            self.tt("dve", self.r[:, c2, c0:c0 + n], bm[:, 0:n], self.r[:, c2, c0:c0 + n], ALU.add, [rm, R("r", c2)], [R("r", c2)])

    def group_norm(self, g, c0, n):
        R = self.R
        bk, br = self.bank()
        for q in range(4):
            k = 4 * g + q
            sq, rsq = (self.ynb, R("ynb")) if q % 2 == 0 else (self.xdt, R("xdt"))
            self.act(sq[:, 0:n], self.Y[:, k, c0:c0 + n], AF.Square, [R("Y", k)], [rsq])
            self.mm(bk[:, 0:n], self.ones_b[:], sq[:, 0:n], q == 0, q == 3, [R("ones_b"), rsq], [br])
        self.act(self.t1[:, 0:n], bk[:, 0:n], AF.Ln, [br], [R("t1")], scale=1.0 / 512, bias=EPS)
        self.act(self.t1[:, 0:n], self.t1[:, 0:n], AF.Exp, [R("t1")], [R("t1")], scale=-0.5)
        for q in range(4):
            k = 4 * g + q
            self.stt("dve", self.Y[:, k, c0:c0 + n], self.Y[:, k, c0:c0 + n], self.snw(k), self.t1[:, 0:n], ALU.mult, ALU.mult,
                     [R("Y", k), R("t1"), R("cvec")], [R("Y", k)])

    def y_normalize(self, sti, ti, c0, n):
        R = self.R
        chunks = self.chunks_of(sti, ti)
        lo, hi = min(ch[3] for ch in chunks) * 4, (max(ch[3] for ch in chunks) + 1) * 4
        rt = self.rstd_tab[:, lo:hi]
        self.ts("dve", rt, self.ss_tab[:, lo:hi], 1.0 / 512, EPS, ALU.mult, ALU.add, [R("ss_tab")], [R("rstd_tab")])
        self.S.op("dve", lambda e: e.reciprocal(out=rt, in_=rt), [R("rstd_tab")], [R("rstd_tab")])
        self.act(rt, rt, AF.Sqrt, [R("rstd_tab")], [R("rstd_tab")])
        for g in range(4):
            bk, br = self.bank()
            for (kind, cc0, TK, ci) in chunks:
                lc = cc0 - c0
                col = ci * 4 + g
                self.mm(bk[:, lc:lc + TK], self.rstd_tab[0:TK, col:col + 1].broadcast_to([TK, 128]), self.ident_f[0:TK, 0:TK], True, True,
                        [R("rstd_tab"), R("ident_f")], [br])
            for q in range(4):
                k = 4 * g + q
                self.stt("dve", self.Y[:, k, c0:c0 + n], self.Y[:, k, c0:c0 + n], self.snw(k), bk[:, 0:n], ALU.mult, ALU.mult,
                         [R("Y", k), br, R("cvec")], [R("Y", k)])

    def btail_item(self, sti, w_in, w_b_out, w_o, tiles):
        def load(pids):
            v = lambda i: self.pages[pids[i]][:, :].rearrange("p (k n) -> p k n", k=8)
            src = self.wview(w_b_out)
            self.wdma(v(0), src[:, 0:8, :], pids[0])
            self.wdma(v(1), src[:, 8:16, :], pids[1])
            self.wdma(v(2), self.wview(w_in)[:, :, OFF_GB:OFF_GB + 1024], pids[2])
            self.wdma(v(3), self.wview(w_o), pids[3])

        def compute(pids):
            R = self.R
            v = lambda i: self.pages[pids[i]][:, :].rearrange("p (k n) -> p k n", k=8)
            pr = [R("page", p) for p in pids]
            for ti, (c0, n) in enumerate(tiles):
                for c in range(8):
                    by, ry = self.bank()
                    for k in range(16):
                        self.mm(by[:, 0:n], v(k // 8)[:, k % 8, c * 128:(c + 1) * 128], self.Y[:, k, c0:c0 + n], k == 0, k == 15, [pr[k // 8], R("Y", k)], [ry])
                    bg, rg = self.bank()
                    for k in range(8):
                        self.mm(bg[:, 0:n], v(2)[:, k, c * 128:(c + 1) * 128], self.xn[:, k, c0:c0 + n], k == 0, k == 7, [pr[2], R("xn", k)], [rg])
                    sg = self.sg[c % 2]
                    self.act(sg[:, 0:n], bg[:, 0:n], AF.Sigmoid, [rg], [R("sg", c % 2)])
                    self.tt("dve", self.ma[:, c, 0:n], sg[:, 0:n], by[:, 0:n], ALU.mult, [R("sg", c % 2), ry], [R("ma", c)])
                self.wo_apply(v(3), pr[3], c0, n)
        return (4, load, compute)

    def bview(self, bk):
        return bk[:].bitcast(BF16)

    def bc_hp(self, t, off, TK, pstep):
        return bass.AP(t, off, [[pstep, TK], [1, 8], [0, 64]])

    def chunks_of(self, sti, ti):
        if sti == 0:
            if ti == 0:
                return [("s", 0, 64, 0), ("m", 64, 16, 1)]
            return [("p", 80 + 128 * i, 128, 2 + i) for i in range(4)]
        return [("p", 128 * i, 128, i) for i in range(4)]

    def chunk_pre(self, kind, cc0, TK, ci):
        R = self.R
        NCH32 = 6 * 32
        BTx = self.BT64 if kind == "s" else self.BT
        BOx = self.BO64 if kind == "s" else self.ones_f
        bk, br = self.bank()
        self.tp(bk[0:TK, 0:32], self.dtfm[:, cc0:cc0 + TK], self.ident_f[0:32, 0:32], [R("dtfm"), R("ident_f")], [br])
        self.tp(bk[0:TK, 32:64], self.dafm[:, cc0:cc0 + TK], self.ident_f[0:32, 0:32], [R("dafm"), R("ident_f")], [br])
        self.tp(bk[0:TK, 64:96], self.ldtfm[:, cc0:cc0 + TK], self.ident_f[0:32, 0:32], [R("ldtfm"), R("ident_f")], [br])
        self.cp("dve", self.dt_t[0:TK, ci, :], bk[0:TK, 0:32], [br], [R("dt_t", ci)])
        self.cp("dve", self.da_t[0:TK, ci, :], bk[0:TK, 32:64], [br], [R("da_t", ci)])
        b2, r2 = self.bank()
        self.mm(b2[0:TK, 0:32], BTx[0:TK, 0:TK], self.da_t[0:TK, ci, :], True, True, [R("da_t", ci), R("BT"), R("BT64")], [r2])
        self.mm(b2[0:TK, 32:64], BOx[0:TK, 0:TK], self.da_t[0:TK, ci, :], True, True, [R("da_t", ci), R("BO64"), R("ones_f")], [r2])
        self.ts("dve", self.nacs_t[0:TK, ci, :], b2[0:TK, 0:32], -1.0, None, ALU.mult, ALU.bypass, [r2], [R("nacs", ci)])
        self.act(self.eacs_t[0:TK, ci, :], b2[0:TK, 0:32], AF.Exp, [r2], [R("eacs", ci)])
        self.act(self.etot_t[0:TK, ci, :], b2[0:TK, 32:64], AF.Exp, [r2], [R("etot", ci)])
        self.tt("dve", self.tmp32[0:TK, :], b2[0:TK, 32:64], self.nacs_t[0:TK, ci, :], ALU.add, [r2, R("nacs", ci)], [R("tmp32")])
        self.act(self.tmp32[0:TK, :], self.tmp32[0:TK, :], AF.Exp, [R("tmp32")], [R("tmp32")])
        self.tt("dve", self.wdec_t[0:TK, ci, :], self.tmp32[0:TK, :], self.dt_t[0:TK, ci, :], ALU.mult, [R("tmp32"), R("dt_t", ci)], [R("wdec", ci)])
        self.tt("dve", self.nacs_t[0:TK, ci, :], self.nacs_t[0:TK, ci, :], bk[0:TK, 64:96], ALU.add, [R("nacs", ci), br], [R("nacs", ci)])
        if kind == "s":
            in0 = bass.AP(self.etot_t, ci * 32, [[NCH32, 64], [0, 16], [1, 32]])
            in1 = bass.AP(self.sel3, 0, [[16, 64], [1, 16], [0, 32]])
            self.tt("dve", self.yt[0:64, :].rearrange("p (b h) -> p b h", b=16), in0, in1, ALU.mult, [R("etot", ci), R("sel3")], [R("yt")])
            b3, r3 = self.bank()
            self.mm(b3[:, 0:512], self.ones64[0:64, 0:128], self.yt[0:64, :], True, True, [R("yt"), R("ones64")], [r3])
            self.cp("dve", self.t1[:], b3[:], [r3], [R("t1")])
            tv = self.t1[:].rearrange("p (x two) -> p x two", two=2)
            ev = self.sz[:, 0:256]
            self.tt("dve", ev, tv[:, :, 1], tv[:, :, 0], ALU.subtract, [R("t1")], [R("sz")])
            self.stt("dve", self.etn[:].rearrange("p b i -> p (b i)"), ev, self.hhm[:, 0:1], tv[:, :, 0], ALU.mult, ALU.add,
                     [R("sz"), R("t1"), R("hhm")], [R("etn")])

    def bufset(self, bs):
        R = self.R
        if bs == 0:
            par = {"xc": [], "gz": [], "Bc": [], "Cc": []}
            bufs = dict(xc=self.xc, Bc=self.Bc, Cc=self.Cc, gz=self.gz)
        elif bs == "t0":
            pm = [R("ma", c) for c in range(3)]
            par = {"xc": pm, "gz": pm, "Bc": pm, "Cc": pm}
            mb = self.ma[:, :, :].rearrange("p c n -> p (c n)")
            mf = mb.bitcast(F32)
            bufs = dict(xc=mf[:, 0:320].rearrange("p (q n) -> p q n", q=4), Bc=mb[:, 640:720], Cc=mb[:, 720:800],
                        gz=mb[:, 800:1120].rearrange("p (q n) -> p q n", q=4))
        else:
            par = {"xc": [R("ma", c) for c in range(8)], "gz": [R("hb", c) for c in range(4)], "Bc": [R("sg", 0)], "Cc": [R("sg", 0)]}
            xc1 = self.ma[:, :, :].rearrange("p c n -> p (c n)").bitcast(F32).rearrange("p (q n) -> p q n", q=4)
            s0 = self.sg[0][:, :].bitcast(BF16)
            bufs = dict(xc=xc1, Bc=s0[:, 0:512], Cc=s0[:, 512:1024], gz=self.hb)
        B = dict(bufs)
        B["bs"] = bs
        B["rd"] = lambda name, q=0: [R(name + "_s%s" % bs, q)] + par[name]
        B["wr"] = lambda name, q=0: ([R(name + "_s%s" % bs, q)], par[name])
        return B

    def inproj_gen(self, sti, g, ti, c0, n, pids, B):
        R = self.R
        pa = self.pages[pids[0]][:, 0:8 * 800].rearrange("p (k n) -> p k n", k=8)
        pz = self.pages[pids[1]][:, 0:8 * 512].rearrange("p (k n) -> p k n", k=8)
        pr0, pr1 = R("page", pids[0]), R("page", pids[1])
        special = (sti == 0 and ti == 0)

        def cidof(q):
            return (4 * g + q) if q < 4 else (16 + g if q == 4 else 20 + g)

        def scratch(pq):
            return (self.sz, R("sz")) if pq == 0 else (self.sg[1], R("sg", 1))

        def stageA(q):
            cid = cidof(q)
            bk, br = self.bank()
            for k in range(8):
                self.mm(bk[:, 0:n], pa[:, k, q * 128:(q + 1) * 128], self.xn[:, k, c0:c0 + n], k == 0, k == 7, [pr0, R("xn", k)], [br])
            pq = q % 2
            rp = R("pre", pq)
            if special:
                p3 = self.pre[:, pq, 0:112].rearrange("p (b t) -> p b t", t=7)
                self.cp("dve", p3[:, :, 0:3], self.stB[:, cid, :].rearrange("p (b k) -> p b k", k=3), [R("stB", cid)], [rp])
                self.cp("act", p3[:, :, 3:7], bk[:, 0:64].rearrange("p (b t) -> p b t", t=4), [br], [rp])
                self.S.op("dve", lambda e, pq=pq: e.memset(self.pre[:, pq, 120:123], 0.0), [], [rp])
                self.cp("act", self.pre[:, pq, 123:139], bk[:, 64:80], [br], [rp])
            else:
                self.cp("dve", self.pre[:, pq, 0:3], self.haloB[:, cid, :], [R("haloB", cid)], [rp])
                self.cp("act", self.pre[:, pq, 3:3 + n], bk[:, 0:n], [br], [rp])

        def stageB(q):
            cid = cidof(q)
            pq = q % 2
            rp = R("pre", pq)
            cvb, rcv = scratch(pq)
            wc = [self.scw(k, cid) for k in range(4)]
            if q < 4:
                dst, (wdst, pdst) = B["xc"][:, q, 0:n], B["wr"]("xc", q)
            elif q == 4:
                dst, (wdst, pdst) = B["Bc"][:, 0:n], B["wr"]("Bc")
            else:
                dst, (wdst, pdst) = B["Cc"][:, 0:n], B["wr"]("Cc")
            if special:
                p3 = self.pre[:, pq, 0:112].rearrange("p (b t) -> p b t", t=7)
                self.conv(cvb[:, 0:64].rearrange("p (b t) -> p b t", t=4), p3, 4, wc, [rp, R("cw")], [rcv])
                self.conv(cvb[:, 64:80].rearrange("p (b t) -> p b t", b=1),
                          self.pre[:, pq, 120:139].rearrange("p (b t) -> p b t", b=1), 16, wc, [rp, R("cw")], [rcv])
                self.cp("act", self.outB[:, cid, :].rearrange("p (b k) -> p b k", k=3), p3[:, :, 4:7], [rp], [R("stB", cid)])
                self.cp("act", self.haloB[:, cid, :], self.pre[:, pq, 136:139], [rp], [R("haloB", cid)])
            else:
                self.conv(cvb[:, 0:n].rearrange("p (b t) -> p b t", b=1),
                          self.pre[:, pq, 0:3 + n].rearrange("p (b t) -> p b t", b=1), n, wc, [rp, R("cw")], [rcv])
                self.cp("act", self.haloB[:, cid, :], self.pre[:, pq, n:n + 3], [rp], [R("haloB", cid)])
            self.act(dst, cvb[:, 0:n], AF.Silu, [rcv, R("cvec")] + pdst, wdst, bias=self.scb(cid))

        def gate(q):
            bk, br = self.bank()
            for k in range(8):
                self.mm(bk[:, 0:n], pz[:, k, q * 128:(q + 1) * 128], self.xn[:, k, c0:c0 + n], k == 0, k == 7, [pr1, R("xn", k)], [br])
            wg, pg = B["wr"]("gz", q)
            self.act(B["gz"][:, q, 0:n], bk[:, 0:n], AF.Silu, [br] + pg, wg)

        stageA(0)
        yield "A"
        for q in range(6):
            if q + 1 < 6:
                stageA(q + 1)
                yield "A"
            stageB(q)
            if q < 4:
                gate(q)
            yield "B"
        if g == 0:
            bk, br = self.bank()
            for k in range(8):
                self.mm(bk[0:32, 0:n], pa[:, k, 768:800], self.xn[:, k, c0:c0 + n], k == 0, k == 7, [pr0, R("xn", k)], [br])
            self.act(self.dtfm[:, c0:c0 + n], bk[0:32, 0:n], AF.Exp, [br, R("dtb")], [R("dtfm")], bias=self.dtb[:, 0:1])
            self.act(self.dtfm[:, c0:c0 + n], self.dtfm[:, c0:c0 + n], AF.Ln, [R("dtfm")], [R("dtfm")], bias=1.0)
            self.ts("dve", self.dafm[:, c0:c0 + n], self.dtfm[:, c0:c0 + n], self.aneg[:, 0:1], None, ALU.mult, ALU.bypass,
                    [R("dtfm"), R("aneg")], [R("dafm")])
            self.act(self.ldtfm[:, c0:c0 + n], self.dtfm[:, c0:c0 + n], AF.Ln, [R("dtfm")], [R("ldtfm")])
            for ch in self.chunks_of(sti, ti):
                self.chunk_pre(*ch)
                yield

    def b_items(self, sti, w_in, tiles):
        items = []
        self.bp = {}

        def drain(gen):
            for _ in gen:
                pass
        for g in range(4):
            def load(pids, g=g):
                self.bp[g] = pids
                pa = self.pages[pids[0]][:, 0:8 * 800].rearrange("p (k n) -> p k n", k=8)
                pz = self.pages[pids[1]][:, 0:8 * 512].rearrange("p (k n) -> p k n", k=8)
                src = self.wview(w_in)
                self.wdma(pa[:, :, 0:512], src[:, :, OFF_X + g * 512: OFF_X + (g + 1) * 512], pids[0])
                self.wdma(pa[:, :, 512:640], src[:, :, OFF_B + g * 128: OFF_B + (g + 1) * 128], pids[0])
                self.wdma(pa[:, :, 640:768], src[:, :, OFF_C + g * 128: OFF_C + (g + 1) * 128], pids[0])
                if g == 0:
                    self.wdma(pa[:, :, 768:800], src[:, :, OFF_DT: OFF_DT + 32], pids[0])
                self.wdma(pz, src[:, :, OFF_Z + g * 512: OFF_Z + (g + 1) * 512], pids[1])

            def merge(ssd, inp, policy):
                ngap = 0
                state = {"inp": inp}

                def adv_in(nb):
                    while state["inp"] is not None and nb > 0:
                        try:
                            if next(state["inp"]) == "B":
                                nb -= 1
                        except StopIteration:
                            state["inp"] = None
                for tag in ssd:
                    if tag == "gap":
                        ngap += 1
                        adv_in(policy(ngap))
                adv_in(100)

            def compute(pids, g=g):
                R = self.R
                steady = lambda k: 2 if k <= 2 else 1
                if sti == 0:
                    (c00, n0), (c01, n1) = tiles
                    BT0, B0 = self.bufset("t0"), self.bufset(0)
                    if g == 0:
                        drain(self.inproj_gen(sti, 0, 0, c00, n0, pids, BT0))
                    merge(self.ssd_tile(sti, g, 0, c00, self.chunks_of(sti, 0), BT0),
                          self.inproj_gen(sti, g, 1, c01, n1, pids, B0), lambda k: 1)
                    self.group_norm(g, c00, n0)
                    nxt = None
                    if g + 1 < 4:
                        assert self.bp.get(g + 1) is not None, "next group's weights not prefetched"
                        nxt = self.inproj_gen(sti, g + 1, 0, c00, n0, self.bp[g + 1], BT0)
                    merge(self.ssd_tile(sti, g, 1, c01, self.chunks_of(sti, 1), B0), nxt, steady)
                    self.group_norm(g, c01, n1)
                    return
                (c0, n) = tiles[0]
                if g == 0:
                    drain(self.inproj_gen(sti, 0, 0, c0, n, pids, self.bufset(0)))
                nxt = None
                if g + 1 < 4:
                    assert self.bp.get(g + 1) is not None, "next group's weights not prefetched"
                    nxt = self.inproj_gen(sti, g + 1, 0, c0, n, self.bp[g + 1], self.bufset((g + 1) % 2))
                merge(self.ssd_tile(sti, g, 0, c0, self.chunks_of(sti, 0), self.bufset(g % 2)), nxt, steady)
                self.group_norm(g, c0, n)
                if sti == 3:
                    self.state_out(g)
            items.append((2, load, compute))
        return items

    def bankp(self):
        bk, br = self.bank()
        i = self.last_bank
        self.pinned.add(i)
        return bk, br, i

    def ssd_front(self, sti, g, c0, ch, seq, fr, B):
        R, S, d = self.R, self.S, self.d
        kind, cc0, TK, ci = ch
        lc = cc0 - c0
        NCH32 = 6 * 32
        off = ci * 32 + 8 * g
        pb = seq % 2
        fr["pb"] = pb
        xw, xD, Btok = self.xw2[pb], self.xD2[pb], self.Btok2[pb]
        BTx = self.BT64 if kind == "s" else self.BT
        NEGx = self.NEG64 if kind == "s" else self.NEG
        bx, rx, bx_i = self.bankp()
        for q in range(4):
            self.tp(bx[0:TK, q * 128:(q + 1) * 128], B["xc"][:, q, lc:lc + TK], self.ident_f[:, :], B["rd"]("xc", q) + [R("ident_f")], [rx])
        bB, rB, bB_i = self.bankp()
        bBv = self.bview(bB)
        self.tp(bBv[0:TK, 0:128], B["Bc"][:, lc:lc + TK], self.ident_b[:, :], B["rd"]("Bc") + [R("ident_b")], [rB])
        yield
        x3 = bx[0:TK, :].rearrange("p (h j) -> p h j", h=8)
        v3 = lambda t: t[0:TK, :].rearrange("p (h j) -> p h j", h=8)
        self.cp("act", self.xdt[0:TK, :], bx[0:TK, :], [rx], [R("xdt")])
        self.tt("dve", v3(xw), x3, self.bc_hp(self.wdec_t, off, TK, NCH32), ALU.mult, [rx, R("wdec", ci)], [R("xw", pb)])
        self.tt("dve", v3(xD), x3, self.bc_hp(self.Dbc, 8 * g, TK, 32), ALU.mult, [rx, R("Dbc")], [R("xD", pb)])
        self.cp("act", Btok[0:TK, :], bBv[0:TK, 0:128], [rB], [R("Btok", pb)])
        self.pinned -= {bx_i, bB_i}
        segs = []
        for half in range(2):
            bs, rs, bs_i = self.bankp()
            segs.append((bs, rs, bs_i))
            for j in range(4):
                hh = half * 4 + j
                h = 8 * g + hh
                o = bs[0:TK, j * 128:j * 128 + TK]
                self.mm(o, self.da_t[0:TK, ci, h:h + 1].broadcast_to([TK, TK]), BTx[0:TK, 0:TK], True, False, [R("da_t", ci), R("BT"), R("BT64")], [rs])
                self.mm(o, self.ident_b[0:TK, 0:TK], NEGx[0:TK, 0:TK], False, True, [R("ident_b"), R("NEG"), R("NEG64")], [rs])
        bc, rc, bc_i = self.bankp()
        self.mm(bc[0:TK, 0:TK], B["Bc"][:, lc:lc + TK], B["Cc"][:, lc:lc + TK], True, True, B["rd"]("Bc") + B["rd"]("Cc"), [rc])
        yield
        self.cp("act", self.cbT[0:TK, 0:TK], bc[0:TK, 0:TK], [rc], [R("cbT")])
        for half in range(2):
            bs, rs, bs_i = segs[half]
            for j in range(4):
                hh = half * 4 + j
                h = 8 * g + hh
                self.act(self.LT[0:TK, hh, 0:TK], bs[0:TK, j * 128:j * 128 + TK], AF.Exp, [rs, R("nacs", ci)], [R("LT", hh)],
                         bias=self.nacs_t[0:TK, ci, h:h + 1])
        self.pinned -= {bc_i, segs[0][2], segs[1][2]}
        yield "gap"
        self.tt("dve", self.MT[0:TK, :, 0:TK], self.LT[0:TK, :, 0:TK], bass.AP(self.cbT, 0, [[128, TK], [0, 8], [1, TK]]), ALU.mult,
                [R("LT", h_) for h_ in range(8)] + [R("cbT")], [R("MT")])
        byd, ryd, byd_i = self.bankp()
        for hh in range(8):
            self.mm(byd[0:TK, hh * 64:(hh + 1) * 64], self.MT[0:TK, hh, 0:TK], self.xdt[0:TK, hh * 64:(hh + 1) * 64], True, True,
                    [R("MT"), R("xdt")], [ryd])
        fr["byd"] = (byd, ryd, byd_i)
        yield

    def ssd_back(self, sti, g, c0, ch, fr, B):
        R, S, d = self.R, self.S, self.d
        kind, cc0, TK, ci = ch
        lc = cc0 - c0
        NCH32 = 6 * 32
        off = ci * 32 + 8 * g
        byd, ryd, byd_i = fr["byd"]
        pb = fr["pb"]
        xw, xD, Btok = self.xw2[pb], self.xD2[pb], self.Btok2[pb]
        rxw, rxD, rBtok = R("xw", pb), R("xD", pb), R("Btok", pb)
        v3 = lambda t: t[0:TK, :].rearrange("p (h j) -> p h j", h=8)
        hv = self.hT[:, g * 512:(g + 1) * 512]
        byo = ryo = byo_i = None
        if kind == "m":
            self.tt("dve", self.ynb[0:TK, :], byd[0:TK, :], xD[0:TK, :], ALU.add, [ryd, rxD], [R("ynb")])
        else:
            self.tt("dve", self.yt[0:TK, :], byd[0:TK, :], xD[0:TK, :], ALU.add, [ryd, rxD], [R("yt")])
        self.pinned.discard(byd_i)
        if kind == "p":
            byo, ryo, byo_i = self.bankp()
            self.mm(byo[0:TK, :], B["Cc"][:, lc:lc + TK], self.hTb[:], True, True, B["rd"]("Cc") + [R("hTb")], [ryo])
        if kind != "s":
            bS, rS, bS_i = self.bankp()
            self.mm(bS[:, :], Btok[0:TK, :], xw[0:TK, :], True, True, [rBtok, rxw], [rS])
        yield
        if kind == "m":
            self.cp("act", hv, bS[:, :], [rS], [R("hT")])
            self.cp("act", self.hTb[:], hv, [R("hT")], [R("hTb")])
            self.pinned.discard(bS_i)
        elif kind == "p":
            self.tt("dve", hv.rearrange("p (h j) -> p h j", h=8), hv.rearrange("p (h j) -> p h j", h=8),
                    self.bc_hp(self.etot_t, off, 128, NCH32), ALU.mult, [R("hT"), R("etot", ci)], [R("hT")])
            self.tt("dve", hv, hv, bS[:, :], ALU.add, [R("hT"), rS], [R("hT")])
            self.cp("act", self.hTb[:], hv, [R("hT")], [R("hTb")])
            self.pinned.discard(bS_i)
        else:
            Cmv = self.MT[:, :, :].rearrange("p a b -> p (a b)").rearrange("p (b t) -> p b t", b=16)
            Btmv = self.hb[:, :, :].rearrange("p c n -> p (c n)")[0:64, 0:2048].rearrange("p (b n) -> p b n", b=16)
            rBtm = [R("hb", c_) for c_ in range(4)]
            self.tt("dve", Cmv, B["Cc"][:, 0:64].unsqueeze(1).broadcast_to([128, 16, 64]), self.maskall[:], ALU.mult,
                    B["rd"]("Cc") + [R("maskall")], [R("MT")])
            self.tt("dve", Btmv, bass.AP(Btok, 0, [[128, 64], [0, 16], [1, 128]]),
                    bass.AP(self.sel, 0, [[16, 64], [1, 16], [0, 128]]), ALU.mult, [rBtok, R("sel")], rBtm)
            byo, ryo, byo_i = self.bankp()
            h0v = lambda i: self.stg[i // 2][:, (i % 2) * 512:(i % 2 + 1) * 512].rearrange("p (i n) -> p i n", i=4)
            rpar = lambda i: R("stg", i // 2)
            src = lambda b: d["sst"][b, 8 * g:8 * g + 8].rearrange("(i hh) p n -> (hh p) i n", hh=2)

            def ld(b):
                S.dma("pool", h0v(b % 4), src(b), reads=[rpar(b % 4)], writes=[R("h0", b % 4)])

            def tpc(b):
                h0, rh0 = h0v(b % 4), R("h0", b % 4)
                bt, rt, bt_i = self.bankp()
                for i in range(4):
                    self.tp(bt[:, i * 128:(i + 1) * 128], h0[:, i, :], self.ident_f[:, :], [rh0, rpar(b % 4), R("ident_f")], [rt])
                hTbb = self.hTb2[b % 2]
                rhTbb = R("hTb") if b % 2 == 0 else R("hTb2", 1)
                self.cp("act", hTbb[:], bt[:], [rt], [rhTbb])
                self.pinned.discard(bt_i)
            ld(0)
            ld(1)
            ld(2)
            ld(3)
            tpc(0)
            for b in range(16):
                if b + 1 < 16:
                    tpc(b + 1)
                h0, rh0 = h0v(b % 4), R("h0", b % 4)
                hTbb = self.hTb2[b % 2]
                rhTbb = R("hTb") if b % 2 == 0 else R("hTb2", 1)
                self.mm(byo[0:64, :], Cmv[:, b, :], hTbb[:], b == 0, b == 15, [R("MT"), rhTbb], [ryo])
                bsn, rsn, bsn_i = self.bankp()
                for i in range(4):
                    self.mm(bsn[:, i * 128:(i + 1) * 128], xw[0:64, i * 128:(i + 1) * 128], Btmv[:, b, :], True, True,
                            [rxw] + rBtm, [rsn])
                self.tt("dve", h0, h0, bass.AP(self.etn, b * 16 + 4 * g, [[256, 128], [1, 4], [0, 128]]), ALU.mult,
                        [rh0, R("etn"), rpar(b % 4)], [rh0])
                self.tt("dve", h0, h0, bsn[:].rearrange("p (i n) -> p i n", i=4), ALU.add, [rh0, rsn, rpar(b % 4)], [rh0])
                self.pinned.discard(bsn_i)
                self.out_toks.append(S.dma("sp", d["sst_o"][b, 8 * g:8 * g + 8].rearrange("(i hh) p n -> (hh p) i n", hh=2), h0,
                                           reads=[rh0, rpar(b % 4)]))
                if b + 4 < 16:
                    ld(b + 4)
                yield "gap"
        if kind != "m":
            self.tt("dve", v3(self.t1), byo[0:TK, :].rearrange("p (h j) -> p h j", h=8), self.bc_hp(self.eacs_t, off, TK, NCH32), ALU.mult,
                    [ryo, R("eacs", ci)], [R("t1")])
            self.tt("dve", self.ynb[0:TK, :], self.yt[0:TK, :], self.t1[0:TK, :], ALU.add, [R("yt"), R("t1")], [R("ynb")])
            self.pinned.discard(byo_i)
        yield
        bv, rv, bv_i = self.bankp()
        bvv = self.bview(bv)
        for q in range(4):
            self.tp(bvv[:, q * 128:q * 128 + TK], self.ynb[0:TK, q * 128:(q + 1) * 128], self.ident_b[0:TK, 0:TK], [R("ynb"), R("ident_b")], [rv])
        self.tt("dve", self.Y[:, 4 * g:4 * g + 4, cc0:cc0 + TK], bvv[:, 0:512].rearrange("p (q t) -> p q t", q=4)[:, :, 0:TK],
                B["gz"][:, :, lc:lc + TK], ALU.mult, [rv] + [r_ for q_ in range(4) for r_ in B["rd"]("gz", q_)], [R("Y", 4 * g + q_) for q_ in range(4)])
        self.pinned.discard(bv_i)
        yield

    def ssd_tile(self, sti, g, ti, c0, chunks, B):
        R = self.R
        if chunks[0][0] == "p" and ti == 0:
            self.cp("act", self.hTb[:], self.hT[:, g * 512:(g + 1) * 512], [R("hT")], [R("hTb")])
        frs = [dict() for _ in chunks]
        for t_ in self.ssd_front(sti, g, c0, chunks[0], 0, frs[0], B):
            yield t_
        for i, ch in enumerate(chunks):
            bgen = self.ssd_back(sti, g, c0, ch, frs[i], B)
            if i + 1 < len(chunks):
                fgen = self.ssd_front(sti, g, c0, chunks[i + 1], i + 1, frs[i + 1], B)
                if ch[0] == "p" and chunks[i + 1][0] == "p":
                    alive = [fgen, bgen]
                    while alive:
                        for gen in list(alive):
                            try:
                                yield next(gen)
                            except StopIteration:
                                alive.remove(gen)
                else:
                    for t_ in fgen:
                        yield t_
                    for t_ in bgen:
                        yield t_
            else:
                for t_ in bgen:
                    yield t_

    def state_out(self, g):
        R, S, d = self.R, self.S, self.d
        bk, br = self.bank()
        for i in range(4):
            self.tp(bk[:, i * 128:(i + 1) * 128], self.hT[:, g * 512 + i * 128: g * 512 + (i + 1) * 128], self.ident_f[:, :], [R("hT"), R("ident_f")], [br])
        hn = self.stg[1][:, 0:512]
        self.cp("act", hn, bk[:], [br], [R("stg", 1)])
        self.out_toks.append(S.dma("sp", d["pss"][8 * g:8 * g + 8].rearrange("(i hh) p n -> (hh p) i n", hh=2),
                                   hn.rearrange("p (i n) -> p i n", i=4), reads=[R("stg", 1)]))

    def final_out(self, sti, tiles):
        R, S, d = self.R, self.S, self.d
        Yf = self.Y[:, :, :].rearrange("p c n -> p (c n)").bitcast(F32).rearrange("p (c n) -> p c n", c=8)
        rY = lambda c: [R("Y", 2 * c), R("Y", 2 * c + 1)]
        for (c0, n) in tiles:
            self.rmsnorm(3, c0, n, lambda c: (Yf[:, c, c0:c0 + n], rY(c)))
        blocks = []
        if sti == 0:
            blocks.append((0, 64, d["ys"][:, :]))
            for i in range(4):
                blocks.append((80 + 128 * i, 128, d["yp"][128 * i:128 * (i + 1), :]))
        else:
            for i in range(4):
                blocks.append((128 * i, 128, d["yp"][512 * sti + 128 * i: 512 * sti + 128 * (i + 1), :]))
        for (cb, nb, dst) in blocks:
            si = self.stg_i % 2
            self.stg_i += 1
            st = self.stg[si]
            for half in range(2):
                bk, br = self.bank()
                for j in range(4):
                    c = half * 4 + j
                    self.tp(bk[0:nb, j * 128:(j + 1) * 128], Yf[:, c, cb:cb + nb], self.ident_f[:, :], rY(c) + [R("ident_f")], [br])
                self.cp("act" if half == 0 else "dve", st[0:nb, half * 512:(half + 1) * 512], bk[0:nb, :], [br], [R("stg", si)])
            self.out_toks.append(S.dma("sp", dst, st[0:nb, :], reads=[R("stg", si)]))

    def conv_state_out(self):
        R, S, d = self.R, self.S, self.d

        def emit(src_fn, nchunks, rows, dst_fn, rsrc):
            for part in range(nchunks // 8):
                si = self.stg_i % 2
                self.stg_i += 1
                st = self.stg[si]
                for half in range(2):
                    bk, br = self.bank()
                    for j in range(4):
                        c = part * 8 + half * 4 + j
                        self.tp(bk[0:rows, j * 128:(j + 1) * 128], src_fn(c), self.ident_f[:, :], [rsrc(c), R("ident_f")], [br])
                    self.cp("act" if half == 0 else "dve", st[0:rows, half * 512:(half + 1) * 512], bk[0:rows, :], [br], [R("stg", si)])
                self.out_toks.append(S.dma("sp", dst_fn(part), st[0:rows, :], reads=[R("stg", si)]))
        emit(lambda c: self.haloA[:, c, :], 8, 2, lambda part: d["pca"][:, :], lambda c: R("haloA", c))
        emit(lambda c: self.haloB[:, c, :], 24, 3, lambda part: d["psc"][:, part * 1024:(part + 1) * 1024], lambda c: R("haloB", c))
        emit(lambda c: self.outA[:, c, :], 8, 32, lambda part: d["sca_o"][:, :], lambda c: R("stA", c))
        emit(lambda c: self.outB[:, c, :], 24, 48, lambda part: d["ssc_o"][:, part * 1024:(part + 1) * 1024], lambda c: R("stB", c))

    def run(self):
        d = self.d
        self.out_toks = []
        items = []
        for sti in range(4):
            tiles = [(0, 80), (80, 512)] if sti == 0 else [(0, 512)]
            items.append((0, (lambda pids, sti=sti: self.load_x_issue(sti)),
                          lambda pids, sti=sti, tiles=tiles: (self.load_x(sti), self.norm_to_xn(0, tiles))))
            items += self.ffn_items(d["w1gu"], d["w1d"], tiles)
            items.append((0, None, lambda pids, tiles=tiles: self.norm_to_xn(1, tiles)))
            items += self.a_items(sti, d["w_in"], tiles)
            items.append(self.atail_item(d["w_in"], d["w_a_out"], d["w_o"], tiles))
            items += self.b_items(sti, d["w_in"], tiles)
            items.append(self.btail_item(sti, d["w_in"], d["w_b_out"], d["w_o"], tiles))
            items.append((0, None, lambda pids, tiles=tiles: self.norm_to_xn(2, tiles)))
            items += self.ffn_items(d["w2gu"], d["w2d"], tiles)
            items.append((0, None, lambda pids, sti=sti, tiles=tiles: self.final_out(sti, tiles)))
        items.append((0, None, lambda pids: self.conv_state_out()))
        N = len(items)
        pids = [None] * N
        loaded = [False] * N

        def do_load(j):
            k, load, _ = items[j]
            pids[j] = self.take_pages(k)
            if load is not None:
                load(pids[j])
            loaded[j] = True

        for i in range(N):
            if not loaded[i]:
                do_load(i)
            j = i + 1
            if j < N and not loaded[j]:
                k = items[j][0]
                cand = [(self.page_i + x) % 4 for x in range(k)]
                if not (set(cand) & set(pids[i])):
                    do_load(j)
            items[i][2](pids[i])
        self.S.wait_all("sp", self.out_toks)


IN_SPECS = [
    ("xp", [2048, 1024]), ("xs", [64, 1024]), ("meta", [16, 1024]), ("sca", [32, 1024]), ("ssc", [48, 3072]),
    ("sst", [16, 32, 64, 128]), ("cst1", [96, 128]), ("cst2", [96, 128]), ("dtb", [1, 32]), ("alog", [1, 32]), ("dsk", [1, 32]),
    ("w1gu", [1, 1024, 5632]), ("w1d", [1, 2816, 1024]), ("w_in", [1, 1024, 10272]), ("w_a_out", [1, 1024, 1024]),
    ("w_b_out", [1, 2048, 1024]), ("w_o", [1, 1024, 1024]), ("w2gu", [1, 1024, 5632]), ("w2d", [1, 2816, 1024]),
]
OUT_SPECS = [
    ("yp", [2048, 1024]), ("ys", [64, 1024]), ("pca", [2, 1024]), ("psc", [3, 3072]), ("pss", [32, 64, 128]),
    ("sca_o", [32, 1024]), ("ssc_o", [48, 3072]), ("sst_o", [16, 32, 64, 128]),
]


def build_nc():
    nc = bass.Bass("TRN2", target_bir_lowering=False)
    d = {}
    for name, shape in IN_SPECS:
        d[name] = nc.dram_tensor(name, shape, F32, kind="ExternalInput").ap()
    for name, shape in OUT_SPECS:
        d[name] = nc.dram_tensor(name, shape, F32, kind="ExternalOutput").ap()
    with ExitStack() as st:
        S = Sched(nc, st)
        K = Kern(S, nc, d)
        K.alloc()
        K.ones_f = S.sb([128, 128], F32, name="ones_f")
        S.op("pool", lambda e: e.memset(K.ones_f[:], 1.0), [], [K.R("ones_f")])
        K.setup()
        K.run()
        S.emit()
    return nc


_NC_CACHE = {}


def kernel(x_prompt, x_sample, state_conv_a, state_ssm_conv, state_ssm, meta_tokens,
           norm_ffn1, ffn1_w_gu, ffn1_w_down, norm_mix, w_in, conv_a_w, w_a_out,
           ssm_conv_w, ssm_conv_b, dt_bias, a_log, d_skip, ssm_norm_w, w_b_out, w_o,
           norm_ffn2, ffn2_w_gu, ffn2_w_down, norm_final):
    f = lambda a: np.ascontiguousarray(np.asarray(a, dtype=np.float32))
    n = 8
    if "nc" not in _NC_CACHE:
        _NC_CACHE["nc"] = build_nc()
    nc = _NC_CACHE["nc"]
    cst1 = np.concatenate([f(norm_ffn1).reshape(8, 128), f(norm_mix).reshape(8, 128), f(norm_ffn2).reshape(8, 128),
                           f(norm_final).reshape(8, 128), f(conv_a_w).reshape(24, 128), f(ssm_conv_b).reshape(24, 128),
                           f(ssm_norm_w).reshape(16, 128)], axis=0)
    cst2 = f(ssm_conv_w).reshape(96, 128)
    shared = {
        "meta": f(meta_tokens), "cst1": f(cst1), "cst2": cst2, "dtb": f(dt_bias).reshape(1, 32), "alog": f(a_log).reshape(1, 32),
        "dsk": f(d_skip).reshape(1, 32), "w1gu": f(ffn1_w_gu), "w1d": f(ffn1_w_down), "w_in": f(w_in), "w_a_out": f(w_a_out),
        "w_b_out": f(w_b_out), "w_o": f(w_o), "w2gu": f(ffn2_w_gu), "w2d": f(ffn2_w_down),
    }
    xp, xs = f(x_prompt), f(x_sample)
    sca, ssc, sst = f(state_conv_a), f(state_ssm_conv), f(state_ssm)
    in_maps = []
    for c in range(n):
        m = dict(shared)
        m["xp"] = xp[c]
        m["xs"] = xs[16 * c:16 * (c + 1)].reshape(64, 1024)
        m["sca"] = sca[0, 16 * c:16 * (c + 1)].reshape(32, 1024)
        m["ssc"] = ssc[0, 16 * c:16 * (c + 1)].reshape(48, 3072)
        m["sst"] = sst[0, 16 * c:16 * (c + 1)]
        in_maps.append(m)
    res = run_bass_kernel_spmd(nc, in_maps, core_ids=list(range(n)))
    rs = res.results
    y_prompt = np.stack([rs[c]["yp"] for c in range(n)], axis=0)
    y_sample = np.concatenate([rs[c]["ys"].reshape(16, 4, 1024) for c in range(n)], axis=0)
    p_ca = np.stack([rs[c]["pca"] for c in range(n)], axis=0)[None]
    p_sc = np.stack([rs[c]["psc"] for c in range(n)], axis=0)[None]
    p_ss = np.stack([rs[c]["pss"] for c in range(n)], axis=0)[None]
    s_ca = np.concatenate([rs[c]["sca_o"].reshape(16, 2, 1024) for c in range(n)], axis=0)[None]
    s_sc = np.concatenate([rs[c]["ssc_o"].reshape(16, 3, 3072) for c in range(n)], axis=0)[None]
    s_ss = np.concatenate([rs[c]["sst_o"] for c in range(n)], axis=0)[None]
    return tuple(np.ascontiguousarray(a, dtype=np.float32) for a in (y_prompt, y_sample, p_ca, p_sc, p_ss, s_ca, s_sc, s_ss))
```
